# Optimizing a Trainium2 kernel written in Bass

```python
import math
import jax, jax.numpy as jnp
from jax import lax
import numpy as np

D_MODEL = 2048
BATCH = 4
SEQ = 2048
DEPTH = 2
DEC_BATCH = 128
DEC_SEQ = 1
PAST_LEN = 16384
PAGE_SIZE = 128

D_MIX = D_MODEL
N_BRANCH = 4
W_BRANCH = D_MIX // N_BRANCH
EPS = 1e-6
GLA_HEADS = 4
GLA_DK = W_BRANCH // 2 // GLA_HEADS
GLA_DV = W_BRANCH // GLA_HEADS
GLA_RANK = 16
GLA_TAU = 16.0
GLA_CHUNK = 64
SGU_HEADS = 4
SGU_HD = W_BRANCH // SGU_HEADS
SGU_CHUNK = 128
CONV_W = 3
SSM_GROUP_SIZE = 16
SSM_GROUPS = W_BRANCH // SSM_GROUP_SIZE
SSM_N = 64
PROJ_SIZES = (GLA_HEADS * GLA_DK, GLA_HEADS * GLA_DK, W_BRANCH, GLA_RANK, W_BRANCH,
              W_BRANCH, W_BRANCH, W_BRANCH,
              W_BRANCH, W_BRANCH, W_BRANCH, W_BRANCH,
              W_BRANCH, W_BRANCH)
PROJ_TOTAL = sum(PROJ_SIZES)

kernel_name = "hybrid_gla_sgu_conv_s5_decode_step"

F32 = jnp.float32


def rmsnorm(x, g):
    xf = x.astype(F32)
    y = xf * lax.rsqrt(jnp.mean(xf * xf, axis=-1, keepdims=True) + EPS)
    return (y * g.astype(F32)).astype(x.dtype)


def gla_recurrence(q, k, v, log_a, s0):
    bsz, L = q.shape[0], q.shape[1]
    c = min(GLA_CHUNK, L)
    n = -(-L // c)
    pad = n * c - L

    def prep(t):
        t = jnp.pad(t.astype(F32), ((0, 0), (0, pad), (0, 0), (0, 0)))
        return t.reshape(bsz, n, c, t.shape[2], t.shape[3]).swapaxes(0, 1)

    qc, kc, vc, gc = prep(q), prep(k), prep(v), prep(log_a)
    causal = jnp.tril(jnp.ones((c, c), dtype=bool))[None, :, :, None, None]

    def step(S, inp):
        qb, kb, vb, gb = inp
        b = jnp.cumsum(gb, axis=1)
        o_inter = jnp.einsum('bthk,bhkv->bthv', qb * jnp.exp(b), S)
        diff = b[:, :, None] - b[:, None, :]
        decay = jnp.exp(jnp.where(causal, diff, -jnp.inf))
        scores = jnp.einsum('bthk,bshk,btshk->bths', qb, kb, decay)
        o_intra = jnp.einsum('bths,bshv->bthv', scores, vb)
        b_last = b[:, -1]
        S_new = jnp.exp(b_last)[..., None] * S + jnp.einsum(
            'bshk,bshv->bhkv', kb * jnp.exp(b_last[:, None] - b), vb)
        return S_new, o_inter + o_intra

    S_fin, o = lax.scan(step, s0.astype(F32), (qc, kc, vc, gc))
    o = o.swapaxes(0, 1).reshape(bsz, n * c, o.shape[3], o.shape[4])[:, :L]
    return o, S_fin


def spatial_gating(u, v_n, w_s, b_s):
    bsz, L = u.shape[0], u.shape[1]
    c = min(SGU_CHUNK, L)
    w = w_s[:, :c, :c] * jnp.tril(jnp.ones((c, c), dtype=w_s.dtype))
    vr = v_n.reshape(bsz, L // c, c, SGU_HEADS, SGU_HD)
    mixed = jnp.einsum('hts,bnshd->bnthd', w, vr) + b_s[:, :c].T[None, None, :, :, None]
    return u * mixed.reshape(bsz, L, W_BRANCH).astype(u.dtype)


def short_conv(z, buf, w):
    L = z.shape[1]
    zc = jnp.concatenate([buf.astype(z.dtype), z], axis=1)
    y = w[0] * zc[:, 0:L]
    for j in range(1, CONV_W):
        y = y + w[j] * zc[:, j:j + L]
    return y, zc[:, -(CONV_W - 1):]


def _cplx_combine(e1, e2):
    a1r, a1i, b1r, b1i = e1
    a2r, a2i, b2r, b2i = e2
    return (a2r * a1r - a2i * a1i,
            a2r * a1i + a2i * a1r,
            a2r * b1r - a2i * b1i + b2r,
            a2r * b1i + a2i * b1r + b2i)


def s5_scan(u, x0_re, x0_im, lam_re, lam_im, log_dt, b_re, b_im, c_re, c_im, d):
    bsz, L = u.shape[0], u.shape[1]
    uf = u.astype(F32).reshape(bsz, L, SSM_GROUPS, SSM_GROUP_SIZE)
    dt = jnp.exp(log_dt.astype(F32))[:, None]
    lr, li = lam_re.astype(F32), lam_im.astype(F32)
    mag = jnp.exp(lr * dt)
    ar, ai = mag * jnp.cos(li * dt), mag * jnp.sin(li * dt)
    den = lr * lr + li * li
    cr = ((ar - 1.0) * lr + ai * li) / den
    ci = (ai * lr - (ar - 1.0) * li) / den
    br, bi = b_re.astype(F32), b_im.astype(F32)
    bbr = cr[..., None] * br - ci[..., None] * bi
    bbi = cr[..., None] * bi + ci[..., None] * br
    in_r = jnp.einsum('gnp,blgp->blgn', bbr, uf)
    in_i = jnp.einsum('gnp,blgp->blgn', bbi, uf)
    x0r, x0i = x0_re.astype(F32), x0_im.astype(F32)
    in_r = in_r.at[:, 0].add(ar * x0r - ai * x0i)
    in_i = in_i.at[:, 0].add(ar * x0i + ai * x0r)
    a_r = jnp.broadcast_to(ar, in_r.shape)
    a_i = jnp.broadcast_to(ai, in_i.shape)
    _, _, xr, xi = lax.associative_scan(_cplx_combine, (a_r, a_i, in_r, in_i), axis=1)
    y = (jnp.einsum('gpn,blgn->blgp', c_re.astype(F32), xr)
         - jnp.einsum('gpn,blgn->blgp', c_im.astype(F32), xi)
         + d.astype(F32).reshape(SSM_GROUPS, SSM_GROUP_SIZE) * uf)
    return y.reshape(bsz, L, W_BRANCH).astype(u.dtype), xr[:, -1], xi[:, -1]


def mixer_layer(x, s_gla, conv_buf, ssm_re, ssm_im,
                norm_g, w_in, w_a2, b_a, gla_g, sgu_g, sgu_w, sgu_b, conv_w,
                lam_re, lam_im, log_dt, b_re, b_im, c_re, c_im, ssm_d, glu_w, glu_b, w_out):
    bsz, L, _ = x.shape
    dt = x.dtype
    h = rmsnorm(x, norm_g)
    p = h @ w_in
    offs = [int(o) for o in np.cumsum(PROJ_SIZES)[:-1]]
    (q, k, v, a_lr, g_a, u_b, v_b, g_b,
     cb_gate, cc_gate, h_c, g_c, u_d, g_d) = jnp.split(p, offs, axis=-1)

    q = q.reshape(bsz, L, GLA_HEADS, GLA_DK) * (GLA_DK ** -0.5)
    k = k.reshape(bsz, L, GLA_HEADS, GLA_DK)
    v = v.reshape(bsz, L, GLA_HEADS, GLA_DV)
    log_a = (jax.nn.log_sigmoid((a_lr @ w_a2 + b_a).astype(F32)) / GLA_TAU).reshape(bsz, L, GLA_HEADS, GLA_DK)
    o_a, s_gla_new = gla_recurrence(q, k, v, log_a, s_gla)
    o_a = o_a * lax.rsqrt(jnp.mean(o_a * o_a, axis=-1, keepdims=True) + EPS) * gla_g.astype(F32).reshape(GLA_HEADS, GLA_DV)
    o_a = o_a.reshape(bsz, L, W_BRANCH).astype(dt) * jax.nn.silu(g_a)

    vb = v_b.astype(F32).reshape(bsz, L, SGU_HEADS, SGU_HD)
    vb = vb - jnp.mean(vb, axis=-1, keepdims=True)
    v_n = vb * lax.rsqrt(jnp.mean(vb * vb, axis=-1, keepdims=True) + EPS) * sgu_g.astype(F32).reshape(SGU_HEADS, SGU_HD)
    v_n = v_n.reshape(bsz, L, W_BRANCH).astype(dt)
    o_b = spatial_gating(u_b, v_n, sgu_w, sgu_b) * jax.nn.silu(g_b)

    y_c, conv_new = short_conv(cc_gate * h_c, conv_buf, conv_w)
    o_c = cb_gate * y_c * jax.nn.silu(g_c)

    y_d, sr, si = s5_scan(u_d, ssm_re, ssm_im, lam_re, lam_im, log_dt, b_re, b_im, c_re, c_im, ssm_d)
    gd = jax.nn.gelu(y_d)
    o_d = gd * jax.nn.sigmoid(gd @ glu_w + glu_b) * jax.nn.silu(g_d)

    mixed = jnp.concatenate([o_a, o_b.astype(dt), o_c.astype(dt), o_d.astype(dt)], axis=-1)
    return x + mixed @ w_out, s_gla_new, conv_new, sr, si, v_n


def setup_inputs(seed: int = 0) -> dict:
    key = jax.random.key(seed)
    ks = jax.random.split(key, 32)
    nrm = jax.random.normal
    W = W_BRANCH
    lam_im = jnp.broadcast_to(math.pi * jnp.arange(SSM_N, dtype=F32), (DEPTH, SSM_GROUPS, SSM_N))
    return {
        "x_prompt": nrm(ks[0], (BATCH, SEQ, D_MODEL), F32),
        "x_sample": nrm(ks[1], (DEC_BATCH, DEC_SEQ, D_MODEL), F32),
        "state_gla": nrm(ks[2], (DEPTH, DEC_BATCH, GLA_HEADS, GLA_DK, GLA_DV), F32),
        "state_conv": nrm(ks[3], (DEPTH, DEC_BATCH, CONV_W - 1, W), F32),
        "state_ssm_re": 0.5 * nrm(ks[4], (DEPTH, DEC_BATCH, SSM_GROUPS, SSM_N), F32),
        "state_ssm_im": 0.5 * nrm(ks[5], (DEPTH, DEC_BATCH, SSM_GROUPS, SSM_N), F32),
        "norm_g": 1.0 + 0.01 * nrm(ks[6], (DEPTH, D_MODEL), F32),
        "w_in": nrm(ks[7], (DEPTH, D_MODEL, PROJ_TOTAL), F32) * D_MODEL ** -0.5,
        "w_a2": nrm(ks[8], (DEPTH, GLA_RANK, GLA_HEADS * GLA_DK), F32) * GLA_RANK ** -0.5,
        "b_a": 0.1 * nrm(ks[9], (DEPTH, GLA_HEADS * GLA_DK), F32),
        "gla_g": 1.0 + 0.01 * nrm(ks[10], (DEPTH, W), F32),
        "sgu_g": 1.0 + 0.01 * nrm(ks[11], (DEPTH, W), F32),
        "sgu_w": nrm(ks[12], (DEPTH, SGU_HEADS, SGU_CHUNK, SGU_CHUNK), F32) * SGU_CHUNK ** -0.5,
        "sgu_b": 1.0 + 0.01 * nrm(ks[13], (DEPTH, SGU_HEADS, SGU_CHUNK), F32),
        "conv_w": nrm(ks[14], (DEPTH, CONV_W, W), F32) * CONV_W ** -0.5,
        "ssm_lambda_re": -0.5 * jnp.exp(0.05 * nrm(ks[15], (DEPTH, SSM_GROUPS, SSM_N), F32)),
        "ssm_lambda_im": lam_im,
        "ssm_log_dt": jax.random.uniform(ks[16], (DEPTH, SSM_GROUPS), F32, math.log(1e-3), math.log(1e-1)),
        "ssm_b_re": nrm(ks[17], (DEPTH, SSM_GROUPS, SSM_N, SSM_GROUP_SIZE), F32) * (2.0 * SSM_GROUP_SIZE) ** -0.5,
        "ssm_b_im": nrm(ks[18], (DEPTH, SSM_GROUPS, SSM_N, SSM_GROUP_SIZE), F32) * (2.0 * SSM_GROUP_SIZE) ** -0.5,
        "ssm_c_re": nrm(ks[19], (DEPTH, SSM_GROUPS, SSM_GROUP_SIZE, SSM_N), F32) * (2.0 * SSM_N) ** -0.5,
        "ssm_c_im": nrm(ks[20], (DEPTH, SSM_GROUPS, SSM_GROUP_SIZE, SSM_N), F32) * (2.0 * SSM_N) ** -0.5,
        "ssm_d": nrm(ks[21], (DEPTH, W), F32),
        "glu_w": nrm(ks[22], (DEPTH, W, W), F32) * W ** -0.5,
        "glu_b": 0.01 * nrm(ks[23], (DEPTH, W), F32),
        "w_out": nrm(ks[24], (DEPTH, D_MIX, D_MODEL), F32) * D_MIX ** -0.5,
        "final_norm_g": 1.0 + 0.01 * nrm(ks[25], (D_MODEL,), F32),
    }


def reference(x_prompt, x_sample, state_gla, state_conv, state_ssm_re, state_ssm_im,
              norm_g, w_in, w_a2, b_a, gla_g, sgu_g, sgu_w, sgu_b, conv_w,
              ssm_lambda_re, ssm_lambda_im, ssm_log_dt, ssm_b_re, ssm_b_im, ssm_c_re, ssm_c_im,
              ssm_d, glu_w, glu_b, w_out, final_norm_g):
    bp = x_prompt.shape[0]
    hp, hs = x_prompt, x_sample
    gla_p, gla_s, conv_p, conv_s = [], [], [], []
    sre_p, sim_p, sre_s, sim_s, vn_s = [], [], [], [], []
    for l in range(DEPTH):
        lw = (norm_g[l], w_in[l], w_a2[l], b_a[l], gla_g[l], sgu_g[l], sgu_w[l], sgu_b[l], conv_w[l],
              ssm_lambda_re[l], ssm_lambda_im[l], ssm_log_dt[l], ssm_b_re[l], ssm_b_im[l],
              ssm_c_re[l], ssm_c_im[l], ssm_d[l], glu_w[l], glu_b[l], w_out[l])
        hp, g1, c1, r1, i1, _ = mixer_layer(
            hp, jnp.zeros((bp, GLA_HEADS, GLA_DK, GLA_DV), F32),
            jnp.zeros((bp, CONV_W - 1, W_BRANCH), hp.dtype),
            jnp.zeros((bp, SSM_GROUPS, SSM_N), F32), jnp.zeros((bp, SSM_GROUPS, SSM_N), F32), *lw)
        hs, g2, c2, r2, i2, v2 = mixer_layer(
            hs, state_gla[l], state_conv[l], state_ssm_re[l], state_ssm_im[l], *lw)
        gla_p.append(g1); gla_s.append(g2); conv_p.append(c1); conv_s.append(c2)
        sre_p.append(r1); sim_p.append(i1); sre_s.append(r2); sim_s.append(i2); vn_s.append(v2)
    y_prompt = rmsnorm(hp, final_norm_g)
    y_sample = rmsnorm(hs, final_norm_g)
    return (y_prompt, y_sample,
            jnp.stack(gla_p), jnp.stack(gla_s),
            jnp.stack(conv_p), jnp.stack(conv_s),
            jnp.stack(sre_p), jnp.stack(sim_p),
            jnp.stack(sre_s), jnp.stack(sim_s),
            jnp.stack(vn_s))
```

```python
import types
import numpy as np
import concourse.bass as bass
import concourse.mybir as mybir
from concourse.bass_utils import run_bass_kernel_spmd

F32 = mybir.dt.float32
BF16 = mybir.dt.bfloat16
AF = mybir.ActivationFunctionType
ALU = mybir.AluOpType

D = 2048
PT = 6160
NS = 16
W = 512
EPS = 1e-6
DEBUG_EMIT = False
SAME_SKIP = ("pe",)
STAGE = 99
SUB = 99
NOSAMP = False
DUMP = None
GLABAR = 0
PSSHIFT = 0
DBGT = False
DBGN = []
ERRS = {}
SEG = dict(q=0, k=256, v=512, a=1024, ga=1040, ub=1552, vb=2064, gb=2576,
           cb=3088, cc=3600, hc=4112, gc=4624, ud=5136, gd=5648)


class Prog:
    def __init__(self, nc):
        self.nc = nc
        self.ops = {e: [] for e in ("sync", "pool", "act", "dve", "pe")}
        self.cnt = {}
        self.lastw = {}
        self.readers = {}
        self.vq_phys = {}
        self.floor = {}

    @staticmethod
    def _freeze(fn):
        if fn.__closure__ is None:
            return fn
        cells = []
        for c in fn.__closure__:
            try:
                cells.append(types.CellType(c.cell_contents))
            except ValueError:
                cells.append(c)
        return types.FunctionType(fn.__code__, fn.__globals__, fn.__name__, fn.__defaults__, tuple(cells))

    def _rec(self, phys, q, fn, r, w, inc):
        fn = self._freeze(fn)
        waits = dict(self.floor.get(phys, {}))

        def need(tok):
            s, v = tok
            if s == q and phys in SAME_SKIP:
                return
            if v > waits.get(s, 0):
                waits[s] = v
        for k in r:
            if k in self.lastw:
                need(self.lastw[k])
        for k in w:
            if k in self.lastw:
                need(self.lastw[k])
            for t in self.readers.get(k, ()):
                need(t)
        idx = self.cnt.get(q, 0) + 1
        self.cnt[q] = idx
        if inc == 16 and idx > 1:
            if 16 * (idx - 1) > waits.get(q, 0):
                waits[q] = 16 * (idx - 1)
        self.vq_phys[q] = phys
        self.ops[phys].append((fn, waits, q, inc))
        tok = (q, idx * inc)
        for k in r:
            self.readers.setdefault(k, []).append(tok)
        for k in w:
            self.lastw[k] = tok
            self.readers[k] = []

    def dma(self, q, out, in_, r=(), w=(), phys="sync", **kw):
        if q in ("ld", "st"):
            self.rr = getattr(self, "rr", 0) + 1
            q = "%s%d" % (q, self.rr % 6)
        self._rec(phys, q, lambda e: e.dma_start(out=out, in_=in_, **kw), r, w, 16)

    def act(self, fn, r=(), w=()):
        self._rec("act", "act", fn, r, w, 1)

    def dve(self, fn, r=(), w=()):
        self._rec("dve", "dve", fn, r, w, 1)

    def pe(self, fn, r=(), w=()):
        self._rec("pe", "pe", fn, r, w, 1)

    def barrier(self):
        snap = {q: c * (16 if q not in ("act", "dve", "pe") else 1) for q, c in self.cnt.items()}
        for p in self.ops:
            if p == "pool":
                continue
            self.floor[p] = dict(snap)

    def emit(self):
        nc = self.nc
        qs = list(self.cnt.keys())
        sems = {}
        import contextlib
        with contextlib.ExitStack() as st:
            for q in qs:
                sems[q] = st.enter_context(nc.semaphore("s_" + q))
            block = st.enter_context(nc.Block())
            final = {q: c * (16 if q not in ("act", "dve", "pe") else 1) for q, c in self.cnt.items()}

            def run(phys, eng, last=False):
                have = {}
                for fn, waits, q, inc in self.ops[phys]:
                    for s, v in waits.items():
                        if have.get(s, 0) < v:
                            eng.wait_ge(sems[s], v)
                            have[s] = v
                    if DEBUG_EMIT:
                        try:
                            fn(eng).then_inc(sems[q], inc)
                        except Exception as ex:
                            msg = str(ex)[:300]
                            if msg not in ERRS:
                                ERRS[msg] = (phys, q)
                                print("EMIT-ERR", phys, q, msg, flush=True)
                        continue
                    fn(eng).then_inc(sems[q], inc)
                if last:
                    for q, v in final.items():
                        eng.wait_ge(sems[q], v)

            @block.sync
            def _(e):
                run("sync", e, last=True)

            @block.gpsimd
            def _(e):
                run("pool", e)

            @block.scalar
            def _(e):
                run("act", e)

            @block.vector
            def _(e):
                run("dve", e)

            @block.tensor
            def _(e):
                run("pe", e)


def build_program():
    nc = bass.Bass("TRN2", target_bir_lowering=False)
    P = Prog(nc)

    def din(name, shape):
        return nc.dram_tensor(name, list(shape), F32, kind="ExternalInput").ap()

    def dout(name, shape):
        return nc.dram_tensor(name, list(shape), F32, kind="ExternalOutput").ap()

    xp = din("xp", [2048, D]); xs = din("xs", [NS, D])
    sgla = din("sgla", [2, NS, 4, 64, 128]); sconv = din("sconv", [2, NS, 2, W])
    sre = din("sre", [2, NS, 32, 64]); sim = din("sim", [2, NS, 32, 64])
    norm_g = din("norm_g", [2, D]); w_in = din("w_in", [2, D, PT]); w_a2 = din("w_a2", [2, 16, 256])
    b_a = din("b_a", [2, 256]); gla_g = din("gla_g", [2, W]); sgu_g = din("sgu_g", [2, W])
    sgu_w = din("sgu_w", [2, 4, 128, 128]); sgu_b = din("sgu_b", [2, 4, 128]); conv_w = din("conv_w", [2, 3, W])
    lam_re = din("lam_re", [2, 32, 64]); lam_im = din("lam_im", [2, 32, 64]); log_dt = din("log_dt", [2, 32])
    b_re = din("b_re", [2, 32, 64, 16]); b_im = din("b_im", [2, 32, 64, 16])
    c_re = din("c_re", [2, 32, 16, 64]); c_im = din("c_im", [2, 32, 16, 64])
    ssm_d = din("ssm_d", [2, W]); glu_w = din("glu_w", [2, W, W]); glu_b = din("glu_b", [2, W])
    w_out = din("w_out", [2, D, D]); fng = din("fng", [D])
    ident_d = din("ident", [128, 128]); triu_d = din("triu", [128, 128]); eye16_d = din("eye16", [16, 16])
    hmask_d = din("hmask", [128, 2])

    yp = dout("yp", [2048, D]); ys = dout("ys", [NS, D])
    o_gla_p = dout("gla_p", [2, 4, 64, 128]); o_gla_s = dout("gla_s", [2, NS, 4, 64, 128])
    o_conv_p = dout("conv_p", [2, 2, W]); o_conv_s = dout("conv_s", [2, NS, 2, W])
    o_sre_p = dout("sre_p", [2, 32, 64]); o_sim_p = dout("sim_p", [2, 32, 64])
    o_sre_s = dout("sre_s", [2, NS, 32, 64]); o_sim_s = dout("sim_s", [2, NS, 32, 64])
    o_vn_s = dout("vn_s", [2, NS, W])
    x1 = nc.dram_tensor("x1s", [2048 + NS, D], F32, kind="Internal").ap()
    x2 = nc.dram_tensor("x2s", [2048 + NS, D], F32, kind="Internal").ap()

    import contextlib
    st = contextlib.ExitStack()

    def sb(name, shape, dt=F32):
        return st.enter_context(nc.sbuf_tensor("sb_" + name, list(shape), dt))

    def ps(name, shape, dt=F32):
        return st.enter_context(nc.psum_tensor(name, list(shape), dt))

    hT = sb("hT", [128, 16, 1040], BF16)
    mT = sb("mT", [128, 16, 1040], BF16)
    wbuf = sb("wbuf", [128, 2, 16, 256], BF16)
    ident = sb("ident", [128, 128]); identb = sb("identb", [128, 128], BF16)
    triu = sb("triu", [128, 128]); eye16 = sb("eye16", [16, 16]); hmask = sb("hmask", [128, 2])
    ones1 = sb("ones1", [1, 128], BF16)
    epsb = sb("epsb", [128, 1])
    ng = sb("ng", [128, 2, 16])
    wa2 = sb("wa2", [16, 256], BF16); wa2f = sb("wa2f", [16, 256])
    bafm = sb("bafm", [128, 2]); barow = sb("barow", [1, 256], BF16); barowf = sb("barowf", [1, 256])
    glag = sb("glag", [128, 4]); sgug = sb("sgug", [128, W]); sgub = sb("sgub", [128, 4, 128])
    sguw = sb("sguw", [128, 4, 128], BF16); w00 = sb("w00", [128, 4]); b00 = sb("b00", [128, 4])
    cw = sb("cw", [128, 3, 4]); dsk = sb("dsk", [128, 4]); glub = sb("glub", [128, 4])
    gluw = sb("gluw", [128, 4, W], BF16)
    S32 = sb("S32", [128, 2, 128]); Sbf = sb("Sbf", [128, 2, 128], BF16)
    zcar = sb("zcar", [128, 4, 2])
    nbafm = sb("nbafm", [128, 2])
    Xc = sb("Xc", [128, 2, 16])
    BL = sb("BL", [128, 16, 2, 128], BF16)
    CL = sb("CL", [128, 16, 2, 128], BF16)
    TC = sb("TC", [128, 16, 128]); TS = sb("TS", [128, 16, 128]); MAG = sb("MAG", [128, 16])
    AR = sb("AR", [128, 16]); AI = sb("AI", [128, 16])
    ARENA = 78 * 1024
    arena = sb("arena", [128, ARENA // 4])
    psb = [ps("psb%d" % i, [128, 512]) for i in range(8)]

    class Carver:
        def __init__(self):
            self.off = 0
            self.limit = ARENA

        def reset(self, base=0, limit=None):
            self.off = base
            self.limit = ARENA if limit is None else limit

        def get(self, shape, dt=F32, rows=128):
            n = int(np.prod(shape[1:]))
            nb = n * (4 if dt == F32 else 2)
            nb = (nb + 63) // 64 * 64
            assert self.off + nb <= self.limit, ("arena overflow", self.off, nb, self.limit)
            v = arena[0:shape[0], self.off // 4:(self.off + nb) // 4]
            if dt != F32:
                v = v.bitcast(dt)
            v = v[:, 0:n]
            self.off += nb
            if len(shape) == 3:
                v = v.rearrange("p (a b) -> p a b", a=shape[1])
            elif len(shape) == 4:
                v = v.rearrange("p (a b c) -> p a b c", a=shape[1], b=shape[2])
            return v

    MAINLIM = 52 * 1024
    CVS = Carver()
    TICK = [None]

    def tick():
        gen = TICK[0]
        if gen is not None:
            try:
                next(gen)
            except StopIteration:
                TICK[0] = None

    CV = Carver()
    uid = [0]

    def K(s="t"):
        uid[0] += 1
        return "%s%d" % (s, uid[0])

    def bc(ap, shape):
        return ap.to_broadcast(list(shape))

    P.dma("ld", ident[:], ident_d, w=["ident"])
    P.dma("ld", triu[:], triu_d, w=["triu"])
    P.dma("ld", eye16[:], eye16_d, w=["eye16"])
    P.dma("ld", hmask[:], hmask_d, w=["hmask"])
    P.dma("ld", ng[:], norm_g.rearrange("l (k p) -> p l k", p=128), w=["ng"], allow_slow_non_contiguous=True)
    P.dve(lambda e: e.tensor_copy(out=identb[:], in_=ident[:]), r=["ident"], w=["identb"])
    P.dve(lambda e: e.memset(ones1[:], 1.0), w=["ones1"])
    P.dve(lambda e: e.memset(epsb[:], EPS), w=["epsb"])

    wstate = {"n": 0}

    def load_w(src_cols_ap):
        i = wstate["n"] % 2
        wstate["n"] += 1
        ncols = src_cols_ap.shape[1]
        key = "wbuf%d" % i
        P.dma("wq%d" % i, wbuf[:, i, :, 0:ncols], src_cols_ap.rearrange("(k p) c -> p k c", p=128),
              w=[key], phys="pool")
        return wbuf[:, i, :, 0:ncols], key

    def token_blocks(NT):
        out = []
        t = 0
        while t < NT:
            n = min(512, NT - t)
            out.append((t, n))
            t += n
        return out

    psrr = {"i": 0}

    def next_ps(lo=0, hi=5):
        i = lo + psrr["i"] % (hi - lo)
        psrr["i"] += 1
        return psb[i], "ps%d" % i

    def proj_fm(l, c0, ncols, NT, evac, rows=128):
        for b0 in range(0, ncols, 256):
            nb = min(256, ncols - b0)
            wv, wk = load_w(w_in[l, :, c0 + b0:c0 + b0 + nb])
            for f0 in range(0, nb, 128):
                m = min(128, nb - f0)
                for (t0, n) in token_blocks(NT):
                    pt, pk = next_ps()
                    for k in range(16):
                        P.pe(lambda e, k=k, pt=pt, wv=wv, f0=f0, m=m, t0=t0, n=n: e.matmul(
                            pt[0:m, 0:n], lhsT=wv[:, k, f0:f0 + m], rhs=hT[:, k, t0:t0 + n],
                            start=(k == 0), stop=(k == 15)), r=[wk, "hT"], w=[pk])
                    evac((b0 + f0) // 128, t0, n, pt[0:m, 0:n], pk)
                    tick()

    def proj_tm(l, c0, ncols, tiles, evac):
        blocks = []
        for b0 in range(0, ncols, 256):
            nb = min(256, ncols - b0)
            blocks.append((b0, nb) + load_w(w_in[l, :, c0 + b0:c0 + b0 + nb]))
        assert len(blocks) <= 2
        for ti, (t0, rows) in enumerate(tiles):
            pt, pk = next_ps()
            for (b0, nb, wv, wk) in blocks:
                for k in range(16):
                    P.pe(lambda e, k=k, pt=pt, wv=wv, b0=b0, nb=nb, t0=t0, rows=rows: e.matmul(
                        pt[0:rows, b0:b0 + nb], lhsT=hT[:, k, t0:t0 + rows], rhs=wv[:, k, :],
                        start=(k == 0), stop=(k == 15)), r=[wk, "hT"], w=[pk])
            evac(ti, rows, pt[0:rows, 0:ncols], pk)
            tick()

    def transpose_to(dst_fn, src_ap, rows, ncol_tiles, dt, skey, evac):
        pt, pk = next_ps(6, 8)
        idn = identb if dt == BF16 else ident
        pv = pt[:].bitcast(BF16)[:, 0:ncol_tiles * 128] if dt == BF16 else pt[:, 0:ncol_tiles * 128]
        pv = pv.rearrange("p (a b) -> p a b", a=ncol_tiles)
        for j in range(ncol_tiles):
            P.pe(lambda e, j=j, pv=pv: e.transpose(pv[:, j, 0:rows], src_ap[0:rows, j * 128:(j + 1) * 128],
                                                   idn[0:rows, 0:rows]),
                 r=[skey, "ident", "identb"], w=[pk])
        evac(pv[:, :, 0:rows], pk)

    def layer_params(l):
        P.barrier()
        k = "lp"
        P.dma("ld", wa2f[:], w_a2[l], w=["wa2f"])
        P.dve(lambda e: e.tensor_copy(out=wa2[:], in_=wa2f[:]), r=["wa2f"], w=[k])
        P.dma("ld", barowf[:], b_a[l:l + 1, :], w=["barowf"])
        P.dve(lambda e: e.tensor_copy(out=barow[:], in_=barowf[:]), r=["barowf"], w=[k])
        P.dma("ld", bafm[:], b_a[l].rearrange("(a p) -> p a", p=128), w=[k], allow_slow_non_contiguous=True)
        P.dma("ld", glag[:], gla_g[l].rearrange("(a p) -> p a", p=128), w=[k], allow_slow_non_contiguous=True)
        P.dma("ld", sgug[:], bass.AP(sgu_g.tensor, l * W, [[0, 128], [1, W]]), w=[k])
        P.dma("ld", sgub[:], bass.AP(sgu_b.tensor, l * 512, [[0, 128], [128, 4], [1, 128]]), w=[k])
        P.dma("ld", w00[:], bass.AP(sgu_w.tensor, l * 4 * 16384, [[0, 128], [16384, 4]]), w=[k],
              allow_slow_non_contiguous=True)
        P.dma("ld", b00[:], bass.AP(sgu_b.tensor, l * 512, [[0, 128], [128, 4]]), w=[k],
              allow_slow_non_contiguous=True)
        P.dma("ld", cw[:], conv_w[l].rearrange("j (a p) -> p j a", p=128), w=[k], allow_slow_non_contiguous=True)
        P.dma("ld", dsk[:], ssm_d[l].rearrange("(a p) -> p a", p=128), w=[k], allow_slow_non_contiguous=True)
        P.dma("ld", glub[:], glu_b[l].rearrange("(a p) -> p a", p=128), w=[k], allow_slow_non_contiguous=True)
        P.dma("wq0", gluw[:], glu_w[l].rearrange("(a p) c -> p a c", p=128), w=[k], phys="pool")
        CV.reset()
        swn = CV.get([128, 4, 128])
        P.dma("ld", swn, sgu_w[l].rearrange("h t s -> t h s"), w=["swn"])
        for h in range(4):
            pt, pk = next_ps(6, 8)
            P.pe(lambda e, h=h, pt=pt: e.transpose(pt[:, 0:128], swn[:, h, :], ident[:]), r=["swn", "ident"], w=[pk])
            P.dve(lambda e, h=h, pt=pt: e.tensor_tensor(out=sguw[:, h, :], in0=pt[:, 0:128], in1=triu[:], op=ALU.mult),
                  r=[pk, "triu"], w=[k])
        P.dve(lambda e: e.memset(S32[:], 0.0), w=["S32"])
        P.dve(lambda e: e.memset(Sbf[:], 0.0), w=["Sbf"])
        P.dve(lambda e: e.memset(zcar[:], 0.0), w=["zcar"])
        P.dve(lambda e: e.tensor_scalar(out=nbafm[:], in0=bafm[:], scalar1=-1.0, scalar2=None, op0=ALU.mult), r=[k], w=[k])
        P.dve(lambda e: e.memset(Xc[:], 0.0), w=["Xc"])
        s5_params(l)
        P.barrier()

    def s5_params(l):
        kk = "s5p"
        g = CV.get
        LR = g([128, 16]); LI = g([128, 16]); DT = g([128, 16])
        for hh in range(2):
            rows = slice(hh * 64, hh * 64 + 64)
            P.dma("ld", LR[rows, :], bass.AP(lam_re.tensor, l * 2048 + hh * 64, [[1, 64], [128, 16]]), w=["LR"],
                  allow_slow_non_contiguous=True)
            P.dma("ld", LI[rows, :], bass.AP(lam_im.tensor, l * 2048 + hh * 64, [[1, 64], [128, 16]]), w=["LI"],
                  allow_slow_non_contiguous=True)
            P.dma("ld", DT[rows, :], bass.AP(log_dt.tensor, l * 32 + hh, [[0, 64], [2, 16]]), w=["DT"],
                  allow_slow_non_contiguous=True)
        P.act(lambda e: e.activation(out=DT, in_=DT, func=AF.Exp), r=["DT"], w=["DT"])
        lrd = g([128, 16]); th = g([128, 16]); mag = g([128, 16]); c = g([128, 16]); s = g([128, 16])
        t1 = g([128, 16]); t2 = g([128, 16]); hp = g([128, 1])
        P.dve(lambda e: e.memset(hp, float(np.pi / 2)), w=["hp"])
        P.dve(lambda e: e.tensor_tensor(out=lrd, in0=LR, in1=DT, op=ALU.mult), r=["LR", "DT"], w=["lrd"])
        P.dve(lambda e: e.tensor_tensor(out=th, in0=LI, in1=DT, op=ALU.mult), r=["LI", "DT"], w=["th"])
        P.act(lambda e: e.activation(out=mag, in_=lrd, func=AF.Exp), r=["lrd"], w=["mag"])
        P.act(lambda e: e.activation(out=s, in_=th, func=AF.Sin, scale=1.0 / 16), r=["th"], w=["cs"])
        P.act(lambda e: e.activation(out=c, in_=th, func=AF.Sin, scale=1.0 / 16, bias=hp), r=["th", "hp", "cs"], w=["cs"])

        def csq(cr, ci, key):
            P.dve(lambda e: e.tensor_tensor(out=t1, in0=cr, in1=ci, op=ALU.mult), r=[key], w=["t1"])
            P.dve(lambda e: e.tensor_tensor(out=t2, in0=ci, in1=ci, op=ALU.mult), r=[key], w=["t2"])
            P.dve(lambda e: e.tensor_tensor(out=cr, in0=cr, in1=cr, op=ALU.mult), r=[key], w=[key])
            P.dve(lambda e: e.tensor_tensor(out=cr, in0=cr, in1=t2, op=ALU.subtract), r=[key, "t2"], w=[key])
            P.dve(lambda e: e.tensor_scalar(out=ci, in0=t1, scalar1=2.0, scalar2=None, op0=ALU.mult), r=["t1"], w=[key])
        for _ in range(4):
            csq(c, s, "cs")
        P.dve(lambda e: e.tensor_tensor(out=AR[:], in0=mag, in1=c, op=ALU.mult), r=["mag", "cs"], w=["AR"])
        P.dve(lambda e: e.tensor_tensor(out=AI[:], in0=mag, in1=s, op=ALU.mult), r=["mag", "cs"], w=["AI"])
        P.dve(lambda e: e.tensor_copy(out=TC[:, :, 0], in_=c), r=["cs"], w=["TC"])
        P.dve(lambda e: e.tensor_copy(out=TS[:, :, 0], in_=s), r=["cs"], w=["TS"])
        n = 1
        tA = g([128, 16, 64]); tB = g([128, 16, 64])
        while n < 128:
            cn = bc(TC[:, :, n - 1:n], [128, 16, n]); sn = bc(TS[:, :, n - 1:n], [128, 16, n])
            a = tA[:, :, 0:n]; b = tB[:, :, 0:n]
            P.dve(lambda e, a=a, cn=cn, n=n: e.tensor_tensor(out=a, in0=TC[:, :, 0:n], in1=cn, op=ALU.mult), r=["TC"], w=["tA"])
            P.dve(lambda e, b=b, sn=sn, n=n: e.tensor_tensor(out=b, in0=TS[:, :, 0:n], in1=sn, op=ALU.mult), r=["TS"], w=["tB"])
            P.dve(lambda e, a=a, b=b, n=n: e.tensor_tensor(out=TC[:, :, n:2 * n], in0=a, in1=b, op=ALU.subtract), r=["tA", "tB"], w=["TC"])
            P.dve(lambda e, a=a, sn=sn, n=n: e.tensor_tensor(out=a, in0=TC[:, :, 0:n], in1=sn, op=ALU.mult), r=["TC", "TS"], w=["tA"])
            P.dve(lambda e, b=b, cn=cn, n=n: e.tensor_tensor(out=b, in0=TS[:, :, 0:n], in1=cn, op=ALU.mult), r=["TS", "TC"], w=["tB"])
            P.dve(lambda e, a=a, b=b, n=n: e.tensor_tensor(out=TS[:, :, n:2 * n], in0=a, in1=b, op=ALU.add), r=["tA", "tB"], w=["TS"])
            n *= 2
        P.dve(lambda e: e.tensor_copy(out=MAG[:], in_=mag), r=["mag"], w=["MAG"])
        den = g([128, 16]); cr_ = g([128, 16]); ci_ = g([128, 16]); am1 = g([128, 16])
        P.dve(lambda e: e.tensor_tensor(out=t1, in0=LR, in1=LR, op=ALU.mult), r=["LR"], w=["t1"])
        P.dve(lambda e: e.tensor_tensor(out=t2, in0=LI, in1=LI, op=ALU.mult), r=["LI"], w=["t2"])
        P.dve(lambda e: e.tensor_tensor(out=den, in0=t1, in1=t2, op=ALU.add), r=["t1", "t2"], w=["den"])
        P.dve(lambda e: e.reciprocal(out=den, in_=den), r=["den"], w=["den"])
        P.dve(lambda e: e.tensor_scalar(out=am1, in0=AR[:], scalar1=-1.0, scalar2=None, op0=ALU.add), r=["AR"], w=["am1"])
        P.dve(lambda e: e.tensor_tensor(out=t1, in0=am1, in1=LR, op=ALU.mult), r=["am1", "LR"], w=["t1"])
        P.dve(lambda e: e.tensor_tensor(out=t2, in0=AI[:], in1=LI, op=ALU.mult), r=["AI", "LI"], w=["t2"])
        P.dve(lambda e: e.tensor_tensor(out=cr_, in0=t1, in1=t2, op=ALU.add), r=["t1", "t2"], w=["cr"])
        P.dve(lambda e: e.tensor_tensor(out=cr_, in0=cr_, in1=den, op=ALU.mult), r=["cr", "den"], w=["cr"])
        P.dve(lambda e: e.tensor_tensor(out=t1, in0=AI[:], in1=LR, op=ALU.mult), r=["AI", "LR"], w=["t1"])
        P.dve(lambda e: e.tensor_tensor(out=t2, in0=am1, in1=LI, op=ALU.mult), r=["am1", "LI"], w=["t2"])
        P.dve(lambda e: e.tensor_tensor(out=ci_, in0=t1, in1=t2, op=ALU.subtract), r=["t1", "t2"], w=["ci"])
        P.dve(lambda e: e.tensor_tensor(out=ci_, in0=ci_, in1=den, op=ALU.mult), r=["ci", "den"], w=["ci"])
        BMr = g([128, 16, 32]); BMi = g([128, 16, 32]); BBr = g([128, 16, 32]); BBi = g([128, 16, 32])
        u1 = g([128, 16, 32]); u2 = g([128, 16, 32])
        P.dve(lambda e: e.memset(BMr, 0.0), w=["BMr"])
        P.dve(lambda e: e.memset(BMi, 0.0), w=["BMi"])
        for hh in range(2):
            rows = slice(hh * 64, hh * 64 + 64)
            for (dst, src, key) in ((BMr, b_re, "BMr"), (BMi, b_im, "BMi")):
                P.dma("ld", dst[rows, :, hh * 16:hh * 16 + 16],
                      bass.AP(src.tensor, l * 32768 + hh * 1024, [[16, 64], [2048, 16], [1, 16]]), w=[key])
        crb = bc(cr_.unsqueeze(2), [128, 16, 32]); cib = bc(ci_.unsqueeze(2), [128, 16, 32])
        P.dve(lambda e: e.tensor_tensor(out=u1, in0=BMr, in1=crb, op=ALU.mult), r=["BMr", "cr"], w=["u1"])
        P.dve(lambda e: e.tensor_tensor(out=u2, in0=BMi, in1=cib, op=ALU.mult), r=["BMi", "ci"], w=["u2"])
        P.dve(lambda e: e.tensor_tensor(out=BBr, in0=u1, in1=u2, op=ALU.subtract), r=["u1", "u2"], w=["BBr"])
        P.dve(lambda e: e.tensor_tensor(out=u1, in0=BMi, in1=crb, op=ALU.mult), r=["BMi", "cr", "BBr"], w=["u1"])
        P.dve(lambda e: e.tensor_tensor(out=u2, in0=BMr, in1=cib, op=ALU.mult), r=["BMr", "ci", "BBr"], w=["u2"])
        P.dve(lambda e: e.tensor_tensor(out=BBi, in0=u1, in1=u2, op=ALU.add), r=["u1", "u2"], w=["BBi"])
        P.dve(lambda e: e.memset(BL[:], 0.0), w=["BL"])
        for kt in range(4):
            for ci, src in enumerate((BBr, BBi)):
                pt, pk = next_ps(6, 8)
                P.pe(lambda e, pt=pt, src=src, kt=kt: e.transpose(pt[:, 0:128], src[:, 4 * kt:4 * kt + 4, :].rearrange("p a b -> p (a b)"), ident[:]),
                     r=["BBr", "BBi", "ident"], w=[pk])
                for j in range(4):
                    P.dve(lambda e, pt=pt, kt=kt, ci=ci, j=j: e.tensor_copy(
                        out=BL[32 * j:32 * j + 32, 4 * kt + j, ci, :], in_=pt[32 * j:32 * j + 32, 0:128]), r=[pk], w=["BL"])
        CMr = g([128, 16, 32]); CMi = g([128, 16, 32]); cn_ = g([128, 128])
        P.dve(lambda e: e.memset(CMr, 0.0), w=["CMr"])
        P.dve(lambda e: e.memset(CMi, 0.0), w=["CMi"])
        for (dst, src, key) in ((CMr, c_re, "CMr"), (CMi, c_im, "CMi")):
            for kt in range(4):
                for dd in range(2):
                    P.dma("ld", cn_[:, dd * 64:(dd + 1) * 64], src[l].rearrange("g p n -> (g p) n")[kt * 128:(kt + 1) * 128, :], w=["cn"])
                pt, pk = next_ps(6, 8)
                P.pe(lambda e, pt=pt: e.transpose(pt[:, 0:128], cn_, ident[:]), r=["cn", "ident"], w=[pk])
                pv = pt[:, 0:128].rearrange("q (j a p) -> q j a p", j=4, a=2)
                for hh in range(2):
                    rows = slice(hh * 64, hh * 64 + 64)
                    P.dve(lambda e, dst=dst, pv=pv, rows=rows, hh=hh, kt=kt: e.tensor_copy(
                        out=dst[rows, 4 * kt:4 * kt + 4, hh * 16:hh * 16 + 16], in_=pv[rows, :, hh, :]), r=[pk], w=[key])
        P.dve(lambda e: e.memset(CL[:], 0.0), w=["CL"])
        for gp in range(16):
            j = gp % 4
            P.dve(lambda e, gp=gp, j=j: e.tensor_copy(out=CL[:, gp, 0, 32 * j:32 * j + 32], in_=CMr[:, gp, :]), r=["CMr"], w=["CL"])
            P.dve(lambda e, gp=gp, j=j: e.tensor_scalar(out=CL[:, gp, 1, 32 * j:32 * j + 32], in0=CMi[:, gp, :], scalar1=-1.0, scalar2=None,
                                                        op0=ALU.mult), r=["CMi"], w=["CL"])

        if l == 0:
            dbg_dump("AR", AR[:], ["AR"]); dbg_dump("AI", AI[:], ["AI"]); dbg_dump("MAG", MAG[:], ["MAG"])
            dbg_dump("TC", TC[:], ["TC"]); dbg_dump("TS", TS[:], ["TS"])
            dbg_dump("BL", BL[:], ["BL"]); dbg_dump("CL", CL[:], ["CL"])
            dbg_dump("cr", cr_, ["cr"]); dbg_dump("ci", ci_, ["ci"]); dbg_dump("BBr", BBr, ["BBr"]); dbg_dump("CMr", CMr, ["CMr"])

    def layer_pass(l, h):
        NT = 1024 + (NS if h == 1 else 0)
        tiles = [(i * 128, 128) for i in range(8)] + ([(1024, NS)] if h == 1 else [])
        xin = (xp, xs) if l == 0 else (x1[0:2048, :], x1[2048:2048 + NS, :])
        xout = (x1[0:2048, :], x1[2048:2048 + NS, :]) if l == 0 else (x2[0:2048, :], x2[2048:2048 + NS, :])

        def xrows(pair, ti, rows):
            return pair[0][h * 1024 + ti * 128:h * 1024 + ti * 128 + rows, :] if rows == 128 else pair[1]

        P.barrier()
        CV.reset()
        xt = [CV.get([128, D]) for _ in range(4)]
        xn = [CV.get([128, D], BF16) for _ in range(4)]
        junk = CV.get([128, D], BF16)
        ssq = CV.get([128, 20])
        def p0_a(ti):
            t0, rows = tiles[ti]
            b = ti % 4
            P.dma("xl%d" % b, xt[b][0:rows, :], xrows(xin, ti, rows), w=["xt%d" % b])
            P.act(lambda e: e.activation(out=junk[0:rows, :], in_=xt[b][0:rows, :], func=AF.Square,
                                         accum_out=ssq[0:rows, ti:ti + 1]),
                  r=["xt%d" % b], w=["junk", "ssq%d" % ti])
            P.act(lambda e: e.activation(out=ssq[0:rows, ti:ti + 1], in_=ssq[0:rows, ti:ti + 1], func=AF.Sqrt,
                                         scale=1.0 / D, bias=epsb[0:rows, :]),
                  r=["ssq%d" % ti, "epsb"], w=["ssq%d" % ti])
            P.dve(lambda e: e.reciprocal(out=ssq[0:rows, ti:ti + 1], in_=ssq[0:rows, ti:ti + 1]),
                  r=["ssq%d" % ti], w=["ssq%d" % ti])

        def p0_b(ti):
            t0, rows = tiles[ti]
            b = ti % 4
            P.act(lambda e: e.activation(out=xn[b][0:rows, :], in_=xt[b][0:rows, :], func=AF.Copy,
                                         scale=ssq[0:rows, ti:ti + 1]),
                  r=["xt%d" % b, "ssq%d" % ti], w=["xn%d" % b])
            for k4 in range(4):
                def ev(pv, pk, k4=k4):
                    P.dve(lambda e: e.tensor_tensor(out=hT[:, 4 * k4:4 * k4 + 4, t0:t0 + rows], in0=pv,
                                                    in1=bc(ng[:, l, 4 * k4:4 * k4 + 4].unsqueeze(2), [128, 4, rows]), op=ALU.mult),
                          r=[pk, "ng"], w=["hT"])
                transpose_to(None, xn[b][:, k4 * 512:(k4 + 1) * 512], rows, 4, BF16, "xn%d" % b, ev)

        p0_a(0)
        if len(tiles) > 1:
            p0_a(1)
        for ti in range(len(tiles)):
            if ti + 2 < len(tiles):
                p0_a(ti + 2)
            p0_b(ti)

        if (SUB if h == 1 else STAGE) < 2:
            return
        P.barrier()
        CV.reset()
        qT = CV.get([128, 2, 1040]); kT = CV.get([128, 2, 1040])
        vtok = CV.get([128, 9, W], BF16)
        sga = CV.get([128, 4, 1040], BF16)
        alrT = CV.get([16, 1040], BF16)

        def ev_q(ft, t0, n, pv, pk):
            P.act(lambda e: e.activation(out=qT[:, ft, t0:t0 + n], in_=pv, func=AF.Copy), r=[pk], w=["qT"])
        proj_fm(l, SEG["q"], 256, NT, ev_q)

        def ev_k(ft, t0, n, pv, pk):
            P.act(lambda e: e.activation(out=kT[:, ft, t0:t0 + n], in_=pv, func=AF.Copy), r=[pk], w=["kT"])
        proj_fm(l, SEG["k"], 256, NT, ev_k)

        def ev_a(ft, t0, n, pv, pk):
            P.act(lambda e: e.activation(out=alrT[:, t0:t0 + n], in_=pv, func=AF.Copy), r=[pk], w=["alrT"])
        proj_fm(l, SEG["a"], 16, NT, ev_a)

        def ev_ga(ft, t0, n, pv, pk):
            P.act(lambda e: e.activation(out=sga[:, ft, t0:t0 + n], in_=pv, func=AF.Silu), r=[pk], w=["sga"])
        proj_fm(l, SEG["ga"], 512, NT, ev_ga)

        def ev_v(ti, rows, pv, pk):
            P.act(lambda e: e.activation(out=vtok[0:rows, ti, :], in_=pv, func=AF.Copy), r=[pk], w=["vtok"])
        proj_tm(l, SEG["v"], 512, tiles, ev_v)

        sp = CV.get([128, 256]); eb = CV.get([128, 2, 128]); enb = CV.get([128, 2, 128])
        qmP = [[CV.get([128, 2, 128], BF16) for _ in range(2)] for _ in range(2)]
        kmP = [[CV.get([128, 2, 128], BF16) for _ in range(2)] for _ in range(2)]
        qfP = [CV.get([128, 2, 128], BF16) for _ in range(2)]
        kf = CV.get([128, 2, 128], BF16)
        ktokP = [CV.get([128, 256], BF16) for _ in range(2)]
        eblP = [CV.get([128, 2]) for _ in range(2)]
        scm = CV.get([128, 128], BF16)
        on = CV.get([128, W], BF16)
        osq = CV.get([128, 8]); tmpS = CV.get([128, 128])
        ojunk = CV.get([128, 128])

        def gla_out(rows, t0, opsum, opk):
            for hd in range(4):
                P.act(lambda e, hd=hd: e.activation(out=ojunk[0:rows, :], in_=opsum[0:rows, hd * 128:(hd + 1) * 128],
                                                    func=AF.Square, accum_out=osq[0:rows, hd:hd + 1]),
                      r=[opk], w=["ojunk", "osq"])
            P.act(lambda e: e.activation(out=osq[0:rows, 0:4], in_=osq[0:rows, 0:4], func=AF.Sqrt, scale=1.0 / 128,
                                         bias=epsb[0:rows, :]), r=["osq", "epsb"], w=["osq"])
            P.dve(lambda e: e.reciprocal(out=osq[0:rows, 0:4], in_=osq[0:rows, 0:4]), r=["osq"], w=["osq"])
            P.dve(lambda e: e.tensor_tensor(out=on[0:rows, :].rearrange("p (a b) -> p a b", a=4),
                                            in0=opsum[0:rows, :].rearrange("p (a b) -> p a b", a=4),
                                            in1=bc(osq[0:rows, 0:4].unsqueeze(2), [rows, 4, 128]), op=ALU.mult),
                  r=[opk, "osq"], w=["on"])

            def ev(pv, pk):
                for hd in range(4):
                    P.dve(lambda e, hd=hd: e.scalar_tensor_tensor(out=mT[:, hd, t0:t0 + rows], in0=pv[:, hd, :],
                                                                  scalar=glag[:, hd:hd + 1], in1=sga[:, hd, t0:t0 + rows],
                                                                  op0=ALU.mult, op1=ALU.mult),
                          r=[pk, "lp", "sga"], w=["mT"])
            if False:
                dbg_dump("on%d" % (t0 // 128), on, ["on"]); dbg_dump("osq%d" % (t0 // 128), osq, ["osq"])
            transpose_to(None, on, rows, 4, BF16, "on", ev)
            if False:
                dbg_dump("mt%d" % (t0 // 128), mT[:, 0:4, t0:t0 + 128], ["mT"])
                dbg_dump("sga%d" % (t0 // 128), sga[:, :, t0:t0 + 128], ["sga"])

        def stage1(ti):
            t0 = ti * 128
            pz = ti % 2
            qf = qfP[pz]; qm = qmP[pz]; km = kmP[pz]; ktok = ktokP[pz]; ebl = eblP[pz]
            zp, zk = next_ps()
            P.pe(lambda e: e.matmul(zp[:, 0:256], lhsT=alrT[:, t0:t0 + 128], rhs=wa2[:], start=True, stop=False),
                 r=["alrT", "lp"], w=[zk])
            P.pe(lambda e: e.matmul(zp[:, 0:256], lhsT=ones1[:], rhs=barow[:], start=False, stop=True),
                 r=["ones1", "lp"], w=[zk])
            P.act(lambda e: e.activation(out=sp, in_=zp[:, 0:256], func=AF.Exp, scale=-1.0), r=[zk], w=["sp"])
            P.act(lambda e: e.activation(out=sp, in_=sp, func=AF.Ln, bias=1.0), r=["sp"], w=["sp"])
            cp, ck = next_ps()
            for hp_ in range(2):
                P.pe(lambda e, hp_=hp_: e.matmul(cp[:, hp_ * 128:(hp_ + 1) * 128], lhsT=sp[:, hp_ * 128:(hp_ + 1) * 128],
                                                 rhs=triu[:], start=True, stop=True), r=["sp", "triu"], w=[ck])
            cpv = cp[:, 0:256].rearrange("p (a b) -> p a b", a=2)
            P.act(lambda e: e.activation(out=eb, in_=cpv, func=AF.Exp, scale=-1.0 / 16), r=[ck], w=["eb"])
            P.act(lambda e: e.activation(out=enb, in_=cpv, func=AF.Exp, scale=1.0 / 16), r=[ck], w=["enb"])
            P.dve(lambda e: e.scalar_tensor_tensor(out=qf, in0=qT[:, :, t0:t0 + 128], scalar=0.125, in1=eb,
                                                   op0=ALU.mult, op1=ALU.mult), r=["qT", "eb"], w=["qf%d" % pz])
            P.dve(lambda e: e.tensor_tensor(out=kf, in0=kT[:, :, t0:t0 + 128], in1=enb, op=ALU.mult),
                  r=["kT", "enb"], w=["kf"])
            P.dve(lambda e: e.tensor_copy(out=ebl, in_=eb[:, :, 127]), r=["eb"], w=["ebl%d" % pz])
            for h2 in range(2):
                P.dve(lambda e, h2=h2: e.tensor_scalar(out=qm[h2], in0=qf, scalar1=hmask[:, h2:h2 + 1], scalar2=None, op0=ALU.mult),
                      r=["qf%d" % pz, "hmask"], w=["qm%d_%d" % (pz, h2)])
                P.dve(lambda e, h2=h2: e.tensor_scalar(out=km[h2], in0=kf, scalar1=hmask[:, h2:h2 + 1], scalar2=None, op0=ALU.mult),
                      r=["kf", "hmask"], w=["km%d_%d" % (pz, h2)])

            def ev_kt(pv, pk):
                P.act(lambda e: e.activation(out=ktok.rearrange("p (a b) -> p a b", a=2), in_=pv, func=AF.Copy), r=[pk], w=["ktok%d" % pz])
            transpose_to(None, kf.rearrange("p a b -> p (a b)"), 128, 2, BF16, "kf", ev_kt)

        def stage2(ti):
            t0 = ti * 128
            pz = ti % 2
            qf = qfP[pz]; qm = qmP[pz]; km = kmP[pz]; ktok = ktokP[pz]; ebl = eblP[pz]
            op_, ok = next_ps()
            for hd in range(4):
                hp_, h2 = hd // 2, hd % 2
                s_p, s_k = next_ps()
                P.pe(lambda e, hp_=hp_, h2=h2: e.matmul(s_p[:, 0:128], lhsT=km[h2][:, hp_, :], rhs=qf[:, hp_, :],
                                                        start=True, stop=True), r=["km%d_%d" % (pz, h2), "qf%d" % pz], w=[s_k])
                P.dve(lambda e: e.tensor_tensor(out=scm, in0=s_p[:, 0:128], in1=triu[:], op=ALU.mult),
                      r=[s_k, "triu"], w=["scm"])
                P.pe(lambda e, hd=hd: e.matmul(op_[:, hd * 128:(hd + 1) * 128], lhsT=scm, rhs=vtok[:, ti, hd * 128:(hd + 1) * 128],
                                               start=True, stop=False), r=["scm", "vtok"], w=[ok])
                P.pe(lambda e, hd=hd, hp_=hp_, h2=h2: e.matmul(op_[:, hd * 128:(hd + 1) * 128], lhsT=qm[h2][:, hp_, :], rhs=Sbf[:, hp_, :],
                                                               start=False, stop=True), r=["qm%d_%d" % (pz, h2), "Sbf"], w=[ok])
            gla_out(128, t0, op_, ok)
            for hp_ in range(2):
                kv, kvk = next_ps()
                P.pe(lambda e, hp_=hp_: e.matmul(kv[:, 0:256], lhsT=ktok[:, hp_ * 128:(hp_ + 1) * 128],
                                                 rhs=vtok[:, ti, hp_ * 256:(hp_ + 1) * 256], start=True, stop=True),
                     r=["ktok%d" % pz, "vtok"], w=[kvk])
                for h2 in range(2):
                    rs = slice(h2 * 64, h2 * 64 + 64)
                    P.dve(lambda e, rs=rs, h2=h2, hp_=hp_: e.tensor_tensor(out=tmpS[rs, :], in0=kv[rs, h2 * 128:(h2 + 1) * 128],
                                                                          in1=S32[rs, hp_, :], op=ALU.add),
                          r=[kvk, "S32"], w=["tmpS"])
                    P.dve(lambda e, rs=rs, hp_=hp_: e.tensor_scalar(out=S32[rs, hp_, :], in0=tmpS[rs, :], scalar1=ebl[rs, hp_:hp_ + 1],
                                                                    scalar2=None, op0=ALU.mult), r=["tmpS", "ebl%d" % pz], w=["S32"])
            P.act(lambda e: e.activation(out=Sbf[:], in_=S32[:], func=AF.Copy), r=["S32"], w=["Sbf"])

        stage1(0)
        for ti in range(8):
            if ti + 1 < 8:
                stage1(ti + 1)
            stage2(ti)
        if h == 1:
            P.dma("st", bass.AP(o_gla_p.tensor, l * 32768, [[128, 128], [16384, 2], [1, 128]]), S32[:], r=["S32"])
            if not NOSAMP:
                gla_samples(l, qT, kT, vtok, alrT, gla_out)

        if (SUB if h == 1 else STAGE) < 3:
            return
        P.barrier()
        CV.reset(0, MAINLIM)
        s5ctx = s5_start(l, h, NT)
        sgb = CV.get([128, 4, 1040], BF16); ug = CV.get([128, 4, 1040], BF16)
        vn = CV.get([128, W], BF16); vnf = CV.get([128, W]); st6 = CV.get([128, 4, 6]); mv = CV.get([128, 4, 2])
        tmpb = CV.get([128, 4, 128])
        vnT = CV.get([128, 4, NS])

        def ev_gb(ft, t0, n, pv, pk):
            P.act(lambda e: e.activation(out=sgb[:, ft, t0:t0 + n], in_=pv, func=AF.Silu), r=[pk], w=["sgb"])
        proj_fm(l, SEG["gb"], 512, NT, ev_gb)

        def ev_ub(ft, t0, n, pv, pk):
            P.dve(lambda e: e.tensor_tensor(out=ug[:, ft, t0:t0 + n], in0=pv, in1=sgb[:, ft, t0:t0 + n], op=ALU.mult),
                  r=[pk, "sgb"], w=["ug"])
        proj_fm(l, SEG["ub"], 512, NT, ev_ub)

        def ev_vb(ti, rows, pv, pk):
            t0 = tiles[ti][0]
            for hd in range(4):
                P.dve(lambda e, hd=hd: e.bn_stats(out=st6[0:rows, hd, :], in_=pv[:, hd * 128:(hd + 1) * 128]), r=[pk], w=["st6"])
                P.dve(lambda e, hd=hd: e.bn_aggr(out=mv[0:rows, hd, :], in_=st6[0:rows, hd, :]), r=["st6"], w=["mv"])
            P.act(lambda e: e.activation(out=mv[0:rows, :, 1], in_=mv[0:rows, :, 1], func=AF.Sqrt, bias=epsb[0:rows, :]),
                  r=["mv", "epsb"], w=["mv"])
            P.dve(lambda e: e.reciprocal(out=mv[0:rows, :, 1], in_=mv[0:rows, :, 1]), r=["mv"], w=["mv"])
            for hd in range(4):
                P.dve(lambda e, hd=hd: e.tensor_scalar(out=vnf[0:rows, hd * 128:(hd + 1) * 128], in0=pv[:, hd * 128:(hd + 1) * 128],
                                                       scalar1=mv[0:rows, hd, 0:1], scalar2=mv[0:rows, hd, 1:2],
                                                       op0=ALU.subtract, op1=ALU.mult), r=[pk, "mv"], w=["vnf"])
            P.dve(lambda e: e.tensor_tensor(out=vnf[0:rows, :], in0=vnf[0:rows, :], in1=sgug[0:rows, :], op=ALU.mult),
                  r=["vnf", "lp"], w=["vnf"])
            P.act(lambda e: e.activation(out=vn[0:rows, :], in_=vnf[0:rows, :], func=AF.Copy), r=["vnf"], w=["vn"])
            if rows == 128:
                mp, mk = next_ps()
                for hd in range(4):
                    P.pe(lambda e, hd=hd, mp=mp: e.matmul(mp[:, hd * 128:(hd + 1) * 128], lhsT=vn[:, hd * 128:(hd + 1) * 128],
                                                          rhs=sguw[:, hd, :], start=True, stop=True), r=["vn", "lp"], w=[mk])
                P.dve(lambda e, mp=mp: e.tensor_tensor(out=tmpb, in0=mp[:, :].rearrange("p (a b) -> p a b", a=4), in1=sgub[:], op=ALU.add),
                      r=[mk, "lp"], w=["tmpb"])
                P.dve(lambda e, t0=t0: e.tensor_tensor(out=mT[:, 4:8, t0:t0 + 128], in0=tmpb, in1=ug[:, :, t0:t0 + 128], op=ALU.mult),
                      r=["tmpb", "ug"], w=["mT"])
            else:
                P.dma("st", o_vn_s[l], vnf[0:NS, :], r=["vnf"])

                def ev(pv2, pk2):
                    P.dve(lambda e: e.tensor_tensor(out=vnT, in0=pv2, in1=bc(w00[:].unsqueeze(2), [128, 4, NS]), op=ALU.mult),
                          r=[pk2, "lp"], w=["vnT"])
                    P.dve(lambda e: e.tensor_tensor(out=vnT, in0=vnT, in1=bc(b00[:].unsqueeze(2), [128, 4, NS]), op=ALU.add),
                          r=["vnT", "lp"], w=["vnT"])
                    P.dve(lambda e: e.tensor_tensor(out=mT[:, 4:8, t0:t0 + NS], in0=vnT, in1=ug[:, :, t0:t0 + NS], op=ALU.mult),
                          r=["vnT", "ug"], w=["mT"])
                transpose_to(None, vnf, NS, 4, F32, "vnf", ev)
        proj_tm(l, SEG["vb"], 512, tiles, ev_vb)

        if (SUB if h == 1 else STAGE) < 4:
            return
        P.barrier()
        CV.reset(0, MAINLIM)
        ccT = CV.get([128, 4, 1040], BF16)
        cg = mT[:, 8:12, :]
        zc = CV.get([128, 4, 1042])
        P.dve(lambda e: e.tensor_copy(out=zc[:, :, 0:2], in_=zcar[:]), r=["zcar"], w=["zc"])
        yc = CV.get([128, 4, 1040])

        def ev_cc(ft, t0, n, pv, pk):
            P.act(lambda e: e.activation(out=ccT[:, ft, t0:t0 + n], in_=pv, func=AF.Copy), r=[pk], w=["ccT"])
        proj_fm(l, SEG["cc"], 512, NT, ev_cc)

        def ev_hc(ft, t0, n, pv, pk):
            P.dve(lambda e: e.tensor_tensor(out=zc[:, ft, 2 + t0:2 + t0 + n], in0=pv, in1=ccT[:, ft, t0:t0 + n], op=ALU.mult),
                  r=[pk, "ccT"], w=["zc"])
        proj_fm(l, SEG["hc"], 512, NT, ev_hc)

        def ev_gc(ft, t0, n, pv, pk):
            P.act(lambda e: e.activation(out=cg[:, ft, t0:t0 + n], in_=pv, func=AF.Silu), r=[pk], w=["cg"])
        proj_fm(l, SEG["gc"], 512, NT, ev_gc)

        def ev_cb(ft, t0, n, pv, pk):
            P.dve(lambda e: e.tensor_tensor(out=cg[:, ft, t0:t0 + n], in0=pv, in1=cg[:, ft, t0:t0 + n], op=ALU.mult),
                  r=[pk, "cg"], w=["cg"])
        proj_fm(l, SEG["cb"], 512, NT, ev_cb)
        for ct in range(4):
            P.dve(lambda e, ct=ct: e.tensor_scalar(out=yc[:, ct, 0:1024], in0=zc[:, ct, 0:1024], scalar1=cw[:, 0, ct:ct + 1],
                                                   scalar2=None, op0=ALU.mult), r=["zc", "lp"], w=["yc"])
            P.dve(lambda e, ct=ct: e.scalar_tensor_tensor(out=yc[:, ct, 0:1024], in0=zc[:, ct, 1:1025], scalar=cw[:, 1, ct:ct + 1],
                                                          in1=yc[:, ct, 0:1024], op0=ALU.mult, op1=ALU.add), r=["zc", "lp", "yc"], w=["yc"])
            P.dve(lambda e, ct=ct: e.scalar_tensor_tensor(out=yc[:, ct, 0:1024], in0=zc[:, ct, 2:1026], scalar=cw[:, 2, ct:ct + 1],
                                                          in1=yc[:, ct, 0:1024], op0=ALU.mult, op1=ALU.add), r=["zc", "lp", "yc"], w=["yc"])
        P.dve(lambda e: e.tensor_tensor(out=mT[:, 8:12, 0:1024], in0=yc[:, :, 0:1024], in1=cg[:, :, 0:1024], op=ALU.mult),
              r=["yc", "cg"], w=["mT"])
        if h == 1:
            conv_samples(l, yc, cg, zc)
            for jj in range(2):
                P.dma("st", o_conv_p[l, jj].rearrange("(a p) -> p a", p=128), zc[:, :, 1024 + jj], r=["zc"],
                      allow_slow_non_contiguous=True)
        P.dve(lambda e: e.tensor_copy(out=zcar[:], in_=zc[:, :, 1024:1026]), r=["zc"], w=["zcar"])

        if (SUB if h == 1 else STAGE) < 5:
            return
        while TICK[0] is not None:
            tick()
        P.barrier()
        CV.reset(0, MAINLIM)
        s5_finish(l, h, s5ctx)

        if (SUB if h == 1 else STAGE) < 6:
            return
        P.barrier()
        CV.reset()
        xo = [CV.get([128, 256]) for _ in range(6)]
        cnt = 0
        for cb_ in range(8):
            wv, wk = load_w(w_out[l, :, cb_ * 256:(cb_ + 1) * 256])
            for ti, (t0, rows) in enumerate(tiles):
                b = cnt % 6
                cnt += 1
                P.dma("xo%d" % b, xo[b][0:rows, :], xrows(xin, ti, rows)[:, cb_ * 256:(cb_ + 1) * 256], w=["xo%d" % b])
                pt, pk = next_ps()
                for k in range(16):
                    P.pe(lambda e, k=k, pt=pt, wv=wv, t0=t0, rows=rows: e.matmul(pt[0:rows, 0:256], lhsT=mT[:, k, t0:t0 + rows],
                                                                                rhs=wv[:, k, :], start=(k == 0), stop=(k == 15)),
                         r=[wk, "mT", "mT3"], w=[pk])
                P.dve(lambda e, b=b, pt=pt, rows=rows: e.tensor_tensor(out=xo[b][0:rows, :], in0=pt[0:rows, 0:256], in1=xo[b][0:rows, :],
                                                                      op=ALU.add), r=[pk, "xo%d" % b], w=["xo%d" % b])
                P.dma("xs%d" % b, xrows(xout, ti, rows)[:, cb_ * 256:(cb_ + 1) * 256], xo[b][0:rows, :], r=["xo%d" % b], w=["xscratch"],
                      phys="act")

    def gla_samples(l, qT, kT, vtok, alrT, gla_out):
        g = CV.get
        c0 = 1024
        afm = g([128, 2, NS]); SS = g([128, 2, NS, 128])
        kms = g([NS, NS, 256], BF16, rows=NS); ktk = g([NS, 256], BF16, rows=NS)
        qs = g([128, 2, NS]); kbf = g([128, 2, NS], BF16)
        for hp_ in range(2):
            P.dma("ld", SS[:, hp_], bass.AP(sgla.tensor, l * NS * 32768 + hp_ * 16384, [[128, 128], [32768, NS], [1, 128]]), w=["SS"])
        zp, zk = next_ps()
        for hp_ in range(2):
            P.pe(lambda e, hp_=hp_, zp=zp: e.matmul(zp[:, hp_ * NS:(hp_ + 1) * NS], lhsT=wa2[:, hp_ * 128:(hp_ + 1) * 128],
                                                    rhs=alrT[:, c0:c0 + NS], start=True, stop=True), r=["lp", "alrT"], w=[zk])
        for hp_ in range(2):
            P.act(lambda e, hp_=hp_, zp=zp: e.activation(out=afm[:, hp_, :], in_=zp[:, hp_ * NS:(hp_ + 1) * NS], func=AF.Exp,
                                                         scale=-1.0, bias=nbafm[:, hp_:hp_ + 1]), r=[zk, "lp"], w=["afm"])
        P.act(lambda e: e.activation(out=afm, in_=afm, func=AF.Ln, bias=1.0), r=["afm"], w=["afm"])
        P.act(lambda e: e.activation(out=afm, in_=afm, func=AF.Exp, scale=-1.0 / 16), r=["afm"], w=["afm"])
        P.dve(lambda e: e.tensor_copy(out=kbf, in_=kT[:, :, c0:c0 + NS]), r=["kT"], w=["kbf"])
        pt, pk = next_ps(6, 8)
        pvb = pt[:].bitcast(BF16)
        for hp_ in range(2):
            P.pe(lambda e, hp_=hp_: e.transpose(pvb[0:NS, hp_ * 128:(hp_ + 1) * 128], kbf[:, hp_, :], identb[:]),
                 r=["kbf", "identb"], w=[pk])
        P.act(lambda e: e.activation(out=ktk, in_=pvb[0:NS, 0:256], func=AF.Copy), r=[pk], w=["ktk"])
        P.dve(lambda e: e.tensor_tensor(out=kms, in0=bc(ktk.unsqueeze(1), [NS, NS, 256]), in1=bc(eye16[:].unsqueeze(2), [NS, NS, 256]),
                                        op=ALU.mult), r=["ktk", "eye16"], w=["kms"])
        P.dve(lambda e: e.tensor_scalar(out=qs, in0=qT[:, :, c0:c0 + NS], scalar1=0.125, scalar2=None, op0=ALU.mult), r=["qT"], w=["qs"])
        eyeb = g([128, NS, NS])
        P.dma("ld", eyeb, bass.AP(eye16_d.tensor, 0, [[0, 128], [NS, NS], [1, NS]]), w=["eyeb"])
        qmsm = [g([128, 2, NS, NS]) for _ in range(2)]
        qsm = [g([128, 2, NS]) for _ in range(2)]
        for h2 in range(2):
            P.dve(lambda e, h2=h2: e.tensor_scalar(out=qsm[h2], in0=qs, scalar1=hmask[:, h2:h2 + 1], scalar2=None, op0=ALU.mult),
                  r=["qs", "hmask"], w=["qsm%d" % h2])
            for hp_ in range(2):
                P.dve(lambda e, h2=h2, hp_=hp_: e.tensor_tensor(out=qmsm[h2][:, hp_], in0=eyeb, in1=bc(qsm[h2][:, hp_, :].unsqueeze(2), [128, NS, NS]),
                                                                op=ALU.mult), r=["eyeb", "qsm%d" % h2], w=["qmsm%d" % h2])
        for hp_ in range(2):
            for b in range(NS):
                kv, kvk = next_ps()
                P.pe(lambda e, kv=kv, b=b, hp_=hp_: e.matmul(kv[:, 0:256], lhsT=kms[:, b, hp_ * 128:(hp_ + 1) * 128],
                                                             rhs=vtok[0:NS, 8, hp_ * 256:(hp_ + 1) * 256], start=True, stop=True),
                     r=["kms", "vtok"], w=[kvk])
                for h2 in range(2):
                    rs = slice(h2 * 64, h2 * 64 + 64)
                    P.dve(lambda e, kv=kv, rs=rs, h2=h2, hp_=hp_, b=b: e.scalar_tensor_tensor(
                        out=SS[rs, hp_, b, :], in0=SS[rs, hp_, b, :], scalar=afm[rs, hp_, b:b + 1], in1=kv[rs, h2 * 128:(h2 + 1) * 128],
                        op0=ALU.mult, op1=ALU.add), r=[kvk, "SS", "afm"], w=["SS"])
        for hp_ in range(2):
            P.dma("st", bass.AP(o_gla_s.tensor, l * NS * 32768 + hp_ * 16384, [[128, 128], [32768, NS], [1, 128]]), SS[:, hp_], r=["SS"])
        op_, ok = next_ps()
        for hd in range(4):
            hp_, h2 = hd // 2, hd % 2
            rs = slice(h2 * 64, h2 * 64 + 64)
            for b in range(NS):
                P.pe(lambda e, hd=hd, hp_=hp_, h2=h2, b=b: e.matmul(op_[0:NS, hd * 128:(hd + 1) * 128], lhsT=qmsm[h2][:, hp_, b, :],
                                                                  rhs=SS[:, hp_, b, :], start=(b == 0), stop=(b == NS - 1)),
                     r=["qmsm%d" % h2, "SS"], w=[ok])
        gla_out(NS, c0, op_, ok)

    def conv_samples(l, yc, cg, zc):
        g = CV.get
        c0 = 1024
        cbuf = g([NS, 2 * W], rows=NS); cbT = g([128, 2, 4, NS]); z0t = g([NS, W], rows=NS)
        P.dma("ld", cbuf, sconv[l].rearrange("b j c -> b (j c)"), w=["cbuf"])
        for j in range(2):
            def ev(pv, pk, j=j):
                P.dve(lambda e: e.tensor_copy(out=cbT[:, j], in_=pv), r=[pk], w=["cbT"])
            transpose_to(None, cbuf[:, j * W:(j + 1) * W], NS, 4, F32, "cbuf", ev)
        for ct in range(4):
            P.dve(lambda e, ct=ct: e.tensor_scalar(out=yc[:, ct, c0:c0 + NS], in0=cbT[:, 0, ct, :], scalar1=cw[:, 0, ct:ct + 1],
                                                   scalar2=None, op0=ALU.mult), r=["cbT", "lp", "yc"], w=["yc"])
            P.dve(lambda e, ct=ct: e.scalar_tensor_tensor(out=yc[:, ct, c0:c0 + NS], in0=cbT[:, 1, ct, :], scalar=cw[:, 1, ct:ct + 1],
                                                          in1=yc[:, ct, c0:c0 + NS], op0=ALU.mult, op1=ALU.add), r=["cbT", "lp", "yc"], w=["yc"])
            P.dve(lambda e, ct=ct: e.scalar_tensor_tensor(out=yc[:, ct, c0:c0 + NS], in0=zc[:, ct, 2 + c0:2 + c0 + NS],
                                                          scalar=cw[:, 2, ct:ct + 1], in1=yc[:, ct, c0:c0 + NS], op0=ALU.mult, op1=ALU.add),
                  r=["zc", "lp", "yc"], w=["yc"])
        P.dve(lambda e: e.tensor_tensor(out=mT[:, 8:12, c0:c0 + NS], in0=yc[:, :, c0:c0 + NS], in1=cg[:, :, c0:c0 + NS], op=ALU.mult),
              r=["yc", "cg"], w=["mT"])
        P.dma("st", o_conv_s[l, :, 0, :], cbuf[:, W:2 * W], r=["cbuf"])
        zs = g([128, 4, NS])
        P.dve(lambda e: e.tensor_copy(out=zs, in_=zc[:, :, 2 + c0:2 + c0 + NS]), r=["zc"], w=["zs"])
        pt, pk = next_ps(6, 8)
        for ct in range(4):
            P.pe(lambda e, ct=ct, pt=pt: e.transpose(pt[0:NS, ct * 128:(ct + 1) * 128], zs[:, ct, :], ident[:]), r=["zs", "ident"], w=[pk])
        P.act(lambda e, pt=pt: e.activation(out=z0t, in_=pt[0:NS, 0:W], func=AF.Copy), r=[pk], w=["z0t"])
        P.dma("st", o_conv_s[l, :, 1, :], z0t, r=["z0t"])

    def s5_start(l, h, NT):
        CVS.reset(MAINLIM, ARENA)
        g = CVS.get
        udT = g([128, 4, 1040], BF16)
        gdc = g([128, 4, 128], BF16)

        def ev_ud(ft, t0, n, pv, pk):
            P.act(lambda e: e.activation(out=udT[:, ft, t0:t0 + n], in_=pv, func=AF.Copy), r=[pk], w=["udT"])
        proj_fm(l, SEG["ud"], 512, NT, ev_ud)

        def ev_gd(ft, t0, n, pv, pk):
            P.act(lambda e: e.activation(out=mT[:, 12 + ft, t0:t0 + n], in_=pv, func=AF.Silu), r=[pk], w=["mT3"])
        proj_fm(l, SEG["gd"], 512, NT, ev_gd)

        Er = g([128, 4, 128]); Ei = g([128, 4, 128]); a2 = g([128, 4, 128]); a4 = g([128, 4, 128])
        Xr = g([128, 4, 128], BF16); Xi = g([128, 4, 128], BF16)
        xe = g([128, 2, 16])
        yv = g([128, 4, 128]); y2 = g([128, 4, 128]); sig = g([128, 4, 128])

        def rot(dst_r, dst_i, src_r, src_i, gq, sign, keys_r, key_w):
            tc = TC[:, gq * 4:(gq + 1) * 4, :]; ts = TS[:, gq * 4:(gq + 1) * 4, :]
            ap_, ak = next_ps()
            a1p = ap_[:, :].rearrange("p (a b) -> p a b", a=4)
            bp_, bk = next_ps()
            a3p = bp_[:, :].rearrange("p (a b) -> p a b", a=4)
            P.dve(lambda e: e.tensor_tensor(out=a1p, in0=src_r, in1=tc, op=ALU.mult), r=keys_r + ["TC"], w=[ak])
            P.dve(lambda e: e.tensor_tensor(out=a2, in0=src_i, in1=ts, op=ALU.mult), r=keys_r + ["TS"], w=["a2"])
            P.dve(lambda e: e.tensor_tensor(out=a3p, in0=src_i, in1=tc, op=ALU.mult), r=keys_r + ["TC"], w=[bk])
            P.dve(lambda e: e.tensor_tensor(out=a4, in0=src_r, in1=ts, op=ALU.mult), r=keys_r + ["TS"], w=["a4"])
            P.dve(lambda e: e.tensor_tensor(out=dst_r, in0=a1p, in1=a2, op=(ALU.subtract if sign > 0 else ALU.add)),
                  r=[ak, "a2"], w=[key_w + "r"])
            P.dve(lambda e: e.tensor_tensor(out=dst_i, in0=a3p, in1=a4, op=(ALU.add if sign > 0 else ALU.subtract)),
                  r=[bk, "a4"], w=[key_w + "i"])

        def y_evac(kt, pv, pk, t0, n):
            P.dve(lambda e: e.scalar_tensor_tensor(out=yv[:, kt, 0:n], in0=udT[:, kt, t0:t0 + n], scalar=dsk[:, kt:kt + 1],
                                                   in1=pv, op0=ALU.mult, op1=ALU.add), r=[pk, "udT", "lp"], w=["yv"])

        def glu_tail(t0, n):
            P.dve(lambda e: e.tensor_tensor(out=y2[:, :, 0:n], in0=yv[:, :, 0:n], in1=yv[:, :, 0:n], op=ALU.mult), r=["yv"], w=["y2"])
            P.dve(lambda e: e.tensor_scalar(out=y2[:, :, 0:n], in0=y2[:, :, 0:n], scalar1=0.044715 * 1.5957691216, scalar2=1.5957691216,
                                            op0=ALU.mult, op1=ALU.add), r=["y2"], w=["y2"])
            P.dve(lambda e: e.tensor_tensor(out=y2[:, :, 0:n], in0=y2[:, :, 0:n], in1=yv[:, :, 0:n], op=ALU.mult), r=["y2", "yv"], w=["y2"])
            P.act(lambda e: e.activation(out=sig[:, :, 0:n], in_=y2[:, :, 0:n], func=AF.Sigmoid), r=["y2"], w=["sig"])
            P.dve(lambda e: e.tensor_tensor(out=gdc[:, :, 0:n], in0=yv[:, :, 0:n], in1=sig[:, :, 0:n], op=ALU.mult),
                  r=["yv", "sig"], w=["gdc"])
            for ft in range(4):
                pt, pk = next_ps()
                for kt in range(4):
                    P.pe(lambda e, pt=pt, kt=kt, ft=ft: e.matmul(pt[:, 0:n], lhsT=gluw[:, kt, ft * 128:(ft + 1) * 128], rhs=gdc[:, kt, 0:n],
                                                                 start=(kt == 0), stop=(kt == 3)), r=["lp", "gdc"], w=[pk])
                P.act(lambda e, pt=pt, ft=ft: e.activation(out=sig[:, ft, 0:n], in_=pt[:, 0:n], func=AF.Sigmoid, bias=glub[:, ft:ft + 1]),
                      r=[pk, "lp"], w=["sig"])
            P.dve(lambda e: e.tensor_tensor(out=y2[:, :, 0:n], in0=sig[:, :, 0:n], in1=mT[:, 12:16, t0:t0 + n], op=ALU.mult), r=["sig", "mT3"], w=["y2"])
            P.dve(lambda e: e.tensor_tensor(out=mT[:, 12:16, t0:t0 + n], in0=y2[:, :, 0:n], in1=gdc[:, :, 0:n], op=ALU.mult),
                  r=["y2", "gdc", "mT3"], w=["mT3"])

        pend = [None]

        def gen():
            for ci_ in range(8):
                t0 = ci_ * 128
                for gq in range(4):
                    pr, prk = next_ps(); pi, pik = next_ps()
                    for j in range(4):
                        gp = gq * 4 + j
                        P.pe(lambda e, pr=pr, j=j, gp=gp, gq=gq: e.matmul(pr[:, j * 128:(j + 1) * 128], lhsT=BL[:, gp, 0, :], rhs=udT[:, gq, t0:t0 + 128],
                                                                          start=True, stop=True), r=["BL", "udT"], w=[prk])
                        P.pe(lambda e, pi=pi, j=j, gp=gp, gq=gq: e.matmul(pi[:, j * 128:(j + 1) * 128], lhsT=BL[:, gp, 1, :], rhs=udT[:, gq, t0:t0 + 128],
                                                                          start=True, stop=True), r=["BL", "udT"], w=[pik])
                    prv = pr[:, :].rearrange("p (a b) -> p a b", a=4); piv = pi[:, :].rearrange("p (a b) -> p a b", a=4)
                    rot(Er, Ei, prv, piv, gq, -1, [prk, pik], "E")
                    if pend[0] is not None:
                        pend[0]()
                        pend[0] = None
                    Wr = psb[6][:, :].rearrange("p (a b) -> p a b", a=4); Wi = psb[7][:, :].rearrange("p (a b) -> p a b", a=4)
                    for j in range(4):
                        gp = gq * 4 + j
                        P.dve(lambda e, j=j, gp=gp: e.tensor_tensor_scan(out=Wr[:, j, :], data0=bc(MAG[:, gp:gp + 1], [128, 128]), data1=Er[:, j, :],
                                                                         initial=Xc[:, 0, gp:gp + 1], op0=ALU.mult, op1=ALU.add),
                              r=["MAG", "Er", "Xc"], w=["ps6"])
                        P.dve(lambda e, j=j, gp=gp: e.tensor_tensor_scan(out=Wi[:, j, :], data0=bc(MAG[:, gp:gp + 1], [128, 128]), data1=Ei[:, j, :],
                                                                         initial=Xc[:, 1, gp:gp + 1], op0=ALU.mult, op1=ALU.add),
                              r=["MAG", "Ei", "Xc"], w=["ps7"])
                    if False:
                        dbg_dump("Er", Er, ["Er"]); dbg_dump("Ei", Ei, ["Ei"]); dbg_dump("Wr", Wr, ["Wr"]); dbg_dump("Wi", Wi, ["Wi"])
                    rot(Er, Ei, Wr, Wi, gq, +1, ["ps6", "ps7"], "E")
                    if False:
                        dbg_dump("Xr", Er, ["Er"]); dbg_dump("Xi", Ei, ["Ei"])
                    P.act(lambda e: e.activation(out=Xr, in_=Er, func=AF.Copy), r=["Er"], w=["Xr"])
                    P.act(lambda e: e.activation(out=Xi, in_=Ei, func=AF.Copy), r=["Ei"], w=["Xi"])
                    P.dve(lambda e, gq=gq: e.tensor_copy(out=xe[:, 0, gq * 4:(gq + 1) * 4], in_=Er[:, :, 127]), r=["Er"], w=["xe"])
                    P.dve(lambda e, gq=gq: e.tensor_copy(out=xe[:, 1, gq * 4:(gq + 1) * 4], in_=Ei[:, :, 127]), r=["Ei"], w=["xe"])
                    pt, pk = psb[5], "ps5"
                    for j in range(4):
                        gp = gq * 4 + j
                        P.pe(lambda e, pt=pt, gp=gp, j=j: e.matmul(pt[:, 0:128], lhsT=CL[:, gp, 0, :], rhs=Xr[:, j, :], start=(j == 0), stop=False),
                             r=["CL", "Xr"], w=[pk])
                        P.pe(lambda e, pt=pt, gp=gp, j=j: e.matmul(pt[:, 0:128], lhsT=CL[:, gp, 1, :], rhs=Xi[:, j, :], start=False, stop=(j == 3)),
                             r=["CL", "Xi"], w=[pk])
                    def _pend(gq=gq, t0=t0, last=(gq == 3)):
                        y_evac(gq, psb[5][:, 0:128], "ps5", t0, 128)
                        if last:
                            glu_tail(t0, 128)
                    pend[0] = _pend
                    if gq < 3:
                        yield
                P.dve(lambda e: e.tensor_copy(out=Xc[:], in_=xe), r=["xe", "ps6", "ps7"], w=["Xc"])
                yield
            if pend[0] is not None:
                pend[0]()
                pend[0] = None
            yield

        def finish():
            g = CV.get
            if h == 1:
                for c_, dst in ((0, o_sre_p), (1, o_sim_p)):
                    for hh in range(2):
                        P.dma("st", bass.AP(dst.tensor, l * 2048 + hh * 64, [[1, 64], [128, 16]]), Xc[hh * 64:hh * 64 + 64, c_, :], r=["Xc"],
                              allow_slow_non_contiguous=True)
                c0 = 1024
                x0 = g([NS, 2048], rows=NS); x0T = g([128, 2, 16, NS]); xn_ = g([128, 2, 16, NS]); xnb = g([128, 2, 16, NS], BF16)
                xnt = g([NS, 2048], rows=NS)
                for c_ in range(2):
                    P.dma("ld", x0, (sre, sim)[c_][l].rearrange("b g n -> b (g n)"), w=["x0"])
                    for q4 in range(4):
                        def ev(pv, pk, c_=c_, q4=q4):
                            P.dve(lambda e: e.tensor_copy(out=x0T[:, c_, q4 * 4:(q4 + 1) * 4, :], in_=pv), r=[pk], w=["x0T"])
                        transpose_to(None, x0[:, q4 * 512:(q4 + 1) * 512], NS, 4, F32, "x0", ev)
                arb = bc(AR[:].unsqueeze(2), [128, 16, NS]); aib = bc(AI[:].unsqueeze(2), [128, 16, NS])
                t_a = g([128, 16, NS]); t_b = g([128, 16, NS])
                for gq in range(4):
                    pr, prk = next_ps(); pi, pik = next_ps()
                    for j in range(4):
                        gp = gq * 4 + j
                        P.pe(lambda e, pr=pr, j=j, gp=gp, gq=gq: e.matmul(pr[:, j * NS:(j + 1) * NS], lhsT=BL[:, gp, 0, :], rhs=udT[:, gq, c0:c0 + NS],
                                                                          start=True, stop=True), r=["BL", "udT"], w=[prk])
                        P.pe(lambda e, pi=pi, j=j, gp=gp, gq=gq: e.matmul(pi[:, j * NS:(j + 1) * NS], lhsT=BL[:, gp, 1, :], rhs=udT[:, gq, c0:c0 + NS],
                                                                          start=True, stop=True), r=["BL", "udT"], w=[pik])
                    P.dve(lambda e, pr=pr, gq=gq: e.tensor_copy(out=xn_[:, 0, gq * 4:(gq + 1) * 4, :], in_=pr[:, 0:4 * NS].rearrange("p (a b) -> p a b", a=4)),
                          r=[prk], w=["xn_"])
                    P.dve(lambda e, pi=pi, gq=gq: e.tensor_copy(out=xn_[:, 1, gq * 4:(gq + 1) * 4, :], in_=pi[:, 0:4 * NS].rearrange("p (a b) -> p a b", a=4)),
                          r=[pik], w=["xn_"])
                P.dve(lambda e: e.tensor_tensor(out=t_a, in0=x0T[:, 0], in1=arb, op=ALU.mult), r=["x0T", "AR"], w=["t_a"])
                P.dve(lambda e: e.tensor_tensor(out=xn_[:, 0], in0=xn_[:, 0], in1=t_a, op=ALU.add), r=["xn_", "t_a"], w=["xn_"])
                P.dve(lambda e: e.tensor_tensor(out=t_b, in0=x0T[:, 1], in1=aib, op=ALU.mult), r=["x0T", "AI"], w=["t_b"])
                P.dve(lambda e: e.tensor_tensor(out=xn_[:, 0], in0=xn_[:, 0], in1=t_b, op=ALU.subtract), r=["xn_", "t_b"], w=["xn_"])
                P.dve(lambda e: e.tensor_tensor(out=t_a, in0=x0T[:, 1], in1=arb, op=ALU.mult), r=["x0T", "AR", "xn_"], w=["t_a"])
                P.dve(lambda e: e.tensor_tensor(out=xn_[:, 1], in0=xn_[:, 1], in1=t_a, op=ALU.add), r=["xn_", "t_a"], w=["xn_"])
                P.dve(lambda e: e.tensor_tensor(out=t_b, in0=x0T[:, 0], in1=aib, op=ALU.mult), r=["x0T", "AI", "xn_"], w=["t_b"])
                P.dve(lambda e: e.tensor_tensor(out=xn_[:, 1], in0=xn_[:, 1], in1=t_b, op=ALU.add), r=["xn_", "t_b"], w=["xn_"])
                P.act(lambda e: e.activation(out=xnb, in_=xn_, func=AF.Copy), r=["xn_"], w=["xnb"])
                for kt in range(4):
                    pt, pk = next_ps()
                    for j in range(4):
                        gp = kt * 4 + j
                        P.pe(lambda e, pt=pt, gp=gp, j=j: e.matmul(pt[:, 0:NS], lhsT=CL[:, gp, 0, :], rhs=xnb[:, 0, gp, :], start=(j == 0), stop=False),
                             r=["CL", "xnb"], w=[pk])
                        P.pe(lambda e, pt=pt, gp=gp, j=j: e.matmul(pt[:, 0:NS], lhsT=CL[:, gp, 1, :], rhs=xnb[:, 1, gp, :], start=False, stop=(j == 3)),
                             r=["CL", "xnb"], w=[pk])
                    y_evac(kt, pt[:, 0:NS], pk, c0, NS)
                glu_tail(c0, NS)
                for c_, dst in ((0, o_sre_s), (1, o_sim_s)):
                    for q4 in range(4):
                        pt, pk = next_ps(6, 8)
                        for j in range(4):
                            gp = q4 * 4 + j
                            P.pe(lambda e, pt=pt, j=j, gp=gp, c_=c_: e.transpose(pt[0:NS, j * 128:(j + 1) * 128], xn_[:, c_, gp, :], ident[:]),
                                 r=["xn_", "ident"], w=[pk])
                        P.act(lambda e, pt=pt, c_=c_, q4=q4: e.activation(out=xnt[:, q4 * 512:(q4 + 1) * 512], in_=pt[0:NS, 0:512], func=AF.Copy),
                              r=[pk], w=["xnt"])
                    P.dma("st", dst[l].rearrange("b g n -> b (g n)"), xnt, r=["xnt"])


        TICK[0] = gen()
        return finish

    def s5_finish(l, h, fin):
        fin()

    def final_norm():
        P.barrier()
        CV.reset()
        xt = [CV.get([128, D]) for _ in range(2)]
        yo = [CV.get([128, D]) for _ in range(2)]
        junk = CV.get([128, D], BF16)
        fg = CV.get([128, D]); ssq = CV.get([128, 20])
        P.dma("ld", fg, bass.AP(fng.tensor, 0, [[0, 128], [1, D]]), w=["fg"])
        alltiles = [(x2[i * 128:(i + 1) * 128, :], yp[i * 128:(i + 1) * 128, :], 128) for i in range(16)] + [(x2[2048:2048 + NS, :], ys, NS)]
        for ti, (src, dst, rows) in enumerate(alltiles):
            b = ti % 2
            c = ti % 20
            P.dma("xl%d" % b, xt[b][0:rows, :], src, r=["xscratch"], w=["xt%d" % b])
            P.act(lambda e, b=b, rows=rows, c=c: e.activation(out=junk[0:rows, :], in_=xt[b][0:rows, :], func=AF.Square,
                                                              accum_out=ssq[0:rows, c:c + 1]), r=["xt%d" % b], w=["junk", "fs%d" % c])
            P.act(lambda e, rows=rows, c=c: e.activation(out=ssq[0:rows, c:c + 1], in_=ssq[0:rows, c:c + 1], func=AF.Sqrt, scale=1.0 / D,
                                                         bias=epsb[0:rows, :]), r=["fs%d" % c, "epsb"], w=["fs%d" % c])
            P.dve(lambda e, rows=rows, c=c: e.reciprocal(out=ssq[0:rows, c:c + 1], in_=ssq[0:rows, c:c + 1]), r=["fs%d" % c], w=["fs%d" % c])
            P.dve(lambda e, b=b, rows=rows, c=c: e.scalar_tensor_tensor(out=yo[b][0:rows, :], in0=xt[b][0:rows, :], scalar=ssq[0:rows, c:c + 1],
                                                                        in1=fg[0:rows, :], op0=ALU.mult, op1=ALU.mult),
                  r=["xt%d" % b, "fs%d" % c, "fg"], w=["yo%d" % b])
            P.dma("yo%d" % b, dst, yo[b][0:rows, :], r=["yo%d" % b], phys="pool")

    def dbg_dump(name, ap, keys):
        if not DBGT:
            return
        shape = list(ap.shape)
        dt_ = ap.dtype
        d_ = nc.dram_tensor("dbg_" + name, shape, dt_, kind="ExternalOutput").ap()
        P.barrier()
        P.dma("st", d_, ap, r=keys)
        P.barrier()
        DBGN.append("dbg_" + name)

    def dump():
        dh = nc.dram_tensor("dbg_h", [128, 16 * 1040], BF16, kind="ExternalOutput").ap()
        dm = nc.dram_tensor("dbg_m", [128, 16 * 1040], BF16, kind="ExternalOutput").ap()
        P.barrier()
        P.dma("st", dh, hT[:].rearrange("p a b -> p (a b)"), r=["hT"])
        P.dma("st", dm, mT[:].rearrange("p a b -> p (a b)"), r=["mT", "mT3"])
        P.barrier()

    if STAGE >= 99:
        for l in range(2):
            layer_params(l)
            for h in range(2):
                layer_pass(l, h)
                if DUMP == (l, h):
                    dump()
        final_norm()
    else:
        layer_params(0)
        if STAGE >= 1:
            layer_pass(0, 0)
        if STAGE >= 7:
            layer_pass(0, 1)
        if STAGE >= 8:
            final_norm()
    P.emit()
    st.close()
    return nc


_CACHE = {}


def kernel(**inp):
    if "nc" not in _CACHE:
        _CACHE["nc"] = build_program()
    nc = _CACHE["nc"]
    f = lambda a: np.ascontiguousarray(np.asarray(a, dtype=np.float32))
    ident = np.eye(128, dtype=np.float32)
    triu = np.triu(np.ones((128, 128), np.float32))
    eye16 = np.eye(16, dtype=np.float32)
    hmask = np.zeros((128, 2), np.float32); hmask[:64, 0] = 1; hmask[64:, 1] = 1
    shared = dict(
        norm_g=f(inp["norm_g"]), w_in=f(inp["w_in"]), w_a2=f(inp["w_a2"]), b_a=f(inp["b_a"]), gla_g=f(inp["gla_g"]),
        sgu_g=f(inp["sgu_g"]), sgu_w=f(inp["sgu_w"]), sgu_b=f(inp["sgu_b"]), conv_w=f(inp["conv_w"]),
        lam_re=f(inp["ssm_lambda_re"]), lam_im=f(inp["ssm_lambda_im"]), log_dt=f(inp["ssm_log_dt"]),
        b_re=f(inp["ssm_b_re"]), b_im=f(inp["ssm_b_im"]), c_re=f(inp["ssm_c_re"]), c_im=f(inp["ssm_c_im"]),
        ssm_d=f(inp["ssm_d"]), glu_w=f(inp["glu_w"]), glu_b=f(inp["glu_b"]), w_out=f(inp["w_out"]), fng=f(inp["final_norm_g"]),
        ident=ident, triu=triu, eye16=eye16, hmask=hmask)
    xpr = f(inp["x_prompt"]); xsm = f(inp["x_sample"])[:, 0, :]
    sg = f(inp["state_gla"]); sc = f(inp["state_conv"]); sr = f(inp["state_ssm_re"]); si = f(inp["state_ssm_im"])
    in_maps = []
    for c in range(8):
        m = dict(shared)
        sl = slice(c * NS, (c + 1) * NS)
        m.update(xp=xpr[c % 4], xs=np.ascontiguousarray(xsm[sl]), sgla=np.ascontiguousarray(sg[:, sl]),
                 sconv=np.ascontiguousarray(sc[:, sl]), sre=np.ascontiguousarray(sr[:, sl]), sim=np.ascontiguousarray(si[:, sl]))
        in_maps.append(m)
    res = run_bass_kernel_spmd(nc, in_maps, core_ids=list(range(8))).results
    if DUMP is not None:
        _CACHE["dbg"] = (res[0]["dbg_h"], res[0]["dbg_m"])
    for n_ in DBGN:
        _CACHE[n_] = np.asarray(res[0][n_])
    cat = lambda k, ax: np.concatenate([res[c][k] for c in range(8)], axis=ax)
    stk = lambda k: np.stack([res[c][k] for c in range(4)], axis=1)
    y_prompt = np.stack([res[c]["yp"] for c in range(4)], axis=0)
    y_sample = cat("ys", 0)[:, None, :]
    return (y_prompt.astype(np.float32), y_sample.astype(np.float32),
            stk("gla_p"), cat("gla_s", 1), stk("conv_p"), cat("conv_s", 1),
            stk("sre_p"), stk("sim_p"), cat("sre_s", 1), cat("sim_s", 1), cat("vn_s", 1)[:, :, None, :])
```

```python
import types
import numpy as np
import concourse.bass as bass
import concourse.mybir as mybir
from concourse.bass_utils import run_bass_kernel_spmd

F32 = mybir.dt.float32
BF16 = mybir.dt.bfloat16
AF = mybir.ActivationFunctionType
ALU = mybir.AluOpType

D = 2048
PT = 6160
NS = 16
W = 512
EPS = 1e-6
DEBUG_EMIT = False
SAME_SKIP = ("pe",)
STAGE = 99
SUB = 99
NOSAMP = False
DUMP = None
GLABAR = 0
PSSHIFT = 0
DBGT = False
DBGN = []
ERRS = {}
SEG = dict(q=0, k=256, v=512, a=1024, ga=1040, ub=1552, vb=2064, gb=2576,
           cb=3088, cc=3600, hc=4112, gc=4624, ud=5136, gd=5648)


class Prog:
    def __init__(self, nc):
        self.nc = nc
        self.ops = {e: [] for e in ("sync", "pool", "act", "dve", "pe")}
        self.cnt = {}
        self.lastw = {}
        self.readers = {}
        self.vq_phys = {}
        self.floor = {}

    @staticmethod
    def _freeze(fn):
        if fn.__closure__ is None:
            return fn
        cells = []
        for c in fn.__closure__:
            try:
                cells.append(types.CellType(c.cell_contents))
            except ValueError:
                cells.append(c)
        return types.FunctionType(fn.__code__, fn.__globals__, fn.__name__, fn.__defaults__, tuple(cells))

    def _rec(self, phys, q, fn, r, w, inc):
        fn = self._freeze(fn)
        waits = dict(self.floor.get(phys, {}))

        def need(tok):
            s, v = tok
            if s == q and phys in SAME_SKIP:
                return
            if v > waits.get(s, 0):
                waits[s] = v
        for k in r:
            if k in self.lastw:
                need(self.lastw[k])
        for k in w:
            if k in self.lastw:
                need(self.lastw[k])
            for t in self.readers.get(k, ()):
                need(t)
        idx = self.cnt.get(q, 0) + 1
        self.cnt[q] = idx
        if inc == 16 and idx > 1:
            if 16 * (idx - 1) > waits.get(q, 0):
                waits[q] = 16 * (idx - 1)
        self.vq_phys[q] = phys
        self.ops[phys].append((fn, waits, q, inc))
        tok = (q, idx * inc)
        for k in r:
            self.readers.setdefault(k, []).append(tok)
        for k in w:
            self.lastw[k] = tok
            self.readers[k] = []

    def dma(self, q, out, in_, r=(), w=(), phys="sync", **kw):
        if q in ("ld", "st"):
            self.rr = getattr(self, "rr", 0) + 1
            q = "%s%d" % (q, self.rr % 6)
        self._rec(phys, q, lambda e: e.dma_start(out=out, in_=in_, **kw), r, w, 16)

    def act(self, fn, r=(), w=()):
        self._rec("act", "act", fn, r, w, 1)

    def dve(self, fn, r=(), w=()):
        self._rec("dve", "dve", fn, r, w, 1)

    def pe(self, fn, r=(), w=()):
        self._rec("pe", "pe", fn, r, w, 1)

    def barrier(self):
        snap = {q: c * (16 if q not in ("act", "dve", "pe") else 1) for q, c in self.cnt.items()}
        for p in self.ops:
            if p == "pool":
                continue
            self.floor[p] = dict(snap)

    def emit(self):
        nc = self.nc
        qs = list(self.cnt.keys())
        sems = {}
        import contextlib
        with contextlib.ExitStack() as st:
            for q in qs:
                sems[q] = st.enter_context(nc.semaphore("s_" + q))
            block = st.enter_context(nc.Block())
            final = {q: c * (16 if q not in ("act", "dve", "pe") else 1) for q, c in self.cnt.items()}

            def run(phys, eng, last=False):
                have = {}
                for fn, waits, q, inc in self.ops[phys]:
                    for s, v in waits.items():
                        if have.get(s, 0) < v:
                            eng.wait_ge(sems[s], v)
                            have[s] = v
                    if DEBUG_EMIT:
                        try:
                            fn(eng).then_inc(sems[q], inc)
                        except Exception as ex:
                            msg = str(ex)[:300]
                            if msg not in ERRS:
                                ERRS[msg] = (phys, q)
                                print("EMIT-ERR", phys, q, msg, flush=True)
                        continue
                    fn(eng).then_inc(sems[q], inc)
                if last:
                    for q, v in final.items():
                        eng.wait_ge(sems[q], v)

            @block.sync
            def _(e):
                run("sync", e, last=True)

            @block.gpsimd
            def _(e):
                run("pool", e)

            @block.scalar
            def _(e):
                run("act", e)

            @block.vector
            def _(e):
                run("dve", e)

            @block.tensor
            def _(e):
                run("pe", e)


def build_program():
    nc = bass.Bass("TRN2", target_bir_lowering=False)
    P = Prog(nc)

    def din(name, shape):
        return nc.dram_tensor(name, list(shape), F32, kind="ExternalInput").ap()

    def dout(name, shape):
        return nc.dram_tensor(name, list(shape), F32, kind="ExternalOutput").ap()

    xp = din("xp", [2048, D]); xs = din("xs", [NS, D])
    sgla = din("sgla", [2, NS, 4, 64, 128]); sconv = din("sconv", [2, NS, 2, W])
    sre = din("sre", [2, NS, 32, 64]); sim = din("sim", [2, NS, 32, 64])
    norm_g = din("norm_g", [2, D]); w_in = din("w_in", [2, D, PT]); w_a2 = din("w_a2", [2, 16, 256])
    b_a = din("b_a", [2, 256]); gla_g = din("gla_g", [2, W]); sgu_g = din("sgu_g", [2, W])
    sgu_w = din("sgu_w", [2, 4, 128, 128]); sgu_b = din("sgu_b", [2, 4, 128]); conv_w = din("conv_w", [2, 3, W])
    lam_re = din("lam_re", [2, 32, 64]); lam_im = din("lam_im", [2, 32, 64]); log_dt = din("log_dt", [2, 32])
    b_re = din("b_re", [2, 32, 64, 16]); b_im = din("b_im", [2, 32, 64, 16])
    c_re = din("c_re", [2, 32, 16, 64]); c_im = din("c_im", [2, 32, 16, 64])
    ssm_d = din("ssm_d", [2, W]); glu_w = din("glu_w", [2, W, W]); glu_b = din("glu_b", [2, W])
    w_out = din("w_out", [2, D, D]); fng = din("fng", [D])
    ident_d = din("ident", [128, 128]); triu_d = din("triu", [128, 128]); eye16_d = din("eye16", [16, 16])
    hmask_d = din("hmask", [128, 2])

    yp = dout("yp", [2048, D]); ys = dout("ys", [NS, D])
    o_gla_p = dout("gla_p", [2, 4, 64, 128]); o_gla_s = dout("gla_s", [2, NS, 4, 64, 128])
    o_conv_p = dout("conv_p", [2, 2, W]); o_conv_s = dout("conv_s", [2, NS, 2, W])
    o_sre_p = dout("sre_p", [2, 32, 64]); o_sim_p = dout("sim_p", [2, 32, 64])
    o_sre_s = dout("sre_s", [2, NS, 32, 64]); o_sim_s = dout("sim_s", [2, NS, 32, 64])
    o_vn_s = dout("vn_s", [2, NS, W])
    x1 = nc.dram_tensor("x1s", [2048 + NS, D], F32, kind="Internal").ap()
    x2 = nc.dram_tensor("x2s", [2048 + NS, D], F32, kind="Internal").ap()

    import contextlib
    st = contextlib.ExitStack()

    def sb(name, shape, dt=F32):
        return st.enter_context(nc.sbuf_tensor("sb_" + name, list(shape), dt))

    def ps(name, shape, dt=F32):
        return st.enter_context(nc.psum_tensor(name, list(shape), dt))

    hT = sb("hT", [128, 16, 1040], BF16)
    mT = sb("mT", [128, 16, 1040], BF16)
    wbuf = sb("wbuf", [128, 2, 16, 256], BF16)
    ident = sb("ident", [128, 128]); identb = sb("identb", [128, 128], BF16)
    triu = sb("triu", [128, 128]); eye16 = sb("eye16", [16, 16]); hmask = sb("hmask", [128, 2])
    ones1 = sb("ones1", [1, 128], BF16)
    epsb = sb("epsb", [128, 1])
    ng = sb("ng", [128, 2, 16])
    wa2 = sb("wa2", [16, 256], BF16); wa2f = sb("wa2f", [16, 256])
    bafm = sb("bafm", [128, 2]); barow = sb("barow", [1, 256], BF16); barowf = sb("barowf", [1, 256])
    glag = sb("glag", [128, 4]); sgug = sb("sgug", [128, W]); sgub = sb("sgub", [128, 4, 128])
    sguw = sb("sguw", [128, 4, 128], BF16); w00 = sb("w00", [128, 4]); b00 = sb("b00", [128, 4])
    cw = sb("cw", [128, 3, 4]); dsk = sb("dsk", [128, 4]); glub = sb("glub", [128, 4])
    gluw = sb("gluw", [128, 4, W], BF16)
    S32 = sb("S32", [128, 2, 128]); Sbf = sb("Sbf", [128, 2, 128], BF16)
    zcar = sb("zcar", [128, 4, 2])
    nbafm = sb("nbafm", [128, 2])
    Xc = sb("Xc", [128, 2, 16])
    BL = sb("BL", [128, 16, 2, 128], BF16)
    CL = sb("CL", [128, 16, 2, 128], BF16)
    TC = sb("TC", [128, 16, 128]); TS = sb("TS", [128, 16, 128]); MAG = sb("MAG", [128, 16])
    AR = sb("AR", [128, 16]); AI = sb("AI", [128, 16])
    ARENA = 78 * 1024
    arena = sb("arena", [128, ARENA // 4])
    psb = [ps("psb%d" % i, [128, 512]) for i in range(8)]

    class Carver:
        def __init__(self):
            self.off = 0
            self.limit = ARENA

        def reset(self, base=0, limit=None):
            self.off = base
            self.limit = ARENA if limit is None else limit

        def get(self, shape, dt=F32, rows=128):
            n = int(np.prod(shape[1:]))
            nb = n * (4 if dt == F32 else 2)
            nb = (nb + 63) // 64 * 64
            assert self.off + nb <= self.limit, ("arena overflow", self.off, nb, self.limit)
            v = arena[0:shape[0], self.off // 4:(self.off + nb) // 4]
            if dt != F32:
                v = v.bitcast(dt)
            v = v[:, 0:n]
            self.off += nb
            if len(shape) == 3:
                v = v.rearrange("p (a b) -> p a b", a=shape[1])
            elif len(shape) == 4:
                v = v.rearrange("p (a b c) -> p a b c", a=shape[1], b=shape[2])
            return v

    MAINLIM = 52 * 1024
    CVS = Carver()
    TICK = [None]

    def tick():
        gen = TICK[0]
        if gen is not None:
            try:
                next(gen)
            except StopIteration:
                TICK[0] = None

    CV = Carver()
    uid = [0]

    def K(s="t"):
        uid[0] += 1
        return "%s%d" % (s, uid[0])

    def bc(ap, shape):
        return ap.to_broadcast(list(shape))

    P.dma("ld", ident[:], ident_d, w=["ident"])
    P.dma("ld", triu[:], triu_d, w=["triu"])
    P.dma("ld", eye16[:], eye16_d, w=["eye16"])
    P.dma("ld", hmask[:], hmask_d, w=["hmask"])
    P.dma("ld", ng[:], norm_g.rearrange("l (k p) -> p l k", p=128), w=["ng"], allow_slow_non_contiguous=True)
    P.dve(lambda e: e.tensor_copy(out=identb[:], in_=ident[:]), r=["ident"], w=["identb"])
    P.dve(lambda e: e.memset(ones1[:], 1.0), w=["ones1"])
    P.dve(lambda e: e.memset(epsb[:], EPS), w=["epsb"])

    wstate = {"n": 0}

    def load_w(src_cols_ap):
        i = wstate["n"] % 2
        wstate["n"] += 1
        ncols = src_cols_ap.shape[1]
        key = "wbuf%d" % i
        P.dma("wq%d" % i, wbuf[:, i, :, 0:ncols], src_cols_ap.rearrange("(k p) c -> p k c", p=128),
              w=[key], phys="pool")
        return wbuf[:, i, :, 0:ncols], key

    def token_blocks(NT):
        out = []
        t = 0
        while t < NT:
            n = min(512, NT - t)
            out.append((t, n))
            t += n
        return out

    psrr = {"i": 0}

    def next_ps(lo=0, hi=6):
        i = lo + psrr["i"] % (hi - lo)
        psrr["i"] += 1
        return psb[i], "ps%d" % i

    def proj_fm(l, c0, ncols, NT, evac, rows=128):
        for b0 in range(0, ncols, 256):
            nb = min(256, ncols - b0)
            wv, wk = load_w(w_in[l, :, c0 + b0:c0 + b0 + nb])
            for f0 in range(0, nb, 128):
                m = min(128, nb - f0)
                for (t0, n) in token_blocks(NT):
                    pt, pk = next_ps()
                    for k in range(16):
                        P.pe(lambda e, k=k, pt=pt, wv=wv, f0=f0, m=m, t0=t0, n=n: e.matmul(
                            pt[0:m, 0:n], lhsT=wv[:, k, f0:f0 + m], rhs=hT[:, k, t0:t0 + n],
                            start=(k == 0), stop=(k == 15)), r=[wk, "hT"], w=[pk])
                    evac((b0 + f0) // 128, t0, n, pt[0:m, 0:n], pk)
                    tick()

    def proj_tm(l, c0, ncols, tiles, evac):
        blocks = []
        for b0 in range(0, ncols, 256):
            nb = min(256, ncols - b0)
            blocks.append((b0, nb) + load_w(w_in[l, :, c0 + b0:c0 + b0 + nb]))
        assert len(blocks) <= 2
        for ti, (t0, rows) in enumerate(tiles):
            pt, pk = next_ps()
            for (b0, nb, wv, wk) in blocks:
                for k in range(16):
                    P.pe(lambda e, k=k, pt=pt, wv=wv, b0=b0, nb=nb, t0=t0, rows=rows: e.matmul(
                        pt[0:rows, b0:b0 + nb], lhsT=hT[:, k, t0:t0 + rows], rhs=wv[:, k, :],
                        start=(k == 0), stop=(k == 15)), r=[wk, "hT"], w=[pk])
            evac(ti, rows, pt[0:rows, 0:ncols], pk)
            tick()

    def transpose_to(dst_fn, src_ap, rows, ncol_tiles, dt, skey, evac):
        pt, pk = next_ps(6, 8)
        idn = identb if dt == BF16 else ident
        pv = pt[:].bitcast(BF16)[:, 0:ncol_tiles * 128] if dt == BF16 else pt[:, 0:ncol_tiles * 128]
        pv = pv.rearrange("p (a b) -> p a b", a=ncol_tiles)
        for j in range(ncol_tiles):
            P.pe(lambda e, j=j, pv=pv: e.transpose(pv[:, j, 0:rows], src_ap[0:rows, j * 128:(j + 1) * 128],
                                                   idn[0:rows, 0:rows]),
                 r=[skey, "ident", "identb"], w=[pk])
        evac(pv[:, :, 0:rows], pk)

    def layer_params(l):
        P.barrier()
        k = "lp"
        P.dma("ld", wa2f[:], w_a2[l], w=["wa2f"])
        P.dve(lambda e: e.tensor_copy(out=wa2[:], in_=wa2f[:]), r=["wa2f"], w=[k])
        P.dma("ld", barowf[:], b_a[l:l + 1, :], w=["barowf"])
        P.dve(lambda e: e.tensor_copy(out=barow[:], in_=barowf[:]), r=["barowf"], w=[k])
        P.dma("ld", bafm[:], b_a[l].rearrange("(a p) -> p a", p=128), w=[k], allow_slow_non_contiguous=True)
        P.dma("ld", glag[:], gla_g[l].rearrange("(a p) -> p a", p=128), w=[k], allow_slow_non_contiguous=True)
        P.dma("ld", sgug[:], bass.AP(sgu_g.tensor, l * W, [[0, 128], [1, W]]), w=[k])
        P.dma("ld", sgub[:], bass.AP(sgu_b.tensor, l * 512, [[0, 128], [128, 4], [1, 128]]), w=[k])
        P.dma("ld", w00[:], bass.AP(sgu_w.tensor, l * 4 * 16384, [[0, 128], [16384, 4]]), w=[k],
              allow_slow_non_contiguous=True)
        P.dma("ld", b00[:], bass.AP(sgu_b.tensor, l * 512, [[0, 128], [128, 4]]), w=[k],
              allow_slow_non_contiguous=True)
        P.dma("ld", cw[:], conv_w[l].rearrange("j (a p) -> p j a", p=128), w=[k], allow_slow_non_contiguous=True)
        P.dma("ld", dsk[:], ssm_d[l].rearrange("(a p) -> p a", p=128), w=[k], allow_slow_non_contiguous=True)
        P.dma("ld", glub[:], glu_b[l].rearrange("(a p) -> p a", p=128), w=[k], allow_slow_non_contiguous=True)
        P.dma("wq0", gluw[:], glu_w[l].rearrange("(a p) c -> p a c", p=128), w=[k], phys="pool")
        CV.reset()
        swn = CV.get([128, 4, 128])
        P.dma("ld", swn, sgu_w[l].rearrange("h t s -> t h s"), w=["swn"])
        for h in range(4):
            pt, pk = next_ps(6, 8)
            P.pe(lambda e, h=h, pt=pt: e.transpose(pt[:, 0:128], swn[:, h, :], ident[:]), r=["swn", "ident"], w=[pk])
            P.dve(lambda e, h=h, pt=pt: e.tensor_tensor(out=sguw[:, h, :], in0=pt[:, 0:128], in1=triu[:], op=ALU.mult),
                  r=[pk, "triu"], w=[k])
        P.dve(lambda e: e.memset(S32[:], 0.0), w=["S32"])
        P.dve(lambda e: e.memset(Sbf[:], 0.0), w=["Sbf"])
        P.dve(lambda e: e.memset(zcar[:], 0.0), w=["zcar"])
        P.dve(lambda e: e.tensor_scalar(out=nbafm[:], in0=bafm[:], scalar1=-1.0, scalar2=None, op0=ALU.mult), r=[k], w=[k])
        P.dve(lambda e: e.memset(Xc[:], 0.0), w=["Xc"])
        s5_params(l)
        P.barrier()

    def s5_params(l):
        kk = "s5p"
        g = CV.get
        LR = g([128, 16]); LI = g([128, 16]); DT = g([128, 16])
        for hh in range(2):
            rows = slice(hh * 64, hh * 64 + 64)
            P.dma("ld", LR[rows, :], bass.AP(lam_re.tensor, l * 2048 + hh * 64, [[1, 64], [128, 16]]), w=["LR"],
                  allow_slow_non_contiguous=True)
            P.dma("ld", LI[rows, :], bass.AP(lam_im.tensor, l * 2048 + hh * 64, [[1, 64], [128, 16]]), w=["LI"],
                  allow_slow_non_contiguous=True)
            P.dma("ld", DT[rows, :], bass.AP(log_dt.tensor, l * 32 + hh, [[0, 64], [2, 16]]), w=["DT"],
                  allow_slow_non_contiguous=True)
        P.act(lambda e: e.activation(out=DT, in_=DT, func=AF.Exp), r=["DT"], w=["DT"])
        lrd = g([128, 16]); th = g([128, 16]); mag = g([128, 16]); c = g([128, 16]); s = g([128, 16])
        t1 = g([128, 16]); t2 = g([128, 16]); hp = g([128, 1])
        P.dve(lambda e: e.memset(hp, float(np.pi / 2)), w=["hp"])
        P.dve(lambda e: e.tensor_tensor(out=lrd, in0=LR, in1=DT, op=ALU.mult), r=["LR", "DT"], w=["lrd"])
        P.dve(lambda e: e.tensor_tensor(out=th, in0=LI, in1=DT, op=ALU.mult), r=["LI", "DT"], w=["th"])
        P.act(lambda e: e.activation(out=mag, in_=lrd, func=AF.Exp), r=["lrd"], w=["mag"])
        P.act(lambda e: e.activation(out=s, in_=th, func=AF.Sin, scale=1.0 / 16), r=["th"], w=["cs"])
        P.act(lambda e: e.activation(out=c, in_=th, func=AF.Sin, scale=1.0 / 16, bias=hp), r=["th", "hp", "cs"], w=["cs"])

        def csq(cr, ci, key):
            P.dve(lambda e: e.tensor_tensor(out=t1, in0=cr, in1=ci, op=ALU.mult), r=[key], w=["t1"])
            P.dve(lambda e: e.tensor_tensor(out=t2, in0=ci, in1=ci, op=ALU.mult), r=[key], w=["t2"])
            P.dve(lambda e: e.tensor_tensor(out=cr, in0=cr, in1=cr, op=ALU.mult), r=[key], w=[key])
            P.dve(lambda e: e.tensor_tensor(out=cr, in0=cr, in1=t2, op=ALU.subtract), r=[key, "t2"], w=[key])
            P.dve(lambda e: e.tensor_scalar(out=ci, in0=t1, scalar1=2.0, scalar2=None, op0=ALU.mult), r=["t1"], w=[key])
        for _ in range(4):
            csq(c, s, "cs")
        P.dve(lambda e: e.tensor_tensor(out=AR[:], in0=mag, in1=c, op=ALU.mult), r=["mag", "cs"], w=["AR"])
        P.dve(lambda e: e.tensor_tensor(out=AI[:], in0=mag, in1=s, op=ALU.mult), r=["mag", "cs"], w=["AI"])
        P.dve(lambda e: e.tensor_copy(out=TC[:, :, 0], in_=c), r=["cs"], w=["TC"])
        P.dve(lambda e: e.tensor_copy(out=TS[:, :, 0], in_=s), r=["cs"], w=["TS"])
        n = 1
        tA = g([128, 16, 64]); tB = g([128, 16, 64])
        while n < 128:
            cn = bc(TC[:, :, n - 1:n], [128, 16, n]); sn = bc(TS[:, :, n - 1:n], [128, 16, n])
            a = tA[:, :, 0:n]; b = tB[:, :, 0:n]
            P.dve(lambda e, a=a, cn=cn, n=n: e.tensor_tensor(out=a, in0=TC[:, :, 0:n], in1=cn, op=ALU.mult), r=["TC"], w=["tA"])
            P.dve(lambda e, b=b, sn=sn, n=n: e.tensor_tensor(out=b, in0=TS[:, :, 0:n], in1=sn, op=ALU.mult), r=["TS"], w=["tB"])
            P.dve(lambda e, a=a, b=b, n=n: e.tensor_tensor(out=TC[:, :, n:2 * n], in0=a, in1=b, op=ALU.subtract), r=["tA", "tB"], w=["TC"])
            P.dve(lambda e, a=a, sn=sn, n=n: e.tensor_tensor(out=a, in0=TC[:, :, 0:n], in1=sn, op=ALU.mult), r=["TC", "TS"], w=["tA"])
            P.dve(lambda e, b=b, cn=cn, n=n: e.tensor_tensor(out=b, in0=TS[:, :, 0:n], in1=cn, op=ALU.mult), r=["TS", "TC"], w=["tB"])
            P.dve(lambda e, a=a, b=b, n=n: e.tensor_tensor(out=TS[:, :, n:2 * n], in0=a, in1=b, op=ALU.add), r=["tA", "tB"], w=["TS"])
            n *= 2
        P.dve(lambda e: e.tensor_copy(out=MAG[:], in_=mag), r=["mag"], w=["MAG"])
        den = g([128, 16]); cr_ = g([128, 16]); ci_ = g([128, 16]); am1 = g([128, 16])
        P.dve(lambda e: e.tensor_tensor(out=t1, in0=LR, in1=LR, op=ALU.mult), r=["LR"], w=["t1"])
        P.dve(lambda e: e.tensor_tensor(out=t2, in0=LI, in1=LI, op=ALU.mult), r=["LI"], w=["t2"])
        P.dve(lambda e: e.tensor_tensor(out=den, in0=t1, in1=t2, op=ALU.add), r=["t1", "t2"], w=["den"])
        P.dve(lambda e: e.reciprocal(out=den, in_=den), r=["den"], w=["den"])
        P.dve(lambda e: e.tensor_scalar(out=am1, in0=AR[:], scalar1=-1.0, scalar2=None, op0=ALU.add), r=["AR"], w=["am1"])
        P.dve(lambda e: e.tensor_tensor(out=t1, in0=am1, in1=LR, op=ALU.mult), r=["am1", "LR"], w=["t1"])
        P.dve(lambda e: e.tensor_tensor(out=t2, in0=AI[:], in1=LI, op=ALU.mult), r=["AI", "LI"], w=["t2"])
        P.dve(lambda e: e.tensor_tensor(out=cr_, in0=t1, in1=t2, op=ALU.add), r=["t1", "t2"], w=["cr"])
        P.dve(lambda e: e.tensor_tensor(out=cr_, in0=cr_, in1=den, op=ALU.mult), r=["cr", "den"], w=["cr"])
        P.dve(lambda e: e.tensor_tensor(out=t1, in0=AI[:], in1=LR, op=ALU.mult), r=["AI", "LR"], w=["t1"])
        P.dve(lambda e: e.tensor_tensor(out=t2, in0=am1, in1=LI, op=ALU.mult), r=["am1", "LI"], w=["t2"])
        P.dve(lambda e: e.tensor_tensor(out=ci_, in0=t1, in1=t2, op=ALU.subtract), r=["t1", "t2"], w=["ci"])
        P.dve(lambda e: e.tensor_tensor(out=ci_, in0=ci_, in1=den, op=ALU.mult), r=["ci", "den"], w=["ci"])
        BMr = g([128, 16, 32]); BMi = g([128, 16, 32]); BBr = g([128, 16, 32]); BBi = g([128, 16, 32])
        u1 = g([128, 16, 32]); u2 = g([128, 16, 32])
        P.dve(lambda e: e.memset(BMr, 0.0), w=["BMr"])
        P.dve(lambda e: e.memset(BMi, 0.0), w=["BMi"])
        for hh in range(2):
            rows = slice(hh * 64, hh * 64 + 64)
            for (dst, src, key) in ((BMr, b_re, "BMr"), (BMi, b_im, "BMi")):
                P.dma("ld", dst[rows, :, hh * 16:hh * 16 + 16],
                      bass.AP(src.tensor, l * 32768 + hh * 1024, [[16, 64], [2048, 16], [1, 16]]), w=[key])
        crb = bc(cr_.unsqueeze(2), [128, 16, 32]); cib = bc(ci_.unsqueeze(2), [128, 16, 32])
        P.dve(lambda e: e.tensor_tensor(out=u1, in0=BMr, in1=crb, op=ALU.mult), r=["BMr", "cr"], w=["u1"])
        P.dve(lambda e: e.tensor_tensor(out=u2, in0=BMi, in1=cib, op=ALU.mult), r=["BMi", "ci"], w=["u2"])
        P.dve(lambda e: e.tensor_tensor(out=BBr, in0=u1, in1=u2, op=ALU.subtract), r=["u1", "u2"], w=["BBr"])
        P.dve(lambda e: e.tensor_tensor(out=u1, in0=BMi, in1=crb, op=ALU.mult), r=["BMi", "cr", "BBr"], w=["u1"])
        P.dve(lambda e: e.tensor_tensor(out=u2, in0=BMr, in1=cib, op=ALU.mult), r=["BMr", "ci", "BBr"], w=["u2"])
        P.dve(lambda e: e.tensor_tensor(out=BBi, in0=u1, in1=u2, op=ALU.add), r=["u1", "u2"], w=["BBi"])
        P.dve(lambda e: e.memset(BL[:], 0.0), w=["BL"])
        for kt in range(4):
            for ci, src in enumerate((BBr, BBi)):
                pt, pk = next_ps(6, 8)
                P.pe(lambda e, pt=pt, src=src, kt=kt: e.transpose(pt[:, 0:128], src[:, 4 * kt:4 * kt + 4, :].rearrange("p a b -> p (a b)"), ident[:]),
                     r=["BBr", "BBi", "ident"], w=[pk])
                for j in range(4):
                    P.dve(lambda e, pt=pt, kt=kt, ci=ci, j=j: e.tensor_copy(
                        out=BL[32 * j:32 * j + 32, 4 * kt + j, ci, :], in_=pt[32 * j:32 * j + 32, 0:128]), r=[pk], w=["BL"])
        CMr = g([128, 16, 32]); CMi = g([128, 16, 32]); cn_ = g([128, 128])
        P.dve(lambda e: e.memset(CMr, 0.0), w=["CMr"])
        P.dve(lambda e: e.memset(CMi, 0.0), w=["CMi"])
        for (dst, src, key) in ((CMr, c_re, "CMr"), (CMi, c_im, "CMi")):
            for kt in range(4):
                for dd in range(2):
                    P.dma("ld", cn_[:, dd * 64:(dd + 1) * 64], src[l].rearrange("g p n -> (g p) n")[kt * 128:(kt + 1) * 128, :], w=["cn"])
                pt, pk = next_ps(6, 8)
                P.pe(lambda e, pt=pt: e.transpose(pt[:, 0:128], cn_, ident[:]), r=["cn", "ident"], w=[pk])
                pv = pt[:, 0:128].rearrange("q (j a p) -> q j a p", j=4, a=2)
                for hh in range(2):
                    rows = slice(hh * 64, hh * 64 + 64)
                    P.dve(lambda e, dst=dst, pv=pv, rows=rows, hh=hh, kt=kt: e.tensor_copy(
                        out=dst[rows, 4 * kt:4 * kt + 4, hh * 16:hh * 16 + 16], in_=pv[rows, :, hh, :]), r=[pk], w=[key])
        P.dve(lambda e: e.memset(CL[:], 0.0), w=["CL"])
        for gp in range(16):
            j = gp % 4
            P.dve(lambda e, gp=gp, j=j: e.tensor_copy(out=CL[:, gp, 0, 32 * j:32 * j + 32], in_=CMr[:, gp, :]), r=["CMr"], w=["CL"])
            P.dve(lambda e, gp=gp, j=j: e.tensor_scalar(out=CL[:, gp, 1, 32 * j:32 * j + 32], in0=CMi[:, gp, :], scalar1=-1.0, scalar2=None,
                                                        op0=ALU.mult), r=["CMi"], w=["CL"])

        if l == 0:
            dbg_dump("AR", AR[:], ["AR"]); dbg_dump("AI", AI[:], ["AI"]); dbg_dump("MAG", MAG[:], ["MAG"])
            dbg_dump("TC", TC[:], ["TC"]); dbg_dump("TS", TS[:], ["TS"])
            dbg_dump("BL", BL[:], ["BL"]); dbg_dump("CL", CL[:], ["CL"])
            dbg_dump("cr", cr_, ["cr"]); dbg_dump("ci", ci_, ["ci"]); dbg_dump("BBr", BBr, ["BBr"]); dbg_dump("CMr", CMr, ["CMr"])

    def layer_pass(l, h):
        NT = 1024 + (NS if h == 1 else 0)
        tiles = [(i * 128, 128) for i in range(8)] + ([(1024, NS)] if h == 1 else [])
        xin = (xp, xs) if l == 0 else (x1[0:2048, :], x1[2048:2048 + NS, :])
        xout = (x1[0:2048, :], x1[2048:2048 + NS, :]) if l == 0 else (x2[0:2048, :], x2[2048:2048 + NS, :])

        def xrows(pair, ti, rows):
            return pair[0][h * 1024 + ti * 128:h * 1024 + ti * 128 + rows, :] if rows == 128 else pair[1]

        P.barrier()
        CV.reset()
        xt = [CV.get([128, D]) for _ in range(2)]
        xn = [CV.get([128, D], BF16) for _ in range(2)]
        junk = CV.get([128, D], BF16)
        ssq = CV.get([128, 20])
        for ti, (t0, rows) in enumerate(tiles):
            b = ti % 2
            P.dma("xl%d" % b, xt[b][0:rows, :], xrows(xin, ti, rows), w=["xt%d" % b])
            P.act(lambda e, b=b, rows=rows, ti=ti: e.activation(out=junk[0:rows, :], in_=xt[b][0:rows, :], func=AF.Square,
                                                                accum_out=ssq[0:rows, ti:ti + 1]),
                  r=["xt%d" % b], w=["junk", "ssq%d" % ti])
            P.act(lambda e, rows=rows, ti=ti: e.activation(out=ssq[0:rows, ti:ti + 1], in_=ssq[0:rows, ti:ti + 1], func=AF.Sqrt,
                                                           scale=1.0 / D, bias=epsb[0:rows, :]),
                  r=["ssq%d" % ti, "epsb"], w=["ssq%d" % ti])
            P.dve(lambda e, rows=rows, ti=ti: e.reciprocal(out=ssq[0:rows, ti:ti + 1], in_=ssq[0:rows, ti:ti + 1]),
                  r=["ssq%d" % ti], w=["ssq%d" % ti])
            P.act(lambda e, b=b, rows=rows, ti=ti: e.activation(out=xn[b][0:rows, :], in_=xt[b][0:rows, :], func=AF.Copy,
                                                                scale=ssq[0:rows, ti:ti + 1]),
                  r=["xt%d" % b, "ssq%d" % ti], w=["xn%d" % b])
            for k4 in range(4):
                def ev(pv, pk, k4=k4, t0=t0, rows=rows):
                    P.dve(lambda e: e.tensor_tensor(out=hT[:, 4 * k4:4 * k4 + 4, t0:t0 + rows], in0=pv,
                                                    in1=bc(ng[:, l, 4 * k4:4 * k4 + 4].unsqueeze(2), [128, 4, rows]), op=ALU.mult),
                          r=[pk, "ng"], w=["hT"])
                transpose_to(None, xn[b][:, k4 * 512:(k4 + 1) * 512], rows, 4, BF16, "xn%d" % b, ev)

        if (SUB if h == 1 else STAGE) < 2:
            return
        P.barrier()
        CV.reset()
        qT = CV.get([128, 2, 1040]); kT = CV.get([128, 2, 1040])
        vtok = CV.get([128, 9, W], BF16)
        sga = CV.get([128, 4, 1040], BF16)
        alrT = CV.get([16, 1040], BF16)

        def ev_q(ft, t0, n, pv, pk):
            P.act(lambda e: e.activation(out=qT[:, ft, t0:t0 + n], in_=pv, func=AF.Copy), r=[pk], w=["qT"])
        proj_fm(l, SEG["q"], 256, NT, ev_q)

        def ev_k(ft, t0, n, pv, pk):
            P.act(lambda e: e.activation(out=kT[:, ft, t0:t0 + n], in_=pv, func=AF.Copy), r=[pk], w=["kT"])
        proj_fm(l, SEG["k"], 256, NT, ev_k)

        def ev_a(ft, t0, n, pv, pk):
            P.act(lambda e: e.activation(out=alrT[:, t0:t0 + n], in_=pv, func=AF.Copy), r=[pk], w=["alrT"])
        proj_fm(l, SEG["a"], 16, NT, ev_a)

        def ev_ga(ft, t0, n, pv, pk):
            P.act(lambda e: e.activation(out=sga[:, ft, t0:t0 + n], in_=pv, func=AF.Silu), r=[pk], w=["sga"])
        proj_fm(l, SEG["ga"], 512, NT, ev_ga)

        def ev_v(ti, rows, pv, pk):
            P.act(lambda e: e.activation(out=vtok[0:rows, ti, :], in_=pv, func=AF.Copy), r=[pk], w=["vtok"])
        proj_tm(l, SEG["v"], 512, tiles, ev_v)

        sp = CV.get([128, 256]); eb = CV.get([128, 2, 128]); enb = CV.get([128, 2, 128])
        qmP = [[CV.get([128, 2, 128], BF16) for _ in range(2)] for _ in range(2)]
        kmP = [[CV.get([128, 2, 128], BF16) for _ in range(2)] for _ in range(2)]
        qfP = [CV.get([128, 2, 128], BF16) for _ in range(2)]
        kf = CV.get([128, 2, 128], BF16)
        ktokP = [CV.get([128, 256], BF16) for _ in range(2)]
        eblP = [CV.get([128, 2]) for _ in range(2)]
        scm = CV.get([128, 128], BF16)
        on = CV.get([128, W], BF16)
        osq = CV.get([128, 8]); tmpS = CV.get([128, 128])
        ojunk = CV.get([128, 128])

        def gla_out(rows, t0, opsum, opk):
            for hd in range(4):
                P.act(lambda e, hd=hd: e.activation(out=ojunk[0:rows, :], in_=opsum[0:rows, hd * 128:(hd + 1) * 128],
                                                    func=AF.Square, accum_out=osq[0:rows, hd:hd + 1]),
                      r=[opk], w=["ojunk", "osq"])
            P.act(lambda e: e.activation(out=osq[0:rows, 0:4], in_=osq[0:rows, 0:4], func=AF.Sqrt, scale=1.0 / 128,
                                         bias=epsb[0:rows, :]), r=["osq", "epsb"], w=["osq"])
            P.dve(lambda e: e.reciprocal(out=osq[0:rows, 0:4], in_=osq[0:rows, 0:4]), r=["osq"], w=["osq"])
            P.dve(lambda e: e.tensor_tensor(out=on[0:rows, :].rearrange("p (a b) -> p a b", a=4),
                                            in0=opsum[0:rows, :].rearrange("p (a b) -> p a b", a=4),
                                            in1=bc(osq[0:rows, 0:4].unsqueeze(2), [rows, 4, 128]), op=ALU.mult),
                  r=[opk, "osq"], w=["on"])

            def ev(pv, pk):
                for hd in range(4):
                    P.dve(lambda e, hd=hd: e.scalar_tensor_tensor(out=mT[:, hd, t0:t0 + rows], in0=pv[:, hd, :],
                                                                  scalar=glag[:, hd:hd + 1], in1=sga[:, hd, t0:t0 + rows],
                                                                  op0=ALU.mult, op1=ALU.mult),
                          r=[pk, "lp", "sga"], w=["mT"])
            if False:
                dbg_dump("on%d" % (t0 // 128), on, ["on"]); dbg_dump("osq%d" % (t0 // 128), osq, ["osq"])
            transpose_to(None, on, rows, 4, BF16, "on", ev)
            if False:
                dbg_dump("mt%d" % (t0 // 128), mT[:, 0:4, t0:t0 + 128], ["mT"])
                dbg_dump("sga%d" % (t0 // 128), sga[:, :, t0:t0 + 128], ["sga"])

        def stage1(ti):
            t0 = ti * 128
            pz = ti % 2
            qf = qfP[pz]; qm = qmP[pz]; km = kmP[pz]; ktok = ktokP[pz]; ebl = eblP[pz]
            zp, zk = next_ps()
            P.pe(lambda e: e.matmul(zp[:, 0:256], lhsT=alrT[:, t0:t0 + 128], rhs=wa2[:], start=True, stop=False),
                 r=["alrT", "lp"], w=[zk])
            P.pe(lambda e: e.matmul(zp[:, 0:256], lhsT=ones1[:], rhs=barow[:], start=False, stop=True),
                 r=["ones1", "lp"], w=[zk])
            P.act(lambda e: e.activation(out=sp, in_=zp[:, 0:256], func=AF.Exp, scale=-1.0), r=[zk], w=["sp"])
            P.act(lambda e: e.activation(out=sp, in_=sp, func=AF.Ln, bias=1.0), r=["sp"], w=["sp"])
            cp, ck = next_ps()
            for hp_ in range(2):
                P.pe(lambda e, hp_=hp_: e.matmul(cp[:, hp_ * 128:(hp_ + 1) * 128], lhsT=sp[:, hp_ * 128:(hp_ + 1) * 128],
                                                 rhs=triu[:], start=True, stop=True), r=["sp", "triu"], w=[ck])
            cpv = cp[:, 0:256].rearrange("p (a b) -> p a b", a=2)
            P.act(lambda e: e.activation(out=eb, in_=cpv, func=AF.Exp, scale=-1.0 / 16), r=[ck], w=["eb"])
            P.act(lambda e: e.activation(out=enb, in_=cpv, func=AF.Exp, scale=1.0 / 16), r=[ck], w=["enb"])
            P.dve(lambda e: e.scalar_tensor_tensor(out=qf, in0=qT[:, :, t0:t0 + 128], scalar=0.125, in1=eb,
                                                   op0=ALU.mult, op1=ALU.mult), r=["qT", "eb"], w=["qf%d" % pz])
            P.dve(lambda e: e.tensor_tensor(out=kf, in0=kT[:, :, t0:t0 + 128], in1=enb, op=ALU.mult),
                  r=["kT", "enb"], w=["kf"])
            P.dve(lambda e: e.tensor_copy(out=ebl, in_=eb[:, :, 127]), r=["eb"], w=["ebl%d" % pz])
            for h2 in range(2):
                P.dve(lambda e, h2=h2: e.tensor_scalar(out=qm[h2], in0=qf, scalar1=hmask[:, h2:h2 + 1], scalar2=None, op0=ALU.mult),
                      r=["qf%d" % pz, "hmask"], w=["qm%d_%d" % (pz, h2)])
                P.dve(lambda e, h2=h2: e.tensor_scalar(out=km[h2], in0=kf, scalar1=hmask[:, h2:h2 + 1], scalar2=None, op0=ALU.mult),
                      r=["kf", "hmask"], w=["km%d_%d" % (pz, h2)])

            def ev_kt(pv, pk):
                P.act(lambda e: e.activation(out=ktok.rearrange("p (a b) -> p a b", a=2), in_=pv, func=AF.Copy), r=[pk], w=["ktok%d" % pz])
            transpose_to(None, kf.rearrange("p a b -> p (a b)"), 128, 2, BF16, "kf", ev_kt)

        def stage2(ti):
            t0 = ti * 128
            pz = ti % 2
            qf = qfP[pz]; qm = qmP[pz]; km = kmP[pz]; ktok = ktokP[pz]; ebl = eblP[pz]
            op_, ok = next_ps()
            for hd in range(4):
                hp_, h2 = hd // 2, hd % 2
                s_p, s_k = next_ps()
                P.pe(lambda e, hp_=hp_, h2=h2: e.matmul(s_p[:, 0:128], lhsT=km[h2][:, hp_, :], rhs=qf[:, hp_, :],
                                                        start=True, stop=True), r=["km%d_%d" % (pz, h2), "qf%d" % pz], w=[s_k])
                P.dve(lambda e: e.tensor_tensor(out=scm, in0=s_p[:, 0:128], in1=triu[:], op=ALU.mult),
                      r=[s_k, "triu"], w=["scm"])
                P.pe(lambda e, hd=hd: e.matmul(op_[:, hd * 128:(hd + 1) * 128], lhsT=scm, rhs=vtok[:, ti, hd * 128:(hd + 1) * 128],
                                               start=True, stop=False), r=["scm", "vtok"], w=[ok])
                P.pe(lambda e, hd=hd, hp_=hp_, h2=h2: e.matmul(op_[:, hd * 128:(hd + 1) * 128], lhsT=qm[h2][:, hp_, :], rhs=Sbf[:, hp_, :],
                                                               start=False, stop=True), r=["qm%d_%d" % (pz, h2), "Sbf"], w=[ok])
            gla_out(128, t0, op_, ok)
            for hp_ in range(2):
                kv, kvk = next_ps()
                P.pe(lambda e, hp_=hp_: e.matmul(kv[:, 0:256], lhsT=ktok[:, hp_ * 128:(hp_ + 1) * 128],
                                                 rhs=vtok[:, ti, hp_ * 256:(hp_ + 1) * 256], start=True, stop=True),
                     r=["ktok%d" % pz, "vtok"], w=[kvk])
                for h2 in range(2):
                    rs = slice(h2 * 64, h2 * 64 + 64)
                    P.dve(lambda e, rs=rs, h2=h2, hp_=hp_: e.tensor_tensor(out=tmpS[rs, :], in0=kv[rs, h2 * 128:(h2 + 1) * 128],
                                                                          in1=S32[rs, hp_, :], op=ALU.add),
                          r=[kvk, "S32"], w=["tmpS"])
                    P.dve(lambda e, rs=rs, hp_=hp_: e.tensor_scalar(out=S32[rs, hp_, :], in0=tmpS[rs, :], scalar1=ebl[rs, hp_:hp_ + 1],
                                                                    scalar2=None, op0=ALU.mult), r=["tmpS", "ebl%d" % pz], w=["S32"])
            P.act(lambda e: e.activation(out=Sbf[:], in_=S32[:], func=AF.Copy), r=["S32"], w=["Sbf"])

        stage1(0)
        for ti in range(8):
            if ti + 1 < 8:
                stage1(ti + 1)
            stage2(ti)
        if h == 1:
            P.dma("st", bass.AP(o_gla_p.tensor, l * 32768, [[128, 128], [16384, 2], [1, 128]]), S32[:], r=["S32"])
            if not NOSAMP:
                gla_samples(l, qT, kT, vtok, alrT, gla_out)

        if (SUB if h == 1 else STAGE) < 3:
            return
        P.barrier()
        CV.reset(0, MAINLIM)
        s5ctx = s5_start(l, h, NT)
        sgb = CV.get([128, 4, 1040], BF16); ug = CV.get([128, 4, 1040], BF16)
        vn = CV.get([128, W], BF16); vnf = CV.get([128, W]); st6 = CV.get([128, 4, 6]); mv = CV.get([128, 4, 2])
        tmpb = CV.get([128, 4, 128])
        vnT = CV.get([128, 4, NS])

        def ev_gb(ft, t0, n, pv, pk):
            P.act(lambda e: e.activation(out=sgb[:, ft, t0:t0 + n], in_=pv, func=AF.Silu), r=[pk], w=["sgb"])
        proj_fm(l, SEG["gb"], 512, NT, ev_gb)

        def ev_ub(ft, t0, n, pv, pk):
            P.dve(lambda e: e.tensor_tensor(out=ug[:, ft, t0:t0 + n], in0=pv, in1=sgb[:, ft, t0:t0 + n], op=ALU.mult),
                  r=[pk, "sgb"], w=["ug"])
        proj_fm(l, SEG["ub"], 512, NT, ev_ub)

        def ev_vb(ti, rows, pv, pk):
            t0 = tiles[ti][0]
            for hd in range(4):
                P.dve(lambda e, hd=hd: e.bn_stats(out=st6[0:rows, hd, :], in_=pv[:, hd * 128:(hd + 1) * 128]), r=[pk], w=["st6"])
                P.dve(lambda e, hd=hd: e.bn_aggr(out=mv[0:rows, hd, :], in_=st6[0:rows, hd, :]), r=["st6"], w=["mv"])
            P.act(lambda e: e.activation(out=mv[0:rows, :, 1], in_=mv[0:rows, :, 1], func=AF.Sqrt, bias=epsb[0:rows, :]),
                  r=["mv", "epsb"], w=["mv"])
            P.dve(lambda e: e.reciprocal(out=mv[0:rows, :, 1], in_=mv[0:rows, :, 1]), r=["mv"], w=["mv"])
            for hd in range(4):
                P.dve(lambda e, hd=hd: e.tensor_scalar(out=vnf[0:rows, hd * 128:(hd + 1) * 128], in0=pv[:, hd * 128:(hd + 1) * 128],
                                                       scalar1=mv[0:rows, hd, 0:1], scalar2=mv[0:rows, hd, 1:2],
                                                       op0=ALU.subtract, op1=ALU.mult), r=[pk, "mv"], w=["vnf"])
            P.dve(lambda e: e.tensor_tensor(out=vnf[0:rows, :], in0=vnf[0:rows, :], in1=sgug[0:rows, :], op=ALU.mult),
                  r=["vnf", "lp"], w=["vnf"])
            P.act(lambda e: e.activation(out=vn[0:rows, :], in_=vnf[0:rows, :], func=AF.Copy), r=["vnf"], w=["vn"])
            if rows == 128:
                mp, mk = next_ps()
                for hd in range(4):
                    P.pe(lambda e, hd=hd, mp=mp: e.matmul(mp[:, hd * 128:(hd + 1) * 128], lhsT=vn[:, hd * 128:(hd + 1) * 128],
                                                          rhs=sguw[:, hd, :], start=True, stop=True), r=["vn", "lp"], w=[mk])
                P.dve(lambda e, mp=mp: e.tensor_tensor(out=tmpb, in0=mp[:, :].rearrange("p (a b) -> p a b", a=4), in1=sgub[:], op=ALU.add),
                      r=[mk, "lp"], w=["tmpb"])
                P.dve(lambda e, t0=t0: e.tensor_tensor(out=mT[:, 4:8, t0:t0 + 128], in0=tmpb, in1=ug[:, :, t0:t0 + 128], op=ALU.mult),
                      r=["tmpb", "ug"], w=["mT"])
            else:
                P.dma("st", o_vn_s[l], vnf[0:NS, :], r=["vnf"])

                def ev(pv2, pk2):
                    P.dve(lambda e: e.tensor_tensor(out=vnT, in0=pv2, in1=bc(w00[:].unsqueeze(2), [128, 4, NS]), op=ALU.mult),
                          r=[pk2, "lp"], w=["vnT"])
                    P.dve(lambda e: e.tensor_tensor(out=vnT, in0=vnT, in1=bc(b00[:].unsqueeze(2), [128, 4, NS]), op=ALU.add),
                          r=["vnT", "lp"], w=["vnT"])
                    P.dve(lambda e: e.tensor_tensor(out=mT[:, 4:8, t0:t0 + NS], in0=vnT, in1=ug[:, :, t0:t0 + NS], op=ALU.mult),
                          r=["vnT", "ug"], w=["mT"])
                transpose_to(None, vnf, NS, 4, F32, "vnf", ev)
        proj_tm(l, SEG["vb"], 512, tiles, ev_vb)

        if (SUB if h == 1 else STAGE) < 4:
            return
        P.barrier()
        CV.reset(0, MAINLIM)
        ccT = CV.get([128, 4, 1040], BF16)
        cg = mT[:, 8:12, :]
        zc = CV.get([128, 4, 1042])
        P.dve(lambda e: e.tensor_copy(out=zc[:, :, 0:2], in_=zcar[:]), r=["zcar"], w=["zc"])
        yc = CV.get([128, 4, 1040])

        def ev_cc(ft, t0, n, pv, pk):
            P.act(lambda e: e.activation(out=ccT[:, ft, t0:t0 + n], in_=pv, func=AF.Copy), r=[pk], w=["ccT"])
        proj_fm(l, SEG["cc"], 512, NT, ev_cc)

        def ev_hc(ft, t0, n, pv, pk):
            P.dve(lambda e: e.tensor_tensor(out=zc[:, ft, 2 + t0:2 + t0 + n], in0=pv, in1=ccT[:, ft, t0:t0 + n], op=ALU.mult),
                  r=[pk, "ccT"], w=["zc"])
        proj_fm(l, SEG["hc"], 512, NT, ev_hc)

        def ev_gc(ft, t0, n, pv, pk):
            P.act(lambda e: e.activation(out=cg[:, ft, t0:t0 + n], in_=pv, func=AF.Silu), r=[pk], w=["cg"])
        proj_fm(l, SEG["gc"], 512, NT, ev_gc)

        def ev_cb(ft, t0, n, pv, pk):
            P.dve(lambda e: e.tensor_tensor(out=cg[:, ft, t0:t0 + n], in0=pv, in1=cg[:, ft, t0:t0 + n], op=ALU.mult),
                  r=[pk, "cg"], w=["cg"])
        proj_fm(l, SEG["cb"], 512, NT, ev_cb)
        for ct in range(4):
            P.dve(lambda e, ct=ct: e.tensor_scalar(out=yc[:, ct, 0:1024], in0=zc[:, ct, 0:1024], scalar1=cw[:, 0, ct:ct + 1],
                                                   scalar2=None, op0=ALU.mult), r=["zc", "lp"], w=["yc"])
            P.dve(lambda e, ct=ct: e.scalar_tensor_tensor(out=yc[:, ct, 0:1024], in0=zc[:, ct, 1:1025], scalar=cw[:, 1, ct:ct + 1],
                                                          in1=yc[:, ct, 0:1024], op0=ALU.mult, op1=ALU.add), r=["zc", "lp", "yc"], w=["yc"])
            P.dve(lambda e, ct=ct: e.scalar_tensor_tensor(out=yc[:, ct, 0:1024], in0=zc[:, ct, 2:1026], scalar=cw[:, 2, ct:ct + 1],
                                                          in1=yc[:, ct, 0:1024], op0=ALU.mult, op1=ALU.add), r=["zc", "lp", "yc"], w=["yc"])
        P.dve(lambda e: e.tensor_tensor(out=mT[:, 8:12, 0:1024], in0=yc[:, :, 0:1024], in1=cg[:, :, 0:1024], op=ALU.mult),
              r=["yc", "cg"], w=["mT"])
        if h == 1:
            conv_samples(l, yc, cg, zc)
            for jj in range(2):
                P.dma("st", o_conv_p[l, jj].rearrange("(a p) -> p a", p=128), zc[:, :, 1024 + jj], r=["zc"],
                      allow_slow_non_contiguous=True)
        P.dve(lambda e: e.tensor_copy(out=zcar[:], in_=zc[:, :, 1024:1026]), r=["zc"], w=["zcar"])

        if (SUB if h == 1 else STAGE) < 5:
            return
        while TICK[0] is not None:
            tick()
        if h == 1:
            P.barrier()
            CV.reset(0, MAINLIM)
            s5_finish(l, h, s5ctx)

        if (SUB if h == 1 else STAGE) < 6:
            return
        P.barrier()
        CV.reset()
        xo = [CV.get([128, 256]) for _ in range(6)]
        cnt = 0
        for cb_ in range(8):
            wv, wk = load_w(w_out[l, :, cb_ * 256:(cb_ + 1) * 256])
            for ti, (t0, rows) in enumerate(tiles):
                b = cnt % 6
                cnt += 1
                P.dma("xo%d" % b, xo[b][0:rows, :], xrows(xin, ti, rows)[:, cb_ * 256:(cb_ + 1) * 256], w=["xo%d" % b])
                pt, pk = next_ps()
                for k in range(16):
                    P.pe(lambda e, k=k, pt=pt, wv=wv, t0=t0, rows=rows: e.matmul(pt[0:rows, 0:256], lhsT=mT[:, k, t0:t0 + rows],
                                                                                rhs=wv[:, k, :], start=(k == 0), stop=(k == 15)),
                         r=[wk, "mT", "mT3"], w=[pk])
                P.dve(lambda e, b=b, pt=pt, rows=rows: e.tensor_tensor(out=xo[b][0:rows, :], in0=pt[0:rows, 0:256], in1=xo[b][0:rows, :],
                                                                      op=ALU.add), r=[pk, "xo%d" % b], w=["xo%d" % b])
                P.dma("xs%d" % b, xrows(xout, ti, rows)[:, cb_ * 256:(cb_ + 1) * 256], xo[b][0:rows, :], r=["xo%d" % b], w=["xscratch"],
                      phys="act")

    def gla_samples(l, qT, kT, vtok, alrT, gla_out):
        g = CV.get
        c0 = 1024
        afm = g([128, 2, NS]); SS = g([128, 2, NS, 128])
        kms = g([NS, NS, 256], BF16, rows=NS); ktk = g([NS, 256], BF16, rows=NS)
        qs = g([128, 2, NS]); kbf = g([128, 2, NS], BF16)
        for hp_ in range(2):
            P.dma("ld", SS[:, hp_], bass.AP(sgla.tensor, l * NS * 32768 + hp_ * 16384, [[128, 128], [32768, NS], [1, 128]]), w=["SS"])
        zp, zk = next_ps()
        for hp_ in range(2):
            P.pe(lambda e, hp_=hp_, zp=zp: e.matmul(zp[:, hp_ * NS:(hp_ + 1) * NS], lhsT=wa2[:, hp_ * 128:(hp_ + 1) * 128],
                                                    rhs=alrT[:, c0:c0 + NS], start=True, stop=True), r=["lp", "alrT"], w=[zk])
        for hp_ in range(2):
            P.act(lambda e, hp_=hp_, zp=zp: e.activation(out=afm[:, hp_, :], in_=zp[:, hp_ * NS:(hp_ + 1) * NS], func=AF.Exp,
                                                         scale=-1.0, bias=nbafm[:, hp_:hp_ + 1]), r=[zk, "lp"], w=["afm"])
        P.act(lambda e: e.activation(out=afm, in_=afm, func=AF.Ln, bias=1.0), r=["afm"], w=["afm"])
        P.act(lambda e: e.activation(out=afm, in_=afm, func=AF.Exp, scale=-1.0 / 16), r=["afm"], w=["afm"])
        P.dve(lambda e: e.tensor_copy(out=kbf, in_=kT[:, :, c0:c0 + NS]), r=["kT"], w=["kbf"])
        pt, pk = next_ps(6, 8)
        pvb = pt[:].bitcast(BF16)
        for hp_ in range(2):
            P.pe(lambda e, hp_=hp_: e.transpose(pvb[0:NS, hp_ * 128:(hp_ + 1) * 128], kbf[:, hp_, :], identb[:]),
                 r=["kbf", "identb"], w=[pk])
        P.act(lambda e: e.activation(out=ktk, in_=pvb[0:NS, 0:256], func=AF.Copy), r=[pk], w=["ktk"])
        P.dve(lambda e: e.tensor_tensor(out=kms, in0=bc(ktk.unsqueeze(1), [NS, NS, 256]), in1=bc(eye16[:].unsqueeze(2), [NS, NS, 256]),
                                        op=ALU.mult), r=["ktk", "eye16"], w=["kms"])
        P.dve(lambda e: e.tensor_scalar(out=qs, in0=qT[:, :, c0:c0 + NS], scalar1=0.125, scalar2=None, op0=ALU.mult), r=["qT"], w=["qs"])
        eyeb = g([128, NS, NS])
        P.dma("ld", eyeb, bass.AP(eye16_d.tensor, 0, [[0, 128], [NS, NS], [1, NS]]), w=["eyeb"])
        qmsm = [g([128, 2, NS, NS]) for _ in range(2)]
        qsm = [g([128, 2, NS]) for _ in range(2)]
        for h2 in range(2):
            P.dve(lambda e, h2=h2: e.tensor_scalar(out=qsm[h2], in0=qs, scalar1=hmask[:, h2:h2 + 1], scalar2=None, op0=ALU.mult),
                  r=["qs", "hmask"], w=["qsm%d" % h2])
            for hp_ in range(2):
                P.dve(lambda e, h2=h2, hp_=hp_: e.tensor_tensor(out=qmsm[h2][:, hp_], in0=eyeb, in1=bc(qsm[h2][:, hp_, :].unsqueeze(2), [128, NS, NS]),
                                                                op=ALU.mult), r=["eyeb", "qsm%d" % h2], w=["qmsm%d" % h2])
        for hp_ in range(2):
            for b in range(NS):
                kv, kvk = next_ps()
                P.pe(lambda e, kv=kv, b=b, hp_=hp_: e.matmul(kv[:, 0:256], lhsT=kms[:, b, hp_ * 128:(hp_ + 1) * 128],
                                                             rhs=vtok[0:NS, 8, hp_ * 256:(hp_ + 1) * 256], start=True, stop=True),
                     r=["kms", "vtok"], w=[kvk])
                for h2 in range(2):
                    rs = slice(h2 * 64, h2 * 64 + 64)
                    P.dve(lambda e, kv=kv, rs=rs, h2=h2, hp_=hp_, b=b: e.scalar_tensor_tensor(
                        out=SS[rs, hp_, b, :], in0=SS[rs, hp_, b, :], scalar=afm[rs, hp_, b:b + 1], in1=kv[rs, h2 * 128:(h2 + 1) * 128],
                        op0=ALU.mult, op1=ALU.add), r=[kvk, "SS", "afm"], w=["SS"])
        for hp_ in range(2):
            P.dma("st", bass.AP(o_gla_s.tensor, l * NS * 32768 + hp_ * 16384, [[128, 128], [32768, NS], [1, 128]]), SS[:, hp_], r=["SS"])
        op_, ok = next_ps()
        for hd in range(4):
            hp_, h2 = hd // 2, hd % 2
            rs = slice(h2 * 64, h2 * 64 + 64)
            for b in range(NS):
                P.pe(lambda e, hd=hd, hp_=hp_, h2=h2, b=b: e.matmul(op_[0:NS, hd * 128:(hd + 1) * 128], lhsT=qmsm[h2][:, hp_, b, :],
                                                                  rhs=SS[:, hp_, b, :], start=(b == 0), stop=(b == NS - 1)),
                     r=["qmsm%d" % h2, "SS"], w=[ok])
        gla_out(NS, c0, op_, ok)

    def conv_samples(l, yc, cg, zc):
        g = CV.get
        c0 = 1024
        cbuf = g([NS, 2 * W], rows=NS); cbT = g([128, 2, 4, NS]); z0t = g([NS, W], rows=NS)
        P.dma("ld", cbuf, sconv[l].rearrange("b j c -> b (j c)"), w=["cbuf"])
        for j in range(2):
            def ev(pv, pk, j=j):
                P.dve(lambda e: e.tensor_copy(out=cbT[:, j], in_=pv), r=[pk], w=["cbT"])
            transpose_to(None, cbuf[:, j * W:(j + 1) * W], NS, 4, F32, "cbuf", ev)
        for ct in range(4):
            P.dve(lambda e, ct=ct: e.tensor_scalar(out=yc[:, ct, c0:c0 + NS], in0=cbT[:, 0, ct, :], scalar1=cw[:, 0, ct:ct + 1],
                                                   scalar2=None, op0=ALU.mult), r=["cbT", "lp", "yc"], w=["yc"])
            P.dve(lambda e, ct=ct: e.scalar_tensor_tensor(out=yc[:, ct, c0:c0 + NS], in0=cbT[:, 1, ct, :], scalar=cw[:, 1, ct:ct + 1],
                                                          in1=yc[:, ct, c0:c0 + NS], op0=ALU.mult, op1=ALU.add), r=["cbT", "lp", "yc"], w=["yc"])
            P.dve(lambda e, ct=ct: e.scalar_tensor_tensor(out=yc[:, ct, c0:c0 + NS], in0=zc[:, ct, 2 + c0:2 + c0 + NS],
                                                          scalar=cw[:, 2, ct:ct + 1], in1=yc[:, ct, c0:c0 + NS], op0=ALU.mult, op1=ALU.add),
                  r=["zc", "lp", "yc"], w=["yc"])
        P.dve(lambda e: e.tensor_tensor(out=mT[:, 8:12, c0:c0 + NS], in0=yc[:, :, c0:c0 + NS], in1=cg[:, :, c0:c0 + NS], op=ALU.mult),
              r=["yc", "cg"], w=["mT"])
        P.dma("st", o_conv_s[l, :, 0, :], cbuf[:, W:2 * W], r=["cbuf"])
        zs = g([128, 4, NS])
        P.dve(lambda e: e.tensor_copy(out=zs, in_=zc[:, :, 2 + c0:2 + c0 + NS]), r=["zc"], w=["zs"])
        pt, pk = next_ps(6, 8)
        for ct in range(4):
            P.pe(lambda e, ct=ct, pt=pt: e.transpose(pt[0:NS, ct * 128:(ct + 1) * 128], zs[:, ct, :], ident[:]), r=["zs", "ident"], w=[pk])
        P.act(lambda e, pt=pt: e.activation(out=z0t, in_=pt[0:NS, 0:W], func=AF.Copy), r=[pk], w=["z0t"])
        P.dma("st", o_conv_s[l, :, 1, :], z0t, r=["z0t"])

    def s5_start(l, h, NT):
        CVS.reset(MAINLIM, ARENA)
        g = CVS.get
        udT = g([128, 4, 1040], BF16)
        gdc = g([128, 4, 128], BF16)

        def ev_ud(ft, t0, n, pv, pk):
            P.act(lambda e: e.activation(out=udT[:, ft, t0:t0 + n], in_=pv, func=AF.Copy), r=[pk], w=["udT"])
        proj_fm(l, SEG["ud"], 512, NT, ev_ud)

        def ev_gd(ft, t0, n, pv, pk):
            P.act(lambda e: e.activation(out=mT[:, 12 + ft, t0:t0 + n], in_=pv, func=AF.Silu), r=[pk], w=["mT3"])
        proj_fm(l, SEG["gd"], 512, NT, ev_gd)

        Er = g([128, 4, 128]); Ei = g([128, 4, 128]); a2 = g([128, 4, 128]); a4 = g([128, 4, 128])
        Xr = g([128, 4, 128], BF16); Xi = g([128, 4, 128], BF16)
        xe = g([128, 2, 16])
        yv = g([128, 4, 128]); y2 = g([128, 4, 128]); sig = g([128, 4, 128])

        def rot(dst_r, dst_i, src_r, src_i, gq, sign, keys_r, key_w):
            tc = TC[:, gq * 4:(gq + 1) * 4, :]; ts = TS[:, gq * 4:(gq + 1) * 4, :]
            ap_, ak = next_ps()
            a1p = ap_[:, :].rearrange("p (a b) -> p a b", a=4)
            bp_, bk = next_ps()
            a3p = bp_[:, :].rearrange("p (a b) -> p a b", a=4)
            P.dve(lambda e: e.tensor_tensor(out=a1p, in0=src_r, in1=tc, op=ALU.mult), r=keys_r + ["TC"], w=[ak])
            P.dve(lambda e: e.tensor_tensor(out=a2, in0=src_i, in1=ts, op=ALU.mult), r=keys_r + ["TS"], w=["a2"])
            P.dve(lambda e: e.tensor_tensor(out=a3p, in0=src_i, in1=tc, op=ALU.mult), r=keys_r + ["TC"], w=[bk])
            P.dve(lambda e: e.tensor_tensor(out=a4, in0=src_r, in1=ts, op=ALU.mult), r=keys_r + ["TS"], w=["a4"])
            P.dve(lambda e: e.tensor_tensor(out=dst_r, in0=a1p, in1=a2, op=(ALU.subtract if sign > 0 else ALU.add)),
                  r=[ak, "a2"], w=[key_w + "r"])
            P.dve(lambda e: e.tensor_tensor(out=dst_i, in0=a3p, in1=a4, op=(ALU.add if sign > 0 else ALU.subtract)),
                  r=[bk, "a4"], w=[key_w + "i"])

        def y_evac(kt, pv, pk, t0, n):
            P.dve(lambda e: e.scalar_tensor_tensor(out=yv[:, kt, 0:n], in0=udT[:, kt, t0:t0 + n], scalar=dsk[:, kt:kt + 1],
                                                   in1=pv, op0=ALU.mult, op1=ALU.add), r=[pk, "udT", "lp"], w=["yv"])

        def glu_tail(t0, n):
            P.dve(lambda e: e.tensor_tensor(out=y2[:, :, 0:n], in0=yv[:, :, 0:n], in1=yv[:, :, 0:n], op=ALU.mult), r=["yv"], w=["y2"])
            P.dve(lambda e: e.tensor_scalar(out=y2[:, :, 0:n], in0=y2[:, :, 0:n], scalar1=0.044715 * 1.5957691216, scalar2=1.5957691216,
                                            op0=ALU.mult, op1=ALU.add), r=["y2"], w=["y2"])
            P.dve(lambda e: e.tensor_tensor(out=y2[:, :, 0:n], in0=y2[:, :, 0:n], in1=yv[:, :, 0:n], op=ALU.mult), r=["y2", "yv"], w=["y2"])
            P.act(lambda e: e.activation(out=sig[:, :, 0:n], in_=y2[:, :, 0:n], func=AF.Sigmoid), r=["y2"], w=["sig"])
            P.dve(lambda e: e.tensor_tensor(out=gdc[:, :, 0:n], in0=yv[:, :, 0:n], in1=sig[:, :, 0:n], op=ALU.mult),
                  r=["yv", "sig"], w=["gdc"])
            for ft in range(4):
                pt, pk = next_ps()
                for kt in range(4):
                    P.pe(lambda e, pt=pt, kt=kt, ft=ft: e.matmul(pt[:, 0:n], lhsT=gluw[:, kt, ft * 128:(ft + 1) * 128], rhs=gdc[:, kt, 0:n],
                                                                 start=(kt == 0), stop=(kt == 3)), r=["lp", "gdc"], w=[pk])
                P.act(lambda e, pt=pt, ft=ft: e.activation(out=sig[:, ft, 0:n], in_=pt[:, 0:n], func=AF.Sigmoid, bias=glub[:, ft:ft + 1]),
                      r=[pk, "lp"], w=["sig"])
            P.dve(lambda e: e.tensor_tensor(out=y2[:, :, 0:n], in0=sig[:, :, 0:n], in1=mT[:, 12:16, t0:t0 + n], op=ALU.mult), r=["sig", "mT3"], w=["y2"])
            P.dve(lambda e: e.tensor_tensor(out=mT[:, 12:16, t0:t0 + n], in0=y2[:, :, 0:n], in1=gdc[:, :, 0:n], op=ALU.mult),
                  r=["y2", "gdc", "mT3"], w=["mT3"])

        def gen():
            for ci_ in range(8):
                t0 = ci_ * 128
                for gq in range(4):
                    pr, prk = next_ps(); pi, pik = next_ps()
                    for j in range(4):
                        gp = gq * 4 + j
                        P.pe(lambda e, pr=pr, j=j, gp=gp, gq=gq: e.matmul(pr[:, j * 128:(j + 1) * 128], lhsT=BL[:, gp, 0, :], rhs=udT[:, gq, t0:t0 + 128],
                                                                          start=True, stop=True), r=["BL", "udT"], w=[prk])
                        P.pe(lambda e, pi=pi, j=j, gp=gp, gq=gq: e.matmul(pi[:, j * 128:(j + 1) * 128], lhsT=BL[:, gp, 1, :], rhs=udT[:, gq, t0:t0 + 128],
                                                                          start=True, stop=True), r=["BL", "udT"], w=[pik])
                    prv = pr[:, :].rearrange("p (a b) -> p a b", a=4); piv = pi[:, :].rearrange("p (a b) -> p a b", a=4)
                    rot(Er, Ei, prv, piv, gq, -1, [prk, pik], "E")
                    Wr = psb[6][:, :].rearrange("p (a b) -> p a b", a=4); Wi = psb[7][:, :].rearrange("p (a b) -> p a b", a=4)
                    for j in range(4):
                        gp = gq * 4 + j
                        P.dve(lambda e, j=j, gp=gp: e.tensor_tensor_scan(out=Wr[:, j, :], data0=bc(MAG[:, gp:gp + 1], [128, 128]), data1=Er[:, j, :],
                                                                         initial=Xc[:, 0, gp:gp + 1], op0=ALU.mult, op1=ALU.add),
                              r=["MAG", "Er", "Xc"], w=["ps6"])
                        P.dve(lambda e, j=j, gp=gp: e.tensor_tensor_scan(out=Wi[:, j, :], data0=bc(MAG[:, gp:gp + 1], [128, 128]), data1=Ei[:, j, :],
                                                                         initial=Xc[:, 1, gp:gp + 1], op0=ALU.mult, op1=ALU.add),
                              r=["MAG", "Ei", "Xc"], w=["ps7"])
                    if False:
                        dbg_dump("Er", Er, ["Er"]); dbg_dump("Ei", Ei, ["Ei"]); dbg_dump("Wr", Wr, ["Wr"]); dbg_dump("Wi", Wi, ["Wi"])
                    rot(Er, Ei, Wr, Wi, gq, +1, ["ps6", "ps7"], "E")
                    if False:
                        dbg_dump("Xr", Er, ["Er"]); dbg_dump("Xi", Ei, ["Ei"])
                    P.act(lambda e: e.activation(out=Xr, in_=Er, func=AF.Copy), r=["Er"], w=["Xr"])
                    P.act(lambda e: e.activation(out=Xi, in_=Ei, func=AF.Copy), r=["Ei"], w=["Xi"])
                    P.dve(lambda e, gq=gq: e.tensor_copy(out=xe[:, 0, gq * 4:(gq + 1) * 4], in_=Er[:, :, 127]), r=["Er"], w=["xe"])
                    P.dve(lambda e, gq=gq: e.tensor_copy(out=xe[:, 1, gq * 4:(gq + 1) * 4], in_=Ei[:, :, 127]), r=["Ei"], w=["xe"])
                    pt, pk = next_ps()
                    for j in range(4):
                        gp = gq * 4 + j
                        P.pe(lambda e, pt=pt, gp=gp, j=j: e.matmul(pt[:, 0:128], lhsT=CL[:, gp, 0, :], rhs=Xr[:, j, :], start=(j == 0), stop=False),
                             r=["CL", "Xr"], w=[pk])
                        P.pe(lambda e, pt=pt, gp=gp, j=j: e.matmul(pt[:, 0:128], lhsT=CL[:, gp, 1, :], rhs=Xi[:, j, :], start=False, stop=(j == 3)),
                             r=["CL", "Xi"], w=[pk])
                    y_evac(gq, pt[:, 0:128], pk, t0, 128)
                    yield
                P.dve(lambda e: e.tensor_copy(out=Xc[:], in_=xe), r=["xe", "ps6", "ps7"], w=["Xc"])
                if False:
                    dbg_dump("yv", yv, ["yv"])
                glu_tail(t0, 128)
                yield

        def finish():
            g = CV.get
            if h == 1:
                for c_, dst in ((0, o_sre_p), (1, o_sim_p)):
                    for hh in range(2):
                        P.dma("st", bass.AP(dst.tensor, l * 2048 + hh * 64, [[1, 64], [128, 16]]), Xc[hh * 64:hh * 64 + 64, c_, :], r=["Xc"],
                              allow_slow_non_contiguous=True)
                c0 = 1024
                x0 = g([NS, 2048], rows=NS); x0T = g([128, 2, 16, NS]); xn_ = g([128, 2, 16, NS]); xnb = g([128, 2, 16, NS], BF16)
                xnt = g([NS, 2048], rows=NS)
                for c_ in range(2):
                    P.dma("ld", x0, (sre, sim)[c_][l].rearrange("b g n -> b (g n)"), w=["x0"])
                    for q4 in range(4):
                        def ev(pv, pk, c_=c_, q4=q4):
                            P.dve(lambda e: e.tensor_copy(out=x0T[:, c_, q4 * 4:(q4 + 1) * 4, :], in_=pv), r=[pk], w=["x0T"])
                        transpose_to(None, x0[:, q4 * 512:(q4 + 1) * 512], NS, 4, F32, "x0", ev)
                arb = bc(AR[:].unsqueeze(2), [128, 16, NS]); aib = bc(AI[:].unsqueeze(2), [128, 16, NS])
                t_a = g([128, 16, NS]); t_b = g([128, 16, NS])
                for gq in range(4):
                    pr, prk = next_ps(); pi, pik = next_ps()
                    for j in range(4):
                        gp = gq * 4 + j
                        P.pe(lambda e, pr=pr, j=j, gp=gp, gq=gq: e.matmul(pr[:, j * NS:(j + 1) * NS], lhsT=BL[:, gp, 0, :], rhs=udT[:, gq, c0:c0 + NS],
                                                                          start=True, stop=True), r=["BL", "udT"], w=[prk])
                        P.pe(lambda e, pi=pi, j=j, gp=gp, gq=gq: e.matmul(pi[:, j * NS:(j + 1) * NS], lhsT=BL[:, gp, 1, :], rhs=udT[:, gq, c0:c0 + NS],
                                                                          start=True, stop=True), r=["BL", "udT"], w=[pik])
                    P.dve(lambda e, pr=pr, gq=gq: e.tensor_copy(out=xn_[:, 0, gq * 4:(gq + 1) * 4, :], in_=pr[:, 0:4 * NS].rearrange("p (a b) -> p a b", a=4)),
                          r=[prk], w=["xn_"])
                    P.dve(lambda e, pi=pi, gq=gq: e.tensor_copy(out=xn_[:, 1, gq * 4:(gq + 1) * 4, :], in_=pi[:, 0:4 * NS].rearrange("p (a b) -> p a b", a=4)),
                          r=[pik], w=["xn_"])
                P.dve(lambda e: e.tensor_tensor(out=t_a, in0=x0T[:, 0], in1=arb, op=ALU.mult), r=["x0T", "AR"], w=["t_a"])
                P.dve(lambda e: e.tensor_tensor(out=xn_[:, 0], in0=xn_[:, 0], in1=t_a, op=ALU.add), r=["xn_", "t_a"], w=["xn_"])
                P.dve(lambda e: e.tensor_tensor(out=t_b, in0=x0T[:, 1], in1=aib, op=ALU.mult), r=["x0T", "AI"], w=["t_b"])
                P.dve(lambda e: e.tensor_tensor(out=xn_[:, 0], in0=xn_[:, 0], in1=t_b, op=ALU.subtract), r=["xn_", "t_b"], w=["xn_"])
                P.dve(lambda e: e.tensor_tensor(out=t_a, in0=x0T[:, 1], in1=arb, op=ALU.mult), r=["x0T", "AR", "xn_"], w=["t_a"])
                P.dve(lambda e: e.tensor_tensor(out=xn_[:, 1], in0=xn_[:, 1], in1=t_a, op=ALU.add), r=["xn_", "t_a"], w=["xn_"])
                P.dve(lambda e: e.tensor_tensor(out=t_b, in0=x0T[:, 0], in1=aib, op=ALU.mult), r=["x0T", "AI", "xn_"], w=["t_b"])
                P.dve(lambda e: e.tensor_tensor(out=xn_[:, 1], in0=xn_[:, 1], in1=t_b, op=ALU.add), r=["xn_", "t_b"], w=["xn_"])
                P.act(lambda e: e.activation(out=xnb, in_=xn_, func=AF.Copy), r=["xn_"], w=["xnb"])
                for kt in range(4):
                    pt, pk = next_ps()
                    for j in range(4):
                        gp = kt * 4 + j
                        P.pe(lambda e, pt=pt, gp=gp, j=j: e.matmul(pt[:, 0:NS], lhsT=CL[:, gp, 0, :], rhs=xnb[:, 0, gp, :], start=(j == 0), stop=False),
                             r=["CL", "xnb"], w=[pk])
                        P.pe(lambda e, pt=pt, gp=gp, j=j: e.matmul(pt[:, 0:NS], lhsT=CL[:, gp, 1, :], rhs=xnb[:, 1, gp, :], start=False, stop=(j == 3)),
                             r=["CL", "xnb"], w=[pk])
                    y_evac(kt, pt[:, 0:NS], pk, c0, NS)
                glu_tail(c0, NS)
                for c_, dst in ((0, o_sre_s), (1, o_sim_s)):
                    for q4 in range(4):
                        pt, pk = next_ps(6, 8)
                        for j in range(4):
                            gp = q4 * 4 + j
                            P.pe(lambda e, pt=pt, j=j, gp=gp, c_=c_: e.transpose(pt[0:NS, j * 128:(j + 1) * 128], xn_[:, c_, gp, :], ident[:]),
                                 r=["xn_", "ident"], w=[pk])
                        P.act(lambda e, pt=pt, c_=c_, q4=q4: e.activation(out=xnt[:, q4 * 512:(q4 + 1) * 512], in_=pt[0:NS, 0:512], func=AF.Copy),
                              r=[pk], w=["xnt"])
                    P.dma("st", dst[l].rearrange("b g n -> b (g n)"), xnt, r=["xnt"])


        TICK[0] = gen()
        return finish

    def s5_finish(l, h, fin):
        fin()

    def final_norm():
        P.barrier()
        CV.reset()
        xt = [CV.get([128, D]) for _ in range(2)]
        yo = [CV.get([128, D]) for _ in range(2)]
        junk = CV.get([128, D], BF16)
        fg = CV.get([128, D]); ssq = CV.get([128, 20])
        P.dma("ld", fg, bass.AP(fng.tensor, 0, [[0, 128], [1, D]]), w=["fg"])
        alltiles = [(x2[i * 128:(i + 1) * 128, :], yp[i * 128:(i + 1) * 128, :], 128) for i in range(16)] + [(x2[2048:2048 + NS, :], ys, NS)]
        for ti, (src, dst, rows) in enumerate(alltiles):
            b = ti % 2
            c = ti % 20
            P.dma("xl%d" % b, xt[b][0:rows, :], src, r=["xscratch"], w=["xt%d" % b])
            P.act(lambda e, b=b, rows=rows, c=c: e.activation(out=junk[0:rows, :], in_=xt[b][0:rows, :], func=AF.Square,
                                                              accum_out=ssq[0:rows, c:c + 1]), r=["xt%d" % b], w=["junk", "fs%d" % c])
            P.act(lambda e, rows=rows, c=c: e.activation(out=ssq[0:rows, c:c + 1], in_=ssq[0:rows, c:c + 1], func=AF.Sqrt, scale=1.0 / D,
                                                         bias=epsb[0:rows, :]), r=["fs%d" % c, "epsb"], w=["fs%d" % c])
            P.dve(lambda e, rows=rows, c=c: e.reciprocal(out=ssq[0:rows, c:c + 1], in_=ssq[0:rows, c:c + 1]), r=["fs%d" % c], w=["fs%d" % c])
            P.dve(lambda e, b=b, rows=rows, c=c: e.scalar_tensor_tensor(out=yo[b][0:rows, :], in0=xt[b][0:rows, :], scalar=ssq[0:rows, c:c + 1],
                                                                        in1=fg[0:rows, :], op0=ALU.mult, op1=ALU.mult),
                  r=["xt%d" % b, "fs%d" % c, "fg"], w=["yo%d" % b])
            P.dma("yo%d" % b, dst, yo[b][0:rows, :], r=["yo%d" % b], phys="pool")

    def dbg_dump(name, ap, keys):
        if not DBGT:
            return
        shape = list(ap.shape)
        dt_ = ap.dtype
        d_ = nc.dram_tensor("dbg_" + name, shape, dt_, kind="ExternalOutput").ap()
        P.barrier()
        P.dma("st", d_, ap, r=keys)
        P.barrier()
        DBGN.append("dbg_" + name)

    def dump():
        dh = nc.dram_tensor("dbg_h", [128, 16 * 1040], BF16, kind="ExternalOutput").ap()
        dm = nc.dram_tensor("dbg_m", [128, 16 * 1040], BF16, kind="ExternalOutput").ap()
        P.barrier()
        P.dma("st", dh, hT[:].rearrange("p a b -> p (a b)"), r=["hT"])
        P.dma("st", dm, mT[:].rearrange("p a b -> p (a b)"), r=["mT", "mT3"])
        P.barrier()

    if STAGE >= 99:
        for l in range(2):
            layer_params(l)
            for h in range(2):
                layer_pass(l, h)
                if DUMP == (l, h):
                    dump()
        final_norm()
    else:
        layer_params(0)
        if STAGE >= 1:
            layer_pass(0, 0)
        if STAGE >= 7:
            layer_pass(0, 1)
        if STAGE >= 8:
            final_norm()
    P.emit()
    st.close()
    return nc


_CACHE = {}


def kernel(**inp):
    if "nc" not in _CACHE:
        _CACHE["nc"] = build_program()
    nc = _CACHE["nc"]
    f = lambda a: np.ascontiguousarray(np.asarray(a, dtype=np.float32))
    ident = np.eye(128, dtype=np.float32)
    triu = np.triu(np.ones((128, 128), np.float32))
    eye16 = np.eye(16, dtype=np.float32)
    hmask = np.zeros((128, 2), np.float32); hmask[:64, 0] = 1; hmask[64:, 1] = 1
    shared = dict(
        norm_g=f(inp["norm_g"]), w_in=f(inp["w_in"]), w_a2=f(inp["w_a2"]), b_a=f(inp["b_a"]), gla_g=f(inp["gla_g"]),
        sgu_g=f(inp["sgu_g"]), sgu_w=f(inp["sgu_w"]), sgu_b=f(inp["sgu_b"]), conv_w=f(inp["conv_w"]),
        lam_re=f(inp["ssm_lambda_re"]), lam_im=f(inp["ssm_lambda_im"]), log_dt=f(inp["ssm_log_dt"]),
        b_re=f(inp["ssm_b_re"]), b_im=f(inp["ssm_b_im"]), c_re=f(inp["ssm_c_re"]), c_im=f(inp["ssm_c_im"]),
        ssm_d=f(inp["ssm_d"]), glu_w=f(inp["glu_w"]), glu_b=f(inp["glu_b"]), w_out=f(inp["w_out"]), fng=f(inp["final_norm_g"]),
        ident=ident, triu=triu, eye16=eye16, hmask=hmask)
    xpr = f(inp["x_prompt"]); xsm = f(inp["x_sample"])[:, 0, :]
    sg = f(inp["state_gla"]); sc = f(inp["state_conv"]); sr = f(inp["state_ssm_re"]); si = f(inp["state_ssm_im"])
    in_maps = []
    for c in range(8):
        m = dict(shared)
        sl = slice(c * NS, (c + 1) * NS)
        m.update(xp=xpr[c % 4], xs=np.ascontiguousarray(xsm[sl]), sgla=np.ascontiguousarray(sg[:, sl]),
                 sconv=np.ascontiguousarray(sc[:, sl]), sre=np.ascontiguousarray(sr[:, sl]), sim=np.ascontiguousarray(si[:, sl]))
        in_maps.append(m)
    res = run_bass_kernel_spmd(nc, in_maps, core_ids=list(range(8))).results
    if DUMP is not None:
        _CACHE["dbg"] = (res[0]["dbg_h"], res[0]["dbg_m"])
    for n_ in DBGN:
        _CACHE[n_] = np.asarray(res[0][n_])
    cat = lambda k, ax: np.concatenate([res[c][k] for c in range(8)], axis=ax)
    stk = lambda k: np.stack([res[c][k] for c in range(4)], axis=1)
    y_prompt = np.stack([res[c]["yp"] for c in range(4)], axis=0)
    y_sample = cat("ys", 0)[:, None, :]
    return (y_prompt.astype(np.float32), y_sample.astype(np.float32),
            stk("gla_p"), cat("gla_s", 1), stk("conv_p"), cat("conv_s", 1),
            stk("sre_p"), stk("sim_p"), cat("sre_s", 1), cat("sim_s", 1), cat("vn_s", 1)[:, :, None, :])
```

```python
import types
import numpy as np
import concourse.bass as bass
import concourse.mybir as mybir
from concourse.bass_utils import run_bass_kernel_spmd

F32 = mybir.dt.float32
BF16 = mybir.dt.bfloat16
AF = mybir.ActivationFunctionType
ALU = mybir.AluOpType

D = 2048
PT = 6160
NS = 16
W = 512
EPS = 1e-6
DEBUG_EMIT = False
SAME_SKIP = ("pe",)
STAGE = 99
SUB = 99
NOSAMP = False
DUMP = None
GLABAR = 0
PSSHIFT = 0
DBGT = False
DBGN = []
ERRS = {}
SEG = dict(q=0, k=256, v=512, a=1024, ga=1040, ub=1552, vb=2064, gb=2576,
           cb=3088, cc=3600, hc=4112, gc=4624, ud=5136, gd=5648)


class Prog:
    def __init__(self, nc):
        self.nc = nc
        self.ops = {e: [] for e in ("sync", "pool", "act", "dve", "pe")}
        self.cnt = {}
        self.lastw = {}
        self.readers = {}
        self.vq_phys = {}
        self.floor = {}

    @staticmethod
    def _freeze(fn):
        if fn.__closure__ is None:
            return fn
        cells = []
        for c in fn.__closure__:
            try:
                cells.append(types.CellType(c.cell_contents))
            except ValueError:
                cells.append(c)
        return types.FunctionType(fn.__code__, fn.__globals__, fn.__name__, fn.__defaults__, tuple(cells))

    def _rec(self, phys, q, fn, r, w, inc):
        fn = self._freeze(fn)
        waits = dict(self.floor.get(phys, {}))

        def need(tok):
            s, v = tok
            if s == q and phys in SAME_SKIP:
                return
            if v > waits.get(s, 0):
                waits[s] = v
        for k in r:
            if k in self.lastw:
                need(self.lastw[k])
        for k in w:
            if k in self.lastw:
                need(self.lastw[k])
            for t in self.readers.get(k, ()):
                need(t)
        idx = self.cnt.get(q, 0) + 1
        self.cnt[q] = idx
        if inc == 16 and idx > 1:
            if 16 * (idx - 1) > waits.get(q, 0):
                waits[q] = 16 * (idx - 1)
        self.vq_phys[q] = phys
        self.ops[phys].append((fn, waits, q, inc))
        tok = (q, idx * inc)
        for k in r:
            self.readers.setdefault(k, []).append(tok)
        for k in w:
            self.lastw[k] = tok
            self.readers[k] = []

    def dma(self, q, out, in_, r=(), w=(), phys="sync", **kw):
        if q in ("ld", "st"):
            self.rr = getattr(self, "rr", 0) + 1
            q = "%s%d" % (q, self.rr % 6)
        self._rec(phys, q, lambda e: e.dma_start(out=out, in_=in_, **kw), r, w, 16)

    def act(self, fn, r=(), w=()):
        self._rec("act", "act", fn, r, w, 1)

    def dve(self, fn, r=(), w=()):
        self._rec("dve", "dve", fn, r, w, 1)

    def pe(self, fn, r=(), w=()):
        self._rec("pe", "pe", fn, r, w, 1)

    def barrier(self):
        snap = {q: c * (16 if q not in ("act", "dve", "pe") else 1) for q, c in self.cnt.items()}
        for p in self.ops:
            if p == "pool":
                continue
            self.floor[p] = dict(snap)

    def emit(self):
        nc = self.nc
        qs = list(self.cnt.keys())
        sems = {}
        import contextlib
        with contextlib.ExitStack() as st:
            for q in qs:
                sems[q] = st.enter_context(nc.semaphore("s_" + q))
            block = st.enter_context(nc.Block())
            final = {q: c * (16 if q not in ("act", "dve", "pe") else 1) for q, c in self.cnt.items()}

            def run(phys, eng, last=False):
                have = {}
                for fn, waits, q, inc in self.ops[phys]:
                    for s, v in waits.items():
                        if have.get(s, 0) < v:
                            eng.wait_ge(sems[s], v)
                            have[s] = v
                    if DEBUG_EMIT:
                        try:
                            fn(eng).then_inc(sems[q], inc)
                        except Exception as ex:
                            msg = str(ex)[:300]
                            if msg not in ERRS:
                                ERRS[msg] = (phys, q)
                                print("EMIT-ERR", phys, q, msg, flush=True)
                        continue
                    fn(eng).then_inc(sems[q], inc)
                if last:
                    for q, v in final.items():
                        eng.wait_ge(sems[q], v)

            @block.sync
            def _(e):
                run("sync", e, last=True)

            @block.gpsimd
            def _(e):
                run("pool", e)

            @block.scalar
            def _(e):
                run("act", e)

            @block.vector
            def _(e):
                run("dve", e)

            @block.tensor
            def _(e):
                run("pe", e)


def build_program():
    nc = bass.Bass("TRN2", target_bir_lowering=False)
    P = Prog(nc)

    def din(name, shape):
        return nc.dram_tensor(name, list(shape), F32, kind="ExternalInput").ap()

    def dout(name, shape):
        return nc.dram_tensor(name, list(shape), F32, kind="ExternalOutput").ap()

    xp = din("xp", [2048, D]); xs = din("xs", [NS, D])
    sgla = din("sgla", [2, NS, 4, 64, 128]); sconv = din("sconv", [2, NS, 2, W])
    sre = din("sre", [2, NS, 32, 64]); sim = din("sim", [2, NS, 32, 64])
    norm_g = din("norm_g", [2, D]); w_in = din("w_in", [2, D, PT]); w_a2 = din("w_a2", [2, 16, 256])
    b_a = din("b_a", [2, 256]); gla_g = din("gla_g", [2, W]); sgu_g = din("sgu_g", [2, W])
    sgu_w = din("sgu_w", [2, 4, 128, 128]); sgu_b = din("sgu_b", [2, 4, 128]); conv_w = din("conv_w", [2, 3, W])
    lam_re = din("lam_re", [2, 32, 64]); lam_im = din("lam_im", [2, 32, 64]); log_dt = din("log_dt", [2, 32])
    b_re = din("b_re", [2, 32, 64, 16]); b_im = din("b_im", [2, 32, 64, 16])
    c_re = din("c_re", [2, 32, 16, 64]); c_im = din("c_im", [2, 32, 16, 64])
    ssm_d = din("ssm_d", [2, W]); glu_w = din("glu_w", [2, W, W]); glu_b = din("glu_b", [2, W])
    w_out = din("w_out", [2, D, D]); fng = din("fng", [D])
    ident_d = din("ident", [128, 128]); triu_d = din("triu", [128, 128]); eye16_d = din("eye16", [16, 16])
    hmask_d = din("hmask", [128, 2])

    yp = dout("yp", [2048, D]); ys = dout("ys", [NS, D])
    o_gla_p = dout("gla_p", [2, 4, 64, 128]); o_gla_s = dout("gla_s", [2, NS, 4, 64, 128])
    o_conv_p = dout("conv_p", [2, 2, W]); o_conv_s = dout("conv_s", [2, NS, 2, W])
    o_sre_p = dout("sre_p", [2, 32, 64]); o_sim_p = dout("sim_p", [2, 32, 64])
    o_sre_s = dout("sre_s", [2, NS, 32, 64]); o_sim_s = dout("sim_s", [2, NS, 32, 64])
    o_vn_s = dout("vn_s", [2, NS, W])
    x1 = nc.dram_tensor("x1s", [2048 + NS, D], F32, kind="Internal").ap()
    x2 = nc.dram_tensor("x2s", [2048 + NS, D], F32, kind="Internal").ap()

    import contextlib
    st = contextlib.ExitStack()

    def sb(name, shape, dt=F32):
        return st.enter_context(nc.sbuf_tensor("sb_" + name, list(shape), dt))

    def ps(name, shape, dt=F32):
        return st.enter_context(nc.psum_tensor(name, list(shape), dt))

    hT = sb("hT", [128, 16, 1040], BF16)
    mT = sb("mT", [128, 16, 1040], BF16)
    wbuf = sb("wbuf", [128, 2, 16, 256], BF16)
    ident = sb("ident", [128, 128]); identb = sb("identb", [128, 128], BF16)
    triu = sb("triu", [128, 128]); eye16 = sb("eye16", [16, 16]); hmask = sb("hmask", [128, 2])
    ones1 = sb("ones1", [1, 128], BF16)
    epsb = sb("epsb", [128, 1])
    ng = sb("ng", [128, 2, 16])
    wa2 = sb("wa2", [16, 256], BF16); wa2f = sb("wa2f", [16, 256])
    bafm = sb("bafm", [128, 2]); barow = sb("barow", [1, 256], BF16); barowf = sb("barowf", [1, 256])
    glag = sb("glag", [128, 4]); sgug = sb("sgug", [128, W]); sgub = sb("sgub", [128, 4, 128])
    sguw = sb("sguw", [128, 4, 128], BF16); w00 = sb("w00", [128, 4]); b00 = sb("b00", [128, 4])
    cw = sb("cw", [128, 3, 4]); dsk = sb("dsk", [128, 4]); glub = sb("glub", [128, 4])
    gluw = sb("gluw", [128, 4, W], BF16)
    S32 = sb("S32", [128, 2, 128]); Sbf = sb("Sbf", [128, 2, 128], BF16)
    zcar = sb("zcar", [128, 4, 2])
    nbafm = sb("nbafm", [128, 2])
    Xc = sb("Xc", [128, 2, 16])
    BL = sb("BL", [128, 16, 2, 128], BF16)
    CL = sb("CL", [128, 16, 2, 128], BF16)
    TC = sb("TC", [128, 16, 128]); TS = sb("TS", [128, 16, 128]); MAG = sb("MAG", [128, 16])
    AR = sb("AR", [128, 16]); AI = sb("AI", [128, 16])
    ARENA = 78 * 1024
    arena = sb("arena", [128, ARENA // 4])
    psb = [ps("psb%d" % i, [128, 512]) for i in range(8)]

    class Carver:
        def __init__(self):
            self.off = 0
            self.limit = ARENA

        def reset(self, base=0, limit=None):
            self.off = base
            self.limit = ARENA if limit is None else limit

        def get(self, shape, dt=F32, rows=128):
            n = int(np.prod(shape[1:]))
            nb = n * (4 if dt == F32 else 2)
            nb = (nb + 63) // 64 * 64
            assert self.off + nb <= self.limit, ("arena overflow", self.off, nb, self.limit)
            v = arena[0:shape[0], self.off // 4:(self.off + nb) // 4]
            if dt != F32:
                v = v.bitcast(dt)
            v = v[:, 0:n]
            self.off += nb
            if len(shape) == 3:
                v = v.rearrange("p (a b) -> p a b", a=shape[1])
            elif len(shape) == 4:
                v = v.rearrange("p (a b c) -> p a b c", a=shape[1], b=shape[2])
            return v

    MAINLIM = 52 * 1024
    CVS = Carver()
    TICK = [None]

    def tick():
        gen = TICK[0]
        if gen is not None:
            try:
                next(gen)
            except StopIteration:
                TICK[0] = None

    CV = Carver()
    uid = [0]

    def K(s="t"):
        uid[0] += 1
        return "%s%d" % (s, uid[0])

    def bc(ap, shape):
        return ap.to_broadcast(list(shape))

    P.dma("ld", ident[:], ident_d, w=["ident"])
    P.dma("ld", triu[:], triu_d, w=["triu"])
    P.dma("ld", eye16[:], eye16_d, w=["eye16"])
    P.dma("ld", hmask[:], hmask_d, w=["hmask"])
    P.dma("ld", ng[:], norm_g.rearrange("l (k p) -> p l k", p=128), w=["ng"], allow_slow_non_contiguous=True)
    P.dve(lambda e: e.tensor_copy(out=identb[:], in_=ident[:]), r=["ident"], w=["identb"])
    P.dve(lambda e: e.memset(ones1[:], 1.0), w=["ones1"])
    P.dve(lambda e: e.memset(epsb[:], EPS), w=["epsb"])

    wstate = {"n": 0}

    def load_w(src_cols_ap):
        i = wstate["n"] % 2
        wstate["n"] += 1
        ncols = src_cols_ap.shape[1]
        key = "wbuf%d" % i
        P.dma("wq%d" % i, wbuf[:, i, :, 0:ncols], src_cols_ap.rearrange("(k p) c -> p k c", p=128),
              w=[key], phys="pool")
        return wbuf[:, i, :, 0:ncols], key

    def token_blocks(NT):
        out = []
        t = 0
        while t < NT:
            n = min(512, NT - t)
            out.append((t, n))
            t += n
        return out

    psrr = {"i": 0}

    def next_ps(lo=0, hi=6):
        i = lo + psrr["i"] % (hi - lo)
        psrr["i"] += 1
        return psb[i], "ps%d" % i

    def proj_fm(l, c0, ncols, NT, evac, rows=128):
        for b0 in range(0, ncols, 256):
            nb = min(256, ncols - b0)
            wv, wk = load_w(w_in[l, :, c0 + b0:c0 + b0 + nb])
            for f0 in range(0, nb, 128):
                m = min(128, nb - f0)
                for (t0, n) in token_blocks(NT):
                    pt, pk = next_ps()
                    for k in range(16):
                        P.pe(lambda e, k=k, pt=pt, wv=wv, f0=f0, m=m, t0=t0, n=n: e.matmul(
                            pt[0:m, 0:n], lhsT=wv[:, k, f0:f0 + m], rhs=hT[:, k, t0:t0 + n],
                            start=(k == 0), stop=(k == 15)), r=[wk, "hT"], w=[pk])
                    evac((b0 + f0) // 128, t0, n, pt[0:m, 0:n], pk)
                    tick()

    def proj_tm(l, c0, ncols, tiles, evac):
        blocks = []
        for b0 in range(0, ncols, 256):
            nb = min(256, ncols - b0)
            blocks.append((b0, nb) + load_w(w_in[l, :, c0 + b0:c0 + b0 + nb]))
        assert len(blocks) <= 2
        for ti, (t0, rows) in enumerate(tiles):
            pt, pk = next_ps()
            for (b0, nb, wv, wk) in blocks:
                for k in range(16):
                    P.pe(lambda e, k=k, pt=pt, wv=wv, b0=b0, nb=nb, t0=t0, rows=rows: e.matmul(
                        pt[0:rows, b0:b0 + nb], lhsT=hT[:, k, t0:t0 + rows], rhs=wv[:, k, :],
                        start=(k == 0), stop=(k == 15)), r=[wk, "hT"], w=[pk])
            evac(ti, rows, pt[0:rows, 0:ncols], pk)
            tick()

    def transpose_to(dst_fn, src_ap, rows, ncol_tiles, dt, skey, evac):
        pt, pk = next_ps(6, 8)
        idn = identb if dt == BF16 else ident
        pv = pt[:].bitcast(BF16)[:, 0:ncol_tiles * 128] if dt == BF16 else pt[:, 0:ncol_tiles * 128]
        pv = pv.rearrange("p (a b) -> p a b", a=ncol_tiles)
        for j in range(ncol_tiles):
            P.pe(lambda e, j=j, pv=pv: e.transpose(pv[:, j, 0:rows], src_ap[0:rows, j * 128:(j + 1) * 128],
                                                   idn[0:rows, 0:rows]),
                 r=[skey, "ident", "identb"], w=[pk])
        evac(pv[:, :, 0:rows], pk)

    def layer_params(l):
        P.barrier()
        k = "lp"
        P.dma("ld", wa2f[:], w_a2[l], w=["wa2f"])
        P.dve(lambda e: e.tensor_copy(out=wa2[:], in_=wa2f[:]), r=["wa2f"], w=[k])
        P.dma("ld", barowf[:], b_a[l:l + 1, :], w=["barowf"])
        P.dve(lambda e: e.tensor_copy(out=barow[:], in_=barowf[:]), r=["barowf"], w=[k])
        P.dma("ld", bafm[:], b_a[l].rearrange("(a p) -> p a", p=128), w=[k], allow_slow_non_contiguous=True)
        P.dma("ld", glag[:], gla_g[l].rearrange("(a p) -> p a", p=128), w=[k], allow_slow_non_contiguous=True)
        P.dma("ld", sgug[:], bass.AP(sgu_g.tensor, l * W, [[0, 128], [1, W]]), w=[k])
        P.dma("ld", sgub[:], bass.AP(sgu_b.tensor, l * 512, [[0, 128], [128, 4], [1, 128]]), w=[k])
        P.dma("ld", w00[:], bass.AP(sgu_w.tensor, l * 4 * 16384, [[0, 128], [16384, 4]]), w=[k],
              allow_slow_non_contiguous=True)
        P.dma("ld", b00[:], bass.AP(sgu_b.tensor, l * 512, [[0, 128], [128, 4]]), w=[k],
              allow_slow_non_contiguous=True)
        P.dma("ld", cw[:], conv_w[l].rearrange("j (a p) -> p j a", p=128), w=[k], allow_slow_non_contiguous=True)
        P.dma("ld", dsk[:], ssm_d[l].rearrange("(a p) -> p a", p=128), w=[k], allow_slow_non_contiguous=True)
        P.dma("ld", glub[:], glu_b[l].rearrange("(a p) -> p a", p=128), w=[k], allow_slow_non_contiguous=True)
        P.dma("wq0", gluw[:], glu_w[l].rearrange("(a p) c -> p a c", p=128), w=[k], phys="pool")
        CV.reset()
        swn = CV.get([128, 4, 128])
        P.dma("ld", swn, sgu_w[l].rearrange("h t s -> t h s"), w=["swn"])
        for h in range(4):
            pt, pk = next_ps(6, 8)
            P.pe(lambda e, h=h, pt=pt: e.transpose(pt[:, 0:128], swn[:, h, :], ident[:]), r=["swn", "ident"], w=[pk])
            P.dve(lambda e, h=h, pt=pt: e.tensor_tensor(out=sguw[:, h, :], in0=pt[:, 0:128], in1=triu[:], op=ALU.mult),
                  r=[pk, "triu"], w=[k])
        P.dve(lambda e: e.memset(S32[:], 0.0), w=["S32"])
        P.dve(lambda e: e.memset(Sbf[:], 0.0), w=["Sbf"])
        P.dve(lambda e: e.memset(zcar[:], 0.0), w=["zcar"])
        P.dve(lambda e: e.tensor_scalar(out=nbafm[:], in0=bafm[:], scalar1=-1.0, scalar2=None, op0=ALU.mult), r=[k], w=[k])
        P.dve(lambda e: e.memset(Xc[:], 0.0), w=["Xc"])
        s5_params(l)
        P.barrier()

    def s5_params(l):
        kk = "s5p"
        g = CV.get
        LR = g([128, 16]); LI = g([128, 16]); DT = g([128, 16])
        for hh in range(2):
            rows = slice(hh * 64, hh * 64 + 64)
            P.dma("ld", LR[rows, :], bass.AP(lam_re.tensor, l * 2048 + hh * 64, [[1, 64], [128, 16]]), w=["LR"],
                  allow_slow_non_contiguous=True)
            P.dma("ld", LI[rows, :], bass.AP(lam_im.tensor, l * 2048 + hh * 64, [[1, 64], [128, 16]]), w=["LI"],
                  allow_slow_non_contiguous=True)
            P.dma("ld", DT[rows, :], bass.AP(log_dt.tensor, l * 32 + hh, [[0, 64], [2, 16]]), w=["DT"],
                  allow_slow_non_contiguous=True)
        P.act(lambda e: e.activation(out=DT, in_=DT, func=AF.Exp), r=["DT"], w=["DT"])
        lrd = g([128, 16]); th = g([128, 16]); mag = g([128, 16]); c = g([128, 16]); s = g([128, 16])
        t1 = g([128, 16]); t2 = g([128, 16]); hp = g([128, 1])
        P.dve(lambda e: e.memset(hp, float(np.pi / 2)), w=["hp"])
        P.dve(lambda e: e.tensor_tensor(out=lrd, in0=LR, in1=DT, op=ALU.mult), r=["LR", "DT"], w=["lrd"])
        P.dve(lambda e: e.tensor_tensor(out=th, in0=LI, in1=DT, op=ALU.mult), r=["LI", "DT"], w=["th"])
        P.act(lambda e: e.activation(out=mag, in_=lrd, func=AF.Exp), r=["lrd"], w=["mag"])
        P.act(lambda e: e.activation(out=s, in_=th, func=AF.Sin, scale=1.0 / 16), r=["th"], w=["cs"])
        P.act(lambda e: e.activation(out=c, in_=th, func=AF.Sin, scale=1.0 / 16, bias=hp), r=["th", "hp", "cs"], w=["cs"])

        def csq(cr, ci, key):
            P.dve(lambda e: e.tensor_tensor(out=t1, in0=cr, in1=ci, op=ALU.mult), r=[key], w=["t1"])
            P.dve(lambda e: e.tensor_tensor(out=t2, in0=ci, in1=ci, op=ALU.mult), r=[key], w=["t2"])
            P.dve(lambda e: e.tensor_tensor(out=cr, in0=cr, in1=cr, op=ALU.mult), r=[key], w=[key])
            P.dve(lambda e: e.tensor_tensor(out=cr, in0=cr, in1=t2, op=ALU.subtract), r=[key, "t2"], w=[key])
            P.dve(lambda e: e.tensor_scalar(out=ci, in0=t1, scalar1=2.0, scalar2=None, op0=ALU.mult), r=["t1"], w=[key])
        for _ in range(4):
            csq(c, s, "cs")
        P.dve(lambda e: e.tensor_tensor(out=AR[:], in0=mag, in1=c, op=ALU.mult), r=["mag", "cs"], w=["AR"])
        P.dve(lambda e: e.tensor_tensor(out=AI[:], in0=mag, in1=s, op=ALU.mult), r=["mag", "cs"], w=["AI"])
        P.dve(lambda e: e.tensor_copy(out=TC[:, :, 0], in_=c), r=["cs"], w=["TC"])
        P.dve(lambda e: e.tensor_copy(out=TS[:, :, 0], in_=s), r=["cs"], w=["TS"])
        n = 1
        tA = g([128, 16, 64]); tB = g([128, 16, 64])
        while n < 128:
            cn = bc(TC[:, :, n - 1:n], [128, 16, n]); sn = bc(TS[:, :, n - 1:n], [128, 16, n])
            a = tA[:, :, 0:n]; b = tB[:, :, 0:n]
            P.dve(lambda e, a=a, cn=cn, n=n: e.tensor_tensor(out=a, in0=TC[:, :, 0:n], in1=cn, op=ALU.mult), r=["TC"], w=["tA"])
            P.dve(lambda e, b=b, sn=sn, n=n: e.tensor_tensor(out=b, in0=TS[:, :, 0:n], in1=sn, op=ALU.mult), r=["TS"], w=["tB"])
            P.dve(lambda e, a=a, b=b, n=n: e.tensor_tensor(out=TC[:, :, n:2 * n], in0=a, in1=b, op=ALU.subtract), r=["tA", "tB"], w=["TC"])
            P.dve(lambda e, a=a, sn=sn, n=n: e.tensor_tensor(out=a, in0=TC[:, :, 0:n], in1=sn, op=ALU.mult), r=["TC", "TS"], w=["tA"])
            P.dve(lambda e, b=b, cn=cn, n=n: e.tensor_tensor(out=b, in0=TS[:, :, 0:n], in1=cn, op=ALU.mult), r=["TS", "TC"], w=["tB"])
            P.dve(lambda e, a=a, b=b, n=n: e.tensor_tensor(out=TS[:, :, n:2 * n], in0=a, in1=b, op=ALU.add), r=["tA", "tB"], w=["TS"])
            n *= 2
        P.dve(lambda e: e.tensor_copy(out=MAG[:], in_=mag), r=["mag"], w=["MAG"])
        den = g([128, 16]); cr_ = g([128, 16]); ci_ = g([128, 16]); am1 = g([128, 16])
        P.dve(lambda e: e.tensor_tensor(out=t1, in0=LR, in1=LR, op=ALU.mult), r=["LR"], w=["t1"])
        P.dve(lambda e: e.tensor_tensor(out=t2, in0=LI, in1=LI, op=ALU.mult), r=["LI"], w=["t2"])
        P.dve(lambda e: e.tensor_tensor(out=den, in0=t1, in1=t2, op=ALU.add), r=["t1", "t2"], w=["den"])
        P.dve(lambda e: e.reciprocal(out=den, in_=den), r=["den"], w=["den"])
        P.dve(lambda e: e.tensor_scalar(out=am1, in0=AR[:], scalar1=-1.0, scalar2=None, op0=ALU.add), r=["AR"], w=["am1"])
        P.dve(lambda e: e.tensor_tensor(out=t1, in0=am1, in1=LR, op=ALU.mult), r=["am1", "LR"], w=["t1"])
        P.dve(lambda e: e.tensor_tensor(out=t2, in0=AI[:], in1=LI, op=ALU.mult), r=["AI", "LI"], w=["t2"])
        P.dve(lambda e: e.tensor_tensor(out=cr_, in0=t1, in1=t2, op=ALU.add), r=["t1", "t2"], w=["cr"])
        P.dve(lambda e: e.tensor_tensor(out=cr_, in0=cr_, in1=den, op=ALU.mult), r=["cr", "den"], w=["cr"])
        P.dve(lambda e: e.tensor_tensor(out=t1, in0=AI[:], in1=LR, op=ALU.mult), r=["AI", "LR"], w=["t1"])
        P.dve(lambda e: e.tensor_tensor(out=t2, in0=am1, in1=LI, op=ALU.mult), r=["am1", "LI"], w=["t2"])
        P.dve(lambda e: e.tensor_tensor(out=ci_, in0=t1, in1=t2, op=ALU.subtract), r=["t1", "t2"], w=["ci"])
        P.dve(lambda e: e.tensor_tensor(out=ci_, in0=ci_, in1=den, op=ALU.mult), r=["ci", "den"], w=["ci"])
        BMr = g([128, 16, 32]); BMi = g([128, 16, 32]); BBr = g([128, 16, 32]); BBi = g([128, 16, 32])
        u1 = g([128, 16, 32]); u2 = g([128, 16, 32])
        P.dve(lambda e: e.memset(BMr, 0.0), w=["BMr"])
        P.dve(lambda e: e.memset(BMi, 0.0), w=["BMi"])
        for hh in range(2):
            rows = slice(hh * 64, hh * 64 + 64)
            for (dst, src, key) in ((BMr, b_re, "BMr"), (BMi, b_im, "BMi")):
                P.dma("ld", dst[rows, :, hh * 16:hh * 16 + 16],
                      bass.AP(src.tensor, l * 32768 + hh * 1024, [[16, 64], [2048, 16], [1, 16]]), w=[key])
        crb = bc(cr_.unsqueeze(2), [128, 16, 32]); cib = bc(ci_.unsqueeze(2), [128, 16, 32])
        P.dve(lambda e: e.tensor_tensor(out=u1, in0=BMr, in1=crb, op=ALU.mult), r=["BMr", "cr"], w=["u1"])
        P.dve(lambda e: e.tensor_tensor(out=u2, in0=BMi, in1=cib, op=ALU.mult), r=["BMi", "ci"], w=["u2"])
        P.dve(lambda e: e.tensor_tensor(out=BBr, in0=u1, in1=u2, op=ALU.subtract), r=["u1", "u2"], w=["BBr"])
        P.dve(lambda e: e.tensor_tensor(out=u1, in0=BMi, in1=crb, op=ALU.mult), r=["BMi", "cr", "BBr"], w=["u1"])
        P.dve(lambda e: e.tensor_tensor(out=u2, in0=BMr, in1=cib, op=ALU.mult), r=["BMr", "ci", "BBr"], w=["u2"])
        P.dve(lambda e: e.tensor_tensor(out=BBi, in0=u1, in1=u2, op=ALU.add), r=["u1", "u2"], w=["BBi"])
        P.dve(lambda e: e.memset(BL[:], 0.0), w=["BL"])
        for kt in range(4):
            for ci, src in enumerate((BBr, BBi)):
                pt, pk = next_ps(6, 8)
                P.pe(lambda e, pt=pt, src=src, kt=kt: e.transpose(pt[:, 0:128], src[:, 4 * kt:4 * kt + 4, :].rearrange("p a b -> p (a b)"), ident[:]),
                     r=["BBr", "BBi", "ident"], w=[pk])
                for j in range(4):
                    P.dve(lambda e, pt=pt, kt=kt, ci=ci, j=j: e.tensor_copy(
                        out=BL[32 * j:32 * j + 32, 4 * kt + j, ci, :], in_=pt[32 * j:32 * j + 32, 0:128]), r=[pk], w=["BL"])
        CMr = g([128, 16, 32]); CMi = g([128, 16, 32]); cnP = [g([128, 128]) for _ in range(2)]
        P.dve(lambda e: e.memset(CMr, 0.0), w=["CMr"])
        P.dve(lambda e: e.memset(CMi, 0.0), w=["CMi"])
        for (dst, src, key) in ((CMr, c_re, "CMr"), (CMi, c_im, "CMi")):
            for kt in range(4):
                cn_ = cnP[kt % 2]; cnk = "cn%d" % (kt % 2)
                for dd in range(2):
                    P.dma("ld", cn_[:, dd * 64:(dd + 1) * 64], src[l].rearrange("g p n -> (g p) n")[kt * 128:(kt + 1) * 128, :], w=[cnk])
                pt, pk = next_ps(6, 8)
                P.pe(lambda e, pt=pt: e.transpose(pt[:, 0:128], cn_, ident[:]), r=[cnk, "ident"], w=[pk])
                pv = pt[:, 0:128].rearrange("q (j a p) -> q j a p", j=4, a=2)
                for hh in range(2):
                    rows = slice(hh * 64, hh * 64 + 64)
                    P.dve(lambda e, dst=dst, pv=pv, rows=rows, hh=hh, kt=kt: e.tensor_copy(
                        out=dst[rows, 4 * kt:4 * kt + 4, hh * 16:hh * 16 + 16], in_=pv[rows, :, hh, :]), r=[pk], w=[key])
        P.dve(lambda e: e.memset(CL[:], 0.0), w=["CL"])
        for gp in range(16):
            j = gp % 4
            P.dve(lambda e, gp=gp, j=j: e.tensor_copy(out=CL[:, gp, 0, 32 * j:32 * j + 32], in_=CMr[:, gp, :]), r=["CMr"], w=["CL"])
            P.dve(lambda e, gp=gp, j=j: e.tensor_scalar(out=CL[:, gp, 1, 32 * j:32 * j + 32], in0=CMi[:, gp, :], scalar1=-1.0, scalar2=None,
                                                        op0=ALU.mult), r=["CMi"], w=["CL"])

        if l == 0:
            dbg_dump("AR", AR[:], ["AR"]); dbg_dump("AI", AI[:], ["AI"]); dbg_dump("MAG", MAG[:], ["MAG"])
            dbg_dump("TC", TC[:], ["TC"]); dbg_dump("TS", TS[:], ["TS"])
            dbg_dump("BL", BL[:], ["BL"]); dbg_dump("CL", CL[:], ["CL"])
            dbg_dump("cr", cr_, ["cr"]); dbg_dump("ci", ci_, ["ci"]); dbg_dump("BBr", BBr, ["BBr"]); dbg_dump("CMr", CMr, ["CMr"])

    def layer_pass(l, h):
        NT = 1024 + (NS if h == 1 else 0)
        tiles = [(i * 128, 128) for i in range(8)] + ([(1024, NS)] if h == 1 else [])
        xin = (xp, xs) if l == 0 else (x1[0:2048, :], x1[2048:2048 + NS, :])
        xout = (x1[0:2048, :], x1[2048:2048 + NS, :]) if l == 0 else (x2[0:2048, :], x2[2048:2048 + NS, :])

        def xrows(pair, ti, rows):
            return pair[0][h * 1024 + ti * 128:h * 1024 + ti * 128 + rows, :] if rows == 128 else pair[1]

        P.barrier()
        CV.reset()
        xt = [CV.get([128, D]) for _ in range(2)]
        xn = [CV.get([128, D], BF16) for _ in range(2)]
        junk = CV.get([128, D], BF16)
        ssq = CV.get([128, 20])
        for ti, (t0, rows) in enumerate(tiles):
            b = ti % 2
            P.dma("xl%d" % b, xt[b][0:rows, :], xrows(xin, ti, rows), w=["xt%d" % b])
            P.act(lambda e, b=b, rows=rows, ti=ti: e.activation(out=junk[0:rows, :], in_=xt[b][0:rows, :], func=AF.Square,
                                                                accum_out=ssq[0:rows, ti:ti + 1]),
                  r=["xt%d" % b], w=["junk", "ssq%d" % ti])
            P.act(lambda e, rows=rows, ti=ti: e.activation(out=ssq[0:rows, ti:ti + 1], in_=ssq[0:rows, ti:ti + 1], func=AF.Sqrt,
                                                           scale=1.0 / D, bias=epsb[0:rows, :]),
                  r=["ssq%d" % ti, "epsb"], w=["ssq%d" % ti])
            P.dve(lambda e, rows=rows, ti=ti: e.reciprocal(out=ssq[0:rows, ti:ti + 1], in_=ssq[0:rows, ti:ti + 1]),
                  r=["ssq%d" % ti], w=["ssq%d" % ti])
            P.act(lambda e, b=b, rows=rows, ti=ti: e.activation(out=xn[b][0:rows, :], in_=xt[b][0:rows, :], func=AF.Copy,
                                                                scale=ssq[0:rows, ti:ti + 1]),
                  r=["xt%d" % b, "ssq%d" % ti], w=["xn%d" % b])
            for k4 in range(4):
                def ev(pv, pk, k4=k4, t0=t0, rows=rows):
                    P.dve(lambda e: e.tensor_tensor(out=hT[:, 4 * k4:4 * k4 + 4, t0:t0 + rows], in0=pv,
                                                    in1=bc(ng[:, l, 4 * k4:4 * k4 + 4].unsqueeze(2), [128, 4, rows]), op=ALU.mult),
                          r=[pk, "ng"], w=["hT"])
                transpose_to(None, xn[b][:, k4 * 512:(k4 + 1) * 512], rows, 4, BF16, "xn%d" % b, ev)

        if (SUB if h == 1 else STAGE) < 2:
            return
        P.barrier()
        CV.reset()
        qT = CV.get([128, 2, 1040]); kT = CV.get([128, 2, 1040])
        vtok = CV.get([128, 9, W], BF16)
        sga = CV.get([128, 4, 1040], BF16)
        alrT = CV.get([16, 1040], BF16)

        def ev_q(ft, t0, n, pv, pk):
            P.act(lambda e: e.activation(out=qT[:, ft, t0:t0 + n], in_=pv, func=AF.Copy), r=[pk], w=["qT"])
        proj_fm(l, SEG["q"], 256, NT, ev_q)

        def ev_k(ft, t0, n, pv, pk):
            P.act(lambda e: e.activation(out=kT[:, ft, t0:t0 + n], in_=pv, func=AF.Copy), r=[pk], w=["kT"])
        proj_fm(l, SEG["k"], 256, NT, ev_k)

        def ev_a(ft, t0, n, pv, pk):
            P.act(lambda e: e.activation(out=alrT[:, t0:t0 + n], in_=pv, func=AF.Copy), r=[pk], w=["alrT"])
        proj_fm(l, SEG["a"], 16, NT, ev_a)

        def ev_ga(ft, t0, n, pv, pk):
            P.act(lambda e: e.activation(out=sga[:, ft, t0:t0 + n], in_=pv, func=AF.Silu), r=[pk], w=["sga"])
        proj_fm(l, SEG["ga"], 512, NT, ev_ga)

        def ev_v(ti, rows, pv, pk):
            P.act(lambda e: e.activation(out=vtok[0:rows, ti, :], in_=pv, func=AF.Copy), r=[pk], w=["vtok"])
        proj_tm(l, SEG["v"], 512, tiles, ev_v)

        sp = CV.get([128, 256]); eb = CV.get([128, 2, 128]); enb = CV.get([128, 2, 128])
        qmP = [[CV.get([128, 2, 128], BF16) for _ in range(2)] for _ in range(2)]
        kmP = [[CV.get([128, 2, 128], BF16) for _ in range(2)] for _ in range(2)]
        qfP = [CV.get([128, 2, 128], BF16) for _ in range(2)]
        kf = CV.get([128, 2, 128], BF16)
        ktokP = [CV.get([128, 256], BF16) for _ in range(2)]
        eblP = [CV.get([128, 2]) for _ in range(2)]
        scm = CV.get([128, 128], BF16)
        on = CV.get([128, W], BF16)
        osq = CV.get([128, 8]); tmpS = CV.get([128, 128])
        ojunk = CV.get([128, 128])

        def gla_out(rows, t0, opsum, opk):
            for hd in range(4):
                P.act(lambda e, hd=hd: e.activation(out=ojunk[0:rows, :], in_=opsum[0:rows, hd * 128:(hd + 1) * 128],
                                                    func=AF.Square, accum_out=osq[0:rows, hd:hd + 1]),
                      r=[opk], w=["ojunk", "osq"])
            P.act(lambda e: e.activation(out=osq[0:rows, 0:4], in_=osq[0:rows, 0:4], func=AF.Sqrt, scale=1.0 / 128,
                                         bias=epsb[0:rows, :]), r=["osq", "epsb"], w=["osq"])
            P.dve(lambda e: e.reciprocal(out=osq[0:rows, 0:4], in_=osq[0:rows, 0:4]), r=["osq"], w=["osq"])
            P.dve(lambda e: e.tensor_tensor(out=on[0:rows, :].rearrange("p (a b) -> p a b", a=4),
                                            in0=opsum[0:rows, :].rearrange("p (a b) -> p a b", a=4),
                                            in1=bc(osq[0:rows, 0:4].unsqueeze(2), [rows, 4, 128]), op=ALU.mult),
                  r=[opk, "osq"], w=["on"])

            def ev(pv, pk):
                for hd in range(4):
                    P.dve(lambda e, hd=hd: e.scalar_tensor_tensor(out=mT[:, hd, t0:t0 + rows], in0=pv[:, hd, :],
                                                                  scalar=glag[:, hd:hd + 1], in1=sga[:, hd, t0:t0 + rows],
                                                                  op0=ALU.mult, op1=ALU.mult),
                          r=[pk, "lp", "sga"], w=["mT"])
            if False:
                dbg_dump("on%d" % (t0 // 128), on, ["on"]); dbg_dump("osq%d" % (t0 // 128), osq, ["osq"])
            transpose_to(None, on, rows, 4, BF16, "on", ev)
            if False:
                dbg_dump("mt%d" % (t0 // 128), mT[:, 0:4, t0:t0 + 128], ["mT"])
                dbg_dump("sga%d" % (t0 // 128), sga[:, :, t0:t0 + 128], ["sga"])

        def stage1(ti):
            t0 = ti * 128
            pz = ti % 2
            qf = qfP[pz]; qm = qmP[pz]; km = kmP[pz]; ktok = ktokP[pz]; ebl = eblP[pz]
            zp, zk = next_ps()
            P.pe(lambda e: e.matmul(zp[:, 0:256], lhsT=alrT[:, t0:t0 + 128], rhs=wa2[:], start=True, stop=False),
                 r=["alrT", "lp"], w=[zk])
            P.pe(lambda e: e.matmul(zp[:, 0:256], lhsT=ones1[:], rhs=barow[:], start=False, stop=True),
                 r=["ones1", "lp"], w=[zk])
            P.act(lambda e: e.activation(out=sp, in_=zp[:, 0:256], func=AF.Exp, scale=-1.0), r=[zk], w=["sp"])
            P.act(lambda e: e.activation(out=sp, in_=sp, func=AF.Ln, bias=1.0), r=["sp"], w=["sp"])
            cp, ck = next_ps()
            for hp_ in range(2):
                P.pe(lambda e, hp_=hp_: e.matmul(cp[:, hp_ * 128:(hp_ + 1) * 128], lhsT=sp[:, hp_ * 128:(hp_ + 1) * 128],
                                                 rhs=triu[:], start=True, stop=True), r=["sp", "triu"], w=[ck])
            cpv = cp[:, 0:256].rearrange("p (a b) -> p a b", a=2)
            P.act(lambda e: e.activation(out=eb, in_=cpv, func=AF.Exp, scale=-1.0 / 16), r=[ck], w=["eb"])
            P.act(lambda e: e.activation(out=enb, in_=cpv, func=AF.Exp, scale=1.0 / 16), r=[ck], w=["enb"])
            P.dve(lambda e: e.scalar_tensor_tensor(out=qf, in0=qT[:, :, t0:t0 + 128], scalar=0.125, in1=eb,
                                                   op0=ALU.mult, op1=ALU.mult), r=["qT", "eb"], w=["qf%d" % pz])
            P.dve(lambda e: e.tensor_tensor(out=kf, in0=kT[:, :, t0:t0 + 128], in1=enb, op=ALU.mult),
                  r=["kT", "enb"], w=["kf"])
            P.dve(lambda e: e.tensor_copy(out=ebl, in_=eb[:, :, 127]), r=["eb"], w=["ebl%d" % pz])
            for h2 in range(2):
                P.dve(lambda e, h2=h2: e.tensor_scalar(out=qm[h2], in0=qf, scalar1=hmask[:, h2:h2 + 1], scalar2=None, op0=ALU.mult),
                      r=["qf%d" % pz, "hmask"], w=["qm%d_%d" % (pz, h2)])
                P.dve(lambda e, h2=h2: e.tensor_scalar(out=km[h2], in0=kf, scalar1=hmask[:, h2:h2 + 1], scalar2=None, op0=ALU.mult),
                      r=["kf", "hmask"], w=["km%d_%d" % (pz, h2)])

            def ev_kt(pv, pk):
                P.act(lambda e: e.activation(out=ktok.rearrange("p (a b) -> p a b", a=2), in_=pv, func=AF.Copy), r=[pk], w=["ktok%d" % pz])
            transpose_to(None, kf.rearrange("p a b -> p (a b)"), 128, 2, BF16, "kf", ev_kt)

        def stage2(ti):
            t0 = ti * 128
            pz = ti % 2
            qf = qfP[pz]; qm = qmP[pz]; km = kmP[pz]; ktok = ktokP[pz]; ebl = eblP[pz]
            op_, ok = next_ps()
            for hd in range(4):
                hp_, h2 = hd // 2, hd % 2
                s_p, s_k = next_ps()
                P.pe(lambda e, hp_=hp_, h2=h2: e.matmul(s_p[:, 0:128], lhsT=km[h2][:, hp_, :], rhs=qf[:, hp_, :],
                                                        start=True, stop=True), r=["km%d_%d" % (pz, h2), "qf%d" % pz], w=[s_k])
                P.dve(lambda e: e.tensor_tensor(out=scm, in0=s_p[:, 0:128], in1=triu[:], op=ALU.mult),
                      r=[s_k, "triu"], w=["scm"])
                P.pe(lambda e, hd=hd: e.matmul(op_[:, hd * 128:(hd + 1) * 128], lhsT=scm, rhs=vtok[:, ti, hd * 128:(hd + 1) * 128],
                                               start=True, stop=False), r=["scm", "vtok"], w=[ok])
                P.pe(lambda e, hd=hd, hp_=hp_, h2=h2: e.matmul(op_[:, hd * 128:(hd + 1) * 128], lhsT=qm[h2][:, hp_, :], rhs=Sbf[:, hp_, :],
                                                               start=False, stop=True), r=["qm%d_%d" % (pz, h2), "Sbf"], w=[ok])
            gla_out(128, t0, op_, ok)
            for hp_ in range(2):
                kv, kvk = next_ps()
                P.pe(lambda e, hp_=hp_: e.matmul(kv[:, 0:256], lhsT=ktok[:, hp_ * 128:(hp_ + 1) * 128],
                                                 rhs=vtok[:, ti, hp_ * 256:(hp_ + 1) * 256], start=True, stop=True),
                     r=["ktok%d" % pz, "vtok"], w=[kvk])
                for h2 in range(2):
                    rs = slice(h2 * 64, h2 * 64 + 64)
                    P.dve(lambda e, rs=rs, h2=h2, hp_=hp_: e.tensor_tensor(out=tmpS[rs, :], in0=kv[rs, h2 * 128:(h2 + 1) * 128],
                                                                          in1=S32[rs, hp_, :], op=ALU.add),
                          r=[kvk, "S32"], w=["tmpS"])
                    P.dve(lambda e, rs=rs, hp_=hp_: e.tensor_scalar(out=S32[rs, hp_, :], in0=tmpS[rs, :], scalar1=ebl[rs, hp_:hp_ + 1],
                                                                    scalar2=None, op0=ALU.mult), r=["tmpS", "ebl%d" % pz], w=["S32"])
            P.act(lambda e: e.activation(out=Sbf[:], in_=S32[:], func=AF.Copy), r=["S32"], w=["Sbf"])

        stage1(0)
        for ti in range(8):
            if ti + 1 < 8:
                stage1(ti + 1)
            stage2(ti)
        if h == 1:
            P.dma("st", bass.AP(o_gla_p.tensor, l * 32768, [[128, 128], [16384, 2], [1, 128]]), S32[:], r=["S32"])
            if not NOSAMP:
                gla_samples(l, qT, kT, vtok, alrT, gla_out)

        if (SUB if h == 1 else STAGE) < 3:
            return
        P.barrier()
        CV.reset(0, MAINLIM)
        s5ctx = s5_start(l, h, NT)
        sgb = CV.get([128, 4, 1040], BF16); ug = CV.get([128, 4, 1040], BF16)
        vn = CV.get([128, W], BF16); vnf = CV.get([128, W]); st6 = CV.get([128, 4, 6]); mv = CV.get([128, 4, 2])
        tmpb = CV.get([128, 4, 128])
        vnT = CV.get([128, 4, NS])

        def ev_gb(ft, t0, n, pv, pk):
            P.act(lambda e: e.activation(out=sgb[:, ft, t0:t0 + n], in_=pv, func=AF.Silu), r=[pk], w=["sgb"])
        proj_fm(l, SEG["gb"], 512, NT, ev_gb)

        def ev_ub(ft, t0, n, pv, pk):
            P.dve(lambda e: e.tensor_tensor(out=ug[:, ft, t0:t0 + n], in0=pv, in1=sgb[:, ft, t0:t0 + n], op=ALU.mult),
                  r=[pk, "sgb"], w=["ug"])
        proj_fm(l, SEG["ub"], 512, NT, ev_ub)

        def ev_vb(ti, rows, pv, pk):
            t0 = tiles[ti][0]
            for hd in range(4):
                P.dve(lambda e, hd=hd: e.bn_stats(out=st6[0:rows, hd, :], in_=pv[:, hd * 128:(hd + 1) * 128]), r=[pk], w=["st6"])
                P.dve(lambda e, hd=hd: e.bn_aggr(out=mv[0:rows, hd, :], in_=st6[0:rows, hd, :]), r=["st6"], w=["mv"])
            P.act(lambda e: e.activation(out=mv[0:rows, :, 1], in_=mv[0:rows, :, 1], func=AF.Sqrt, bias=epsb[0:rows, :]),
                  r=["mv", "epsb"], w=["mv"])
            P.dve(lambda e: e.reciprocal(out=mv[0:rows, :, 1], in_=mv[0:rows, :, 1]), r=["mv"], w=["mv"])
            for hd in range(4):
                P.dve(lambda e, hd=hd: e.tensor_scalar(out=vnf[0:rows, hd * 128:(hd + 1) * 128], in0=pv[:, hd * 128:(hd + 1) * 128],
                                                       scalar1=mv[0:rows, hd, 0:1], scalar2=mv[0:rows, hd, 1:2],
                                                       op0=ALU.subtract, op1=ALU.mult), r=[pk, "mv"], w=["vnf"])
            P.dve(lambda e: e.tensor_tensor(out=vnf[0:rows, :], in0=vnf[0:rows, :], in1=sgug[0:rows, :], op=ALU.mult),
                  r=["vnf", "lp"], w=["vnf"])
            P.act(lambda e: e.activation(out=vn[0:rows, :], in_=vnf[0:rows, :], func=AF.Copy), r=["vnf"], w=["vn"])
            if rows == 128:
                mp, mk = next_ps()
                for hd in range(4):
                    P.pe(lambda e, hd=hd, mp=mp: e.matmul(mp[:, hd * 128:(hd + 1) * 128], lhsT=vn[:, hd * 128:(hd + 1) * 128],
                                                          rhs=sguw[:, hd, :], start=True, stop=True), r=["vn", "lp"], w=[mk])
                P.dve(lambda e, mp=mp: e.tensor_tensor(out=tmpb, in0=mp[:, :].rearrange("p (a b) -> p a b", a=4), in1=sgub[:], op=ALU.add),
                      r=[mk, "lp"], w=["tmpb"])
                P.dve(lambda e, t0=t0: e.tensor_tensor(out=mT[:, 4:8, t0:t0 + 128], in0=tmpb, in1=ug[:, :, t0:t0 + 128], op=ALU.mult),
                      r=["tmpb", "ug"], w=["mT"])
            else:
                P.dma("st", o_vn_s[l], vnf[0:NS, :], r=["vnf"])

                def ev(pv2, pk2):
                    P.dve(lambda e: e.tensor_tensor(out=vnT, in0=pv2, in1=bc(w00[:].unsqueeze(2), [128, 4, NS]), op=ALU.mult),
                          r=[pk2, "lp"], w=["vnT"])
                    P.dve(lambda e: e.tensor_tensor(out=vnT, in0=vnT, in1=bc(b00[:].unsqueeze(2), [128, 4, NS]), op=ALU.add),
                          r=["vnT", "lp"], w=["vnT"])
                    P.dve(lambda e: e.tensor_tensor(out=mT[:, 4:8, t0:t0 + NS], in0=vnT, in1=ug[:, :, t0:t0 + NS], op=ALU.mult),
                          r=["vnT", "ug"], w=["mT"])
                transpose_to(None, vnf, NS, 4, F32, "vnf", ev)
        proj_tm(l, SEG["vb"], 512, tiles, ev_vb)

        if (SUB if h == 1 else STAGE) < 4:
            return
        P.barrier()
        CV.reset(0, MAINLIM)
        ccT = CV.get([128, 4, 1040], BF16)
        cg = mT[:, 8:12, :]
        zc = CV.get([128, 4, 1042])
        P.dve(lambda e: e.tensor_copy(out=zc[:, :, 0:2], in_=zcar[:]), r=["zcar"], w=["zc"])
        yc = CV.get([128, 4, 1040])

        def ev_cc(ft, t0, n, pv, pk):
            P.act(lambda e: e.activation(out=ccT[:, ft, t0:t0 + n], in_=pv, func=AF.Copy), r=[pk], w=["ccT"])
        proj_fm(l, SEG["cc"], 512, NT, ev_cc)

        def ev_hc(ft, t0, n, pv, pk):
            P.dve(lambda e: e.tensor_tensor(out=zc[:, ft, 2 + t0:2 + t0 + n], in0=pv, in1=ccT[:, ft, t0:t0 + n], op=ALU.mult),
                  r=[pk, "ccT"], w=["zc"])
        proj_fm(l, SEG["hc"], 512, NT, ev_hc)

        def ev_gc(ft, t0, n, pv, pk):
            P.act(lambda e: e.activation(out=cg[:, ft, t0:t0 + n], in_=pv, func=AF.Silu), r=[pk], w=["cg"])
        proj_fm(l, SEG["gc"], 512, NT, ev_gc)

        def ev_cb(ft, t0, n, pv, pk):
            P.dve(lambda e: e.tensor_tensor(out=cg[:, ft, t0:t0 + n], in0=pv, in1=cg[:, ft, t0:t0 + n], op=ALU.mult),
                  r=[pk, "cg"], w=["cg"])
        proj_fm(l, SEG["cb"], 512, NT, ev_cb)
        for ct in range(4):
            P.dve(lambda e, ct=ct: e.tensor_scalar(out=yc[:, ct, 0:1024], in0=zc[:, ct, 0:1024], scalar1=cw[:, 0, ct:ct + 1],
                                                   scalar2=None, op0=ALU.mult), r=["zc", "lp"], w=["yc"])
            P.dve(lambda e, ct=ct: e.scalar_tensor_tensor(out=yc[:, ct, 0:1024], in0=zc[:, ct, 1:1025], scalar=cw[:, 1, ct:ct + 1],
                                                          in1=yc[:, ct, 0:1024], op0=ALU.mult, op1=ALU.add), r=["zc", "lp", "yc"], w=["yc"])
            P.dve(lambda e, ct=ct: e.scalar_tensor_tensor(out=yc[:, ct, 0:1024], in0=zc[:, ct, 2:1026], scalar=cw[:, 2, ct:ct + 1],
                                                          in1=yc[:, ct, 0:1024], op0=ALU.mult, op1=ALU.add), r=["zc", "lp", "yc"], w=["yc"])
        P.dve(lambda e: e.tensor_tensor(out=mT[:, 8:12, 0:1024], in0=yc[:, :, 0:1024], in1=cg[:, :, 0:1024], op=ALU.mult),
              r=["yc", "cg"], w=["mT"])
        if h == 1:
            conv_samples(l, yc, cg, zc)
            for jj in range(2):
                P.dma("st", o_conv_p[l, jj].rearrange("(a p) -> p a", p=128), zc[:, :, 1024 + jj], r=["zc"],
                      allow_slow_non_contiguous=True)
        P.dve(lambda e: e.tensor_copy(out=zcar[:], in_=zc[:, :, 1024:1026]), r=["zc"], w=["zcar"])

        if (SUB if h == 1 else STAGE) < 5:
            return
        while TICK[0] is not None:
            tick()
        if h == 1:
            P.barrier()
            CV.reset(0, MAINLIM)
            s5_finish(l, h, s5ctx)

        if (SUB if h == 1 else STAGE) < 6:
            return
        P.barrier()
        CV.reset()
        xo = [CV.get([128, 256]) for _ in range(6)]
        cnt = 0
        for cb_ in range(8):
            wv, wk = load_w(w_out[l, :, cb_ * 256:(cb_ + 1) * 256])
            for ti, (t0, rows) in enumerate(tiles):
                b = cnt % 6
                cnt += 1
                P.dma("xo%d" % b, xo[b][0:rows, :], xrows(xin, ti, rows)[:, cb_ * 256:(cb_ + 1) * 256], w=["xo%d" % b])
                pt, pk = next_ps()
                for k in range(16):
                    P.pe(lambda e, k=k, pt=pt, wv=wv, t0=t0, rows=rows: e.matmul(pt[0:rows, 0:256], lhsT=mT[:, k, t0:t0 + rows],
                                                                                rhs=wv[:, k, :], start=(k == 0), stop=(k == 15)),
                         r=[wk, "mT", "mT3"], w=[pk])
                P.dve(lambda e, b=b, pt=pt, rows=rows: e.tensor_tensor(out=xo[b][0:rows, :], in0=pt[0:rows, 0:256], in1=xo[b][0:rows, :],
                                                                      op=ALU.add), r=[pk, "xo%d" % b], w=["xo%d" % b])
                P.dma("xs%d" % b, xrows(xout, ti, rows)[:, cb_ * 256:(cb_ + 1) * 256], xo[b][0:rows, :], r=["xo%d" % b], w=["xscratch"],
                      phys="act")

    def gla_samples(l, qT, kT, vtok, alrT, gla_out):
        g = CV.get
        c0 = 1024
        afm = g([128, 2, NS]); SS = g([128, 2, NS, 128])
        kms = g([NS, NS, 256], BF16, rows=NS); ktk = g([NS, 256], BF16, rows=NS)
        qs = g([128, 2, NS]); kbf = g([128, 2, NS], BF16)
        for hp_ in range(2):
            P.dma("ld", SS[:, hp_], bass.AP(sgla.tensor, l * NS * 32768 + hp_ * 16384, [[128, 128], [32768, NS], [1, 128]]), w=["SS"])
        zp, zk = next_ps()
        for hp_ in range(2):
            P.pe(lambda e, hp_=hp_, zp=zp: e.matmul(zp[:, hp_ * NS:(hp_ + 1) * NS], lhsT=wa2[:, hp_ * 128:(hp_ + 1) * 128],
                                                    rhs=alrT[:, c0:c0 + NS], start=True, stop=True), r=["lp", "alrT"], w=[zk])
        for hp_ in range(2):
            P.act(lambda e, hp_=hp_, zp=zp: e.activation(out=afm[:, hp_, :], in_=zp[:, hp_ * NS:(hp_ + 1) * NS], func=AF.Exp,
                                                         scale=-1.0, bias=nbafm[:, hp_:hp_ + 1]), r=[zk, "lp"], w=["afm"])
        P.act(lambda e: e.activation(out=afm, in_=afm, func=AF.Ln, bias=1.0), r=["afm"], w=["afm"])
        P.act(lambda e: e.activation(out=afm, in_=afm, func=AF.Exp, scale=-1.0 / 16), r=["afm"], w=["afm"])
        P.dve(lambda e: e.tensor_copy(out=kbf, in_=kT[:, :, c0:c0 + NS]), r=["kT"], w=["kbf"])
        pt, pk = next_ps(6, 8)
        pvb = pt[:].bitcast(BF16)
        for hp_ in range(2):
            P.pe(lambda e, hp_=hp_: e.transpose(pvb[0:NS, hp_ * 128:(hp_ + 1) * 128], kbf[:, hp_, :], identb[:]),
                 r=["kbf", "identb"], w=[pk])
        P.act(lambda e: e.activation(out=ktk, in_=pvb[0:NS, 0:256], func=AF.Copy), r=[pk], w=["ktk"])
        P.dve(lambda e: e.tensor_tensor(out=kms, in0=bc(ktk.unsqueeze(1), [NS, NS, 256]), in1=bc(eye16[:].unsqueeze(2), [NS, NS, 256]),
                                        op=ALU.mult), r=["ktk", "eye16"], w=["kms"])
        P.dve(lambda e: e.tensor_scalar(out=qs, in0=qT[:, :, c0:c0 + NS], scalar1=0.125, scalar2=None, op0=ALU.mult), r=["qT"], w=["qs"])
        eyeb = g([128, NS, NS])
        P.dma("ld", eyeb, bass.AP(eye16_d.tensor, 0, [[0, 128], [NS, NS], [1, NS]]), w=["eyeb"])
        qmsm = [g([128, 2, NS, NS]) for _ in range(2)]
        qsm = [g([128, 2, NS]) for _ in range(2)]
        for h2 in range(2):
            P.dve(lambda e, h2=h2: e.tensor_scalar(out=qsm[h2], in0=qs, scalar1=hmask[:, h2:h2 + 1], scalar2=None, op0=ALU.mult),
                  r=["qs", "hmask"], w=["qsm%d" % h2])
            for hp_ in range(2):
                P.dve(lambda e, h2=h2, hp_=hp_: e.tensor_tensor(out=qmsm[h2][:, hp_], in0=eyeb, in1=bc(qsm[h2][:, hp_, :].unsqueeze(2), [128, NS, NS]),
                                                                op=ALU.mult), r=["eyeb", "qsm%d" % h2], w=["qmsm%d" % h2])
        for hp_ in range(2):
            for b in range(NS):
                kv, kvk = next_ps()
                P.pe(lambda e, kv=kv, b=b, hp_=hp_: e.matmul(kv[:, 0:256], lhsT=kms[:, b, hp_ * 128:(hp_ + 1) * 128],
                                                             rhs=vtok[0:NS, 8, hp_ * 256:(hp_ + 1) * 256], start=True, stop=True),
                     r=["kms", "vtok"], w=[kvk])
                for h2 in range(2):
                    rs = slice(h2 * 64, h2 * 64 + 64)
                    P.dve(lambda e, kv=kv, rs=rs, h2=h2, hp_=hp_, b=b: e.scalar_tensor_tensor(
                        out=SS[rs, hp_, b, :], in0=SS[rs, hp_, b, :], scalar=afm[rs, hp_, b:b + 1], in1=kv[rs, h2 * 128:(h2 + 1) * 128],
                        op0=ALU.mult, op1=ALU.add), r=[kvk, "SS", "afm"], w=["SS"])
        for hp_ in range(2):
            P.dma("st", bass.AP(o_gla_s.tensor, l * NS * 32768 + hp_ * 16384, [[128, 128], [32768, NS], [1, 128]]), SS[:, hp_], r=["SS"])
        op_, ok = next_ps()
        for hd in range(4):
            hp_, h2 = hd // 2, hd % 2
            rs = slice(h2 * 64, h2 * 64 + 64)
            for b in range(NS):
                P.pe(lambda e, hd=hd, hp_=hp_, h2=h2, b=b: e.matmul(op_[0:NS, hd * 128:(hd + 1) * 128], lhsT=qmsm[h2][:, hp_, b, :],
                                                                  rhs=SS[:, hp_, b, :], start=(b == 0), stop=(b == NS - 1)),
                     r=["qmsm%d" % h2, "SS"], w=[ok])
        gla_out(NS, c0, op_, ok)

    def conv_samples(l, yc, cg, zc):
        g = CV.get
        c0 = 1024
        cbuf = g([NS, 2 * W], rows=NS); cbT = g([128, 2, 4, NS]); z0t = g([NS, W], rows=NS)
        P.dma("ld", cbuf, sconv[l].rearrange("b j c -> b (j c)"), w=["cbuf"])
        for j in range(2):
            def ev(pv, pk, j=j):
                P.dve(lambda e: e.tensor_copy(out=cbT[:, j], in_=pv), r=[pk], w=["cbT"])
            transpose_to(None, cbuf[:, j * W:(j + 1) * W], NS, 4, F32, "cbuf", ev)
        for ct in range(4):
            P.dve(lambda e, ct=ct: e.tensor_scalar(out=yc[:, ct, c0:c0 + NS], in0=cbT[:, 0, ct, :], scalar1=cw[:, 0, ct:ct + 1],
                                                   scalar2=None, op0=ALU.mult), r=["cbT", "lp", "yc"], w=["yc"])
            P.dve(lambda e, ct=ct: e.scalar_tensor_tensor(out=yc[:, ct, c0:c0 + NS], in0=cbT[:, 1, ct, :], scalar=cw[:, 1, ct:ct + 1],
                                                          in1=yc[:, ct, c0:c0 + NS], op0=ALU.mult, op1=ALU.add), r=["cbT", "lp", "yc"], w=["yc"])
            P.dve(lambda e, ct=ct: e.scalar_tensor_tensor(out=yc[:, ct, c0:c0 + NS], in0=zc[:, ct, 2 + c0:2 + c0 + NS],
                                                          scalar=cw[:, 2, ct:ct + 1], in1=yc[:, ct, c0:c0 + NS], op0=ALU.mult, op1=ALU.add),
                  r=["zc", "lp", "yc"], w=["yc"])
        P.dve(lambda e: e.tensor_tensor(out=mT[:, 8:12, c0:c0 + NS], in0=yc[:, :, c0:c0 + NS], in1=cg[:, :, c0:c0 + NS], op=ALU.mult),
              r=["yc", "cg"], w=["mT"])
        P.dma("st", o_conv_s[l, :, 0, :], cbuf[:, W:2 * W], r=["cbuf"])
        zs = g([128, 4, NS])
        P.dve(lambda e: e.tensor_copy(out=zs, in_=zc[:, :, 2 + c0:2 + c0 + NS]), r=["zc"], w=["zs"])
        pt, pk = next_ps(6, 8)
        for ct in range(4):
            P.pe(lambda e, ct=ct, pt=pt: e.transpose(pt[0:NS, ct * 128:(ct + 1) * 128], zs[:, ct, :], ident[:]), r=["zs", "ident"], w=[pk])
        P.act(lambda e, pt=pt: e.activation(out=z0t, in_=pt[0:NS, 0:W], func=AF.Copy), r=[pk], w=["z0t"])
        P.dma("st", o_conv_s[l, :, 1, :], z0t, r=["z0t"])

    def s5_start(l, h, NT):
        CVS.reset(MAINLIM, ARENA)
        g = CVS.get
        udT = g([128, 4, 1040], BF16)
        gdc = g([128, 4, 128], BF16)

        def ev_ud(ft, t0, n, pv, pk):
            P.act(lambda e: e.activation(out=udT[:, ft, t0:t0 + n], in_=pv, func=AF.Copy), r=[pk], w=["udT"])
        proj_fm(l, SEG["ud"], 512, NT, ev_ud)

        def ev_gd(ft, t0, n, pv, pk):
            P.act(lambda e: e.activation(out=mT[:, 12 + ft, t0:t0 + n], in_=pv, func=AF.Silu), r=[pk], w=["mT3"])
        proj_fm(l, SEG["gd"], 512, NT, ev_gd)

        Er = g([128, 4, 128]); Ei = g([128, 4, 128]); a2 = g([128, 4, 128]); a4 = g([128, 4, 128])
        Xr = g([128, 4, 128], BF16); Xi = g([128, 4, 128], BF16)
        xe = g([128, 2, 16])
        yv = g([128, 4, 128]); y2 = g([128, 4, 128]); sig = g([128, 4, 128])

        def rot(dst_r, dst_i, src_r, src_i, gq, sign, keys_r, key_w):
            tc = TC[:, gq * 4:(gq + 1) * 4, :]; ts = TS[:, gq * 4:(gq + 1) * 4, :]
            ap_, ak = next_ps()
            a1p = ap_[:, :].rearrange("p (a b) -> p a b", a=4)
            bp_, bk = next_ps()
            a3p = bp_[:, :].rearrange("p (a b) -> p a b", a=4)
            P.dve(lambda e: e.tensor_tensor(out=a1p, in0=src_r, in1=tc, op=ALU.mult), r=keys_r + ["TC"], w=[ak])
            P.dve(lambda e: e.tensor_tensor(out=a2, in0=src_i, in1=ts, op=ALU.mult), r=keys_r + ["TS"], w=["a2"])
            P.dve(lambda e: e.tensor_tensor(out=a3p, in0=src_i, in1=tc, op=ALU.mult), r=keys_r + ["TC"], w=[bk])
            P.dve(lambda e: e.tensor_tensor(out=a4, in0=src_r, in1=ts, op=ALU.mult), r=keys_r + ["TS"], w=["a4"])
            P.dve(lambda e: e.tensor_tensor(out=dst_r, in0=a1p, in1=a2, op=(ALU.subtract if sign > 0 else ALU.add)),
                  r=[ak, "a2"], w=[key_w + "r"])
            P.dve(lambda e: e.tensor_tensor(out=dst_i, in0=a3p, in1=a4, op=(ALU.add if sign > 0 else ALU.subtract)),
                  r=[bk, "a4"], w=[key_w + "i"])

        def y_evac(kt, pv, pk, t0, n):
            P.dve(lambda e: e.scalar_tensor_tensor(out=yv[:, kt, 0:n], in0=udT[:, kt, t0:t0 + n], scalar=dsk[:, kt:kt + 1],
                                                   in1=pv, op0=ALU.mult, op1=ALU.add), r=[pk, "udT", "lp"], w=["yv"])

        def glu_tail(t0, n):
            P.dve(lambda e: e.tensor_tensor(out=y2[:, :, 0:n], in0=yv[:, :, 0:n], in1=yv[:, :, 0:n], op=ALU.mult), r=["yv"], w=["y2"])
            P.dve(lambda e: e.tensor_scalar(out=y2[:, :, 0:n], in0=y2[:, :, 0:n], scalar1=0.044715 * 1.5957691216, scalar2=1.5957691216,
                                            op0=ALU.mult, op1=ALU.add), r=["y2"], w=["y2"])
            P.dve(lambda e: e.tensor_tensor(out=y2[:, :, 0:n], in0=y2[:, :, 0:n], in1=yv[:, :, 0:n], op=ALU.mult), r=["y2", "yv"], w=["y2"])
            P.act(lambda e: e.activation(out=sig[:, :, 0:n], in_=y2[:, :, 0:n], func=AF.Sigmoid), r=["y2"], w=["sig"])
            P.dve(lambda e: e.tensor_tensor(out=gdc[:, :, 0:n], in0=yv[:, :, 0:n], in1=sig[:, :, 0:n], op=ALU.mult),
                  r=["yv", "sig"], w=["gdc"])
            for ft in range(4):
                pt, pk = next_ps()
                for kt in range(4):
                    P.pe(lambda e, pt=pt, kt=kt, ft=ft: e.matmul(pt[:, 0:n], lhsT=gluw[:, kt, ft * 128:(ft + 1) * 128], rhs=gdc[:, kt, 0:n],
                                                                 start=(kt == 0), stop=(kt == 3)), r=["lp", "gdc"], w=[pk])
                P.act(lambda e, pt=pt, ft=ft: e.activation(out=sig[:, ft, 0:n], in_=pt[:, 0:n], func=AF.Sigmoid, bias=glub[:, ft:ft + 1]),
                      r=[pk, "lp"], w=["sig"])
            P.dve(lambda e: e.tensor_tensor(out=y2[:, :, 0:n], in0=sig[:, :, 0:n], in1=mT[:, 12:16, t0:t0 + n], op=ALU.mult), r=["sig", "mT3"], w=["y2"])
            P.dve(lambda e: e.tensor_tensor(out=mT[:, 12:16, t0:t0 + n], in0=y2[:, :, 0:n], in1=gdc[:, :, 0:n], op=ALU.mult),
                  r=["y2", "gdc", "mT3"], w=["mT3"])

        def gen():
            for ci_ in range(8):
                t0 = ci_ * 128
                for gq in range(4):
                    pr, prk = next_ps(); pi, pik = next_ps()
                    for j in range(4):
                        gp = gq * 4 + j
                        P.pe(lambda e, pr=pr, j=j, gp=gp, gq=gq: e.matmul(pr[:, j * 128:(j + 1) * 128], lhsT=BL[:, gp, 0, :], rhs=udT[:, gq, t0:t0 + 128],
                                                                          start=True, stop=True), r=["BL", "udT"], w=[prk])
                        P.pe(lambda e, pi=pi, j=j, gp=gp, gq=gq: e.matmul(pi[:, j * 128:(j + 1) * 128], lhsT=BL[:, gp, 1, :], rhs=udT[:, gq, t0:t0 + 128],
                                                                          start=True, stop=True), r=["BL", "udT"], w=[pik])
                    prv = pr[:, :].rearrange("p (a b) -> p a b", a=4); piv = pi[:, :].rearrange("p (a b) -> p a b", a=4)
                    rot(Er, Ei, prv, piv, gq, -1, [prk, pik], "E")
                    Wr = psb[6][:, :].rearrange("p (a b) -> p a b", a=4); Wi = psb[7][:, :].rearrange("p (a b) -> p a b", a=4)
                    for j in range(4):
                        gp = gq * 4 + j
                        P.dve(lambda e, j=j, gp=gp: e.tensor_tensor_scan(out=Wr[:, j, :], data0=bc(MAG[:, gp:gp + 1], [128, 128]), data1=Er[:, j, :],
                                                                         initial=Xc[:, 0, gp:gp + 1], op0=ALU.mult, op1=ALU.add),
                              r=["MAG", "Er", "Xc"], w=["ps6"])
                        P.dve(lambda e, j=j, gp=gp: e.tensor_tensor_scan(out=Wi[:, j, :], data0=bc(MAG[:, gp:gp + 1], [128, 128]), data1=Ei[:, j, :],
                                                                         initial=Xc[:, 1, gp:gp + 1], op0=ALU.mult, op1=ALU.add),
                              r=["MAG", "Ei", "Xc"], w=["ps7"])
                    if False:
                        dbg_dump("Er", Er, ["Er"]); dbg_dump("Ei", Ei, ["Ei"]); dbg_dump("Wr", Wr, ["Wr"]); dbg_dump("Wi", Wi, ["Wi"])
                    rot(Er, Ei, Wr, Wi, gq, +1, ["ps6", "ps7"], "E")
                    if False:
                        dbg_dump("Xr", Er, ["Er"]); dbg_dump("Xi", Ei, ["Ei"])
                    P.act(lambda e: e.activation(out=Xr, in_=Er, func=AF.Copy), r=["Er"], w=["Xr"])
                    P.act(lambda e: e.activation(out=Xi, in_=Ei, func=AF.Copy), r=["Ei"], w=["Xi"])
                    P.dve(lambda e, gq=gq: e.tensor_copy(out=xe[:, 0, gq * 4:(gq + 1) * 4], in_=Er[:, :, 127]), r=["Er"], w=["xe"])
                    P.dve(lambda e, gq=gq: e.tensor_copy(out=xe[:, 1, gq * 4:(gq + 1) * 4], in_=Ei[:, :, 127]), r=["Ei"], w=["xe"])
                    pt, pk = next_ps()
                    for j in range(4):
                        gp = gq * 4 + j
                        P.pe(lambda e, pt=pt, gp=gp, j=j: e.matmul(pt[:, 0:128], lhsT=CL[:, gp, 0, :], rhs=Xr[:, j, :], start=(j == 0), stop=False),
                             r=["CL", "Xr"], w=[pk])
                        P.pe(lambda e, pt=pt, gp=gp, j=j: e.matmul(pt[:, 0:128], lhsT=CL[:, gp, 1, :], rhs=Xi[:, j, :], start=False, stop=(j == 3)),
                             r=["CL", "Xi"], w=[pk])
                    y_evac(gq, pt[:, 0:128], pk, t0, 128)
                    yield
                P.dve(lambda e: e.tensor_copy(out=Xc[:], in_=xe), r=["xe", "ps6", "ps7"], w=["Xc"])
                if False:
                    dbg_dump("yv", yv, ["yv"])
                glu_tail(t0, 128)
                yield

        def finish():
            g = CV.get
            if h == 1:
                for c_, dst in ((0, o_sre_p), (1, o_sim_p)):
                    for hh in range(2):
                        P.dma("st", bass.AP(dst.tensor, l * 2048 + hh * 64, [[1, 64], [128, 16]]), Xc[hh * 64:hh * 64 + 64, c_, :], r=["Xc"],
                              allow_slow_non_contiguous=True)
                c0 = 1024
                x0 = g([NS, 2048], rows=NS); x0T = g([128, 2, 16, NS]); xn_ = g([128, 2, 16, NS]); xnb = g([128, 2, 16, NS], BF16)
                xnt = g([NS, 2048], rows=NS)
                for c_ in range(2):
                    P.dma("ld", x0, (sre, sim)[c_][l].rearrange("b g n -> b (g n)"), w=["x0"])
                    for q4 in range(4):
                        def ev(pv, pk, c_=c_, q4=q4):
                            P.dve(lambda e: e.tensor_copy(out=x0T[:, c_, q4 * 4:(q4 + 1) * 4, :], in_=pv), r=[pk], w=["x0T"])
                        transpose_to(None, x0[:, q4 * 512:(q4 + 1) * 512], NS, 4, F32, "x0", ev)
                arb = bc(AR[:].unsqueeze(2), [128, 16, NS]); aib = bc(AI[:].unsqueeze(2), [128, 16, NS])
                t_a = g([128, 16, NS]); t_b = g([128, 16, NS])
                for gq in range(4):
                    pr, prk = next_ps(); pi, pik = next_ps()
                    for j in range(4):
                        gp = gq * 4 + j
                        P.pe(lambda e, pr=pr, j=j, gp=gp, gq=gq: e.matmul(pr[:, j * NS:(j + 1) * NS], lhsT=BL[:, gp, 0, :], rhs=udT[:, gq, c0:c0 + NS],
                                                                          start=True, stop=True), r=["BL", "udT"], w=[prk])
                        P.pe(lambda e, pi=pi, j=j, gp=gp, gq=gq: e.matmul(pi[:, j * NS:(j + 1) * NS], lhsT=BL[:, gp, 1, :], rhs=udT[:, gq, c0:c0 + NS],
                                                                          start=True, stop=True), r=["BL", "udT"], w=[pik])
                    P.dve(lambda e, pr=pr, gq=gq: e.tensor_copy(out=xn_[:, 0, gq * 4:(gq + 1) * 4, :], in_=pr[:, 0:4 * NS].rearrange("p (a b) -> p a b", a=4)),
                          r=[prk], w=["xn_"])
                    P.dve(lambda e, pi=pi, gq=gq: e.tensor_copy(out=xn_[:, 1, gq * 4:(gq + 1) * 4, :], in_=pi[:, 0:4 * NS].rearrange("p (a b) -> p a b", a=4)),
                          r=[pik], w=["xn_"])
                P.dve(lambda e: e.tensor_tensor(out=t_a, in0=x0T[:, 0], in1=arb, op=ALU.mult), r=["x0T", "AR"], w=["t_a"])
                P.dve(lambda e: e.tensor_tensor(out=xn_[:, 0], in0=xn_[:, 0], in1=t_a, op=ALU.add), r=["xn_", "t_a"], w=["xn_"])
                P.dve(lambda e: e.tensor_tensor(out=t_b, in0=x0T[:, 1], in1=aib, op=ALU.mult), r=["x0T", "AI"], w=["t_b"])
                P.dve(lambda e: e.tensor_tensor(out=xn_[:, 0], in0=xn_[:, 0], in1=t_b, op=ALU.subtract), r=["xn_", "t_b"], w=["xn_"])
                P.dve(lambda e: e.tensor_tensor(out=t_a, in0=x0T[:, 1], in1=arb, op=ALU.mult), r=["x0T", "AR", "xn_"], w=["t_a"])
                P.dve(lambda e: e.tensor_tensor(out=xn_[:, 1], in0=xn_[:, 1], in1=t_a, op=ALU.add), r=["xn_", "t_a"], w=["xn_"])
                P.dve(lambda e: e.tensor_tensor(out=t_b, in0=x0T[:, 0], in1=aib, op=ALU.mult), r=["x0T", "AI", "xn_"], w=["t_b"])
                P.dve(lambda e: e.tensor_tensor(out=xn_[:, 1], in0=xn_[:, 1], in1=t_b, op=ALU.add), r=["xn_", "t_b"], w=["xn_"])
                P.act(lambda e: e.activation(out=xnb, in_=xn_, func=AF.Copy), r=["xn_"], w=["xnb"])
                for kt in range(4):
                    pt, pk = next_ps()
                    for j in range(4):
                        gp = kt * 4 + j
                        P.pe(lambda e, pt=pt, gp=gp, j=j: e.matmul(pt[:, 0:NS], lhsT=CL[:, gp, 0, :], rhs=xnb[:, 0, gp, :], start=(j == 0), stop=False),
                             r=["CL", "xnb"], w=[pk])
                        P.pe(lambda e, pt=pt, gp=gp, j=j: e.matmul(pt[:, 0:NS], lhsT=CL[:, gp, 1, :], rhs=xnb[:, 1, gp, :], start=False, stop=(j == 3)),
                             r=["CL", "xnb"], w=[pk])
                    y_evac(kt, pt[:, 0:NS], pk, c0, NS)
                glu_tail(c0, NS)
                for c_, dst in ((0, o_sre_s), (1, o_sim_s)):
                    for q4 in range(4):
                        pt, pk = next_ps(6, 8)
                        for j in range(4):
                            gp = q4 * 4 + j
                            P.pe(lambda e, pt=pt, j=j, gp=gp, c_=c_: e.transpose(pt[0:NS, j * 128:(j + 1) * 128], xn_[:, c_, gp, :], ident[:]),
                                 r=["xn_", "ident"], w=[pk])
                        P.act(lambda e, pt=pt, c_=c_, q4=q4: e.activation(out=xnt[:, q4 * 512:(q4 + 1) * 512], in_=pt[0:NS, 0:512], func=AF.Copy),
                              r=[pk], w=["xnt"])
                    P.dma("st", dst[l].rearrange("b g n -> b (g n)"), xnt, r=["xnt"])


        TICK[0] = gen()
        return finish

    def s5_finish(l, h, fin):
        fin()

    def final_norm():
        P.barrier()
        CV.reset()
        xt = [CV.get([128, D]) for _ in range(2)]
        yo = [CV.get([128, D]) for _ in range(2)]
        junk = CV.get([128, D], BF16)
        fg = CV.get([128, D]); ssq = CV.get([128, 20])
        P.dma("ld", fg, bass.AP(fng.tensor, 0, [[0, 128], [1, D]]), w=["fg"])
        alltiles = [(x2[i * 128:(i + 1) * 128, :], yp[i * 128:(i + 1) * 128, :], 128) for i in range(16)] + [(x2[2048:2048 + NS, :], ys, NS)]
        for ti, (src, dst, rows) in enumerate(alltiles):
            b = ti % 2
            c = ti % 20
            P.dma("xl%d" % b, xt[b][0:rows, :], src, r=["xscratch"], w=["xt%d" % b])
            P.act(lambda e, b=b, rows=rows, c=c: e.activation(out=junk[0:rows, :], in_=xt[b][0:rows, :], func=AF.Square,
                                                              accum_out=ssq[0:rows, c:c + 1]), r=["xt%d" % b], w=["junk", "fs%d" % c])
            P.act(lambda e, rows=rows, c=c: e.activation(out=ssq[0:rows, c:c + 1], in_=ssq[0:rows, c:c + 1], func=AF.Sqrt, scale=1.0 / D,
                                                         bias=epsb[0:rows, :]), r=["fs%d" % c, "epsb"], w=["fs%d" % c])
            P.dve(lambda e, rows=rows, c=c: e.reciprocal(out=ssq[0:rows, c:c + 1], in_=ssq[0:rows, c:c + 1]), r=["fs%d" % c], w=["fs%d" % c])
            P.dve(lambda e, b=b, rows=rows, c=c: e.scalar_tensor_tensor(out=yo[b][0:rows, :], in0=xt[b][0:rows, :], scalar=ssq[0:rows, c:c + 1],
                                                                        in1=fg[0:rows, :], op0=ALU.mult, op1=ALU.mult),
                  r=["xt%d" % b, "fs%d" % c, "fg"], w=["yo%d" % b])
            P.dma("yo%d" % b, dst, yo[b][0:rows, :], r=["yo%d" % b], phys="pool")

    def dbg_dump(name, ap, keys):
        if not DBGT:
            return
        shape = list(ap.shape)
        dt_ = ap.dtype
        d_ = nc.dram_tensor("dbg_" + name, shape, dt_, kind="ExternalOutput").ap()
        P.barrier()
        P.dma("st", d_, ap, r=keys)
        P.barrier()
        DBGN.append("dbg_" + name)

    def dump():
        dh = nc.dram_tensor("dbg_h", [128, 16 * 1040], BF16, kind="ExternalOutput").ap()
        dm = nc.dram_tensor("dbg_m", [128, 16 * 1040], BF16, kind="ExternalOutput").ap()
        P.barrier()
        P.dma("st", dh, hT[:].rearrange("p a b -> p (a b)"), r=["hT"])
        P.dma("st", dm, mT[:].rearrange("p a b -> p (a b)"), r=["mT", "mT3"])
        P.barrier()

    if STAGE >= 99:
        for l in range(2):
            layer_params(l)
            for h in range(2):
                layer_pass(l, h)
                if DUMP == (l, h):
                    dump()
        final_norm()
    else:
        layer_params(0)
        if STAGE >= 1:
            layer_pass(0, 0)
        if STAGE >= 7:
            layer_pass(0, 1)
        if STAGE >= 8:
            final_norm()
    P.emit()
    st.close()
    return nc


_CACHE = {}


def kernel(**inp):
    if "nc" not in _CACHE:
        _CACHE["nc"] = build_program()
    nc = _CACHE["nc"]
    f = lambda a: np.ascontiguousarray(np.asarray(a, dtype=np.float32))
    ident = np.eye(128, dtype=np.float32)
    triu = np.triu(np.ones((128, 128), np.float32))
    eye16 = np.eye(16, dtype=np.float32)
    hmask = np.zeros((128, 2), np.float32); hmask[:64, 0] = 1; hmask[64:, 1] = 1
    shared = dict(
        norm_g=f(inp["norm_g"]), w_in=f(inp["w_in"]), w_a2=f(inp["w_a2"]), b_a=f(inp["b_a"]), gla_g=f(inp["gla_g"]),
        sgu_g=f(inp["sgu_g"]), sgu_w=f(inp["sgu_w"]), sgu_b=f(inp["sgu_b"]), conv_w=f(inp["conv_w"]),
        lam_re=f(inp["ssm_lambda_re"]), lam_im=f(inp["ssm_lambda_im"]), log_dt=f(inp["ssm_log_dt"]),
        b_re=f(inp["ssm_b_re"]), b_im=f(inp["ssm_b_im"]), c_re=f(inp["ssm_c_re"]), c_im=f(inp["ssm_c_im"]),
        ssm_d=f(inp["ssm_d"]), glu_w=f(inp["glu_w"]), glu_b=f(inp["glu_b"]), w_out=f(inp["w_out"]), fng=f(inp["final_norm_g"]),
        ident=ident, triu=triu, eye16=eye16, hmask=hmask)
    xpr = f(inp["x_prompt"]); xsm = f(inp["x_sample"])[:, 0, :]
    sg = f(inp["state_gla"]); sc = f(inp["state_conv"]); sr = f(inp["state_ssm_re"]); si = f(inp["state_ssm_im"])
    in_maps = []
    for c in range(8):
        m = dict(shared)
        sl = slice(c * NS, (c + 1) * NS)
        m.update(xp=xpr[c % 4], xs=np.ascontiguousarray(xsm[sl]), sgla=np.ascontiguousarray(sg[:, sl]),
                 sconv=np.ascontiguousarray(sc[:, sl]), sre=np.ascontiguousarray(sr[:, sl]), sim=np.ascontiguousarray(si[:, sl]))
        in_maps.append(m)
    res = run_bass_kernel_spmd(nc, in_maps, core_ids=list(range(8))).results
    if DUMP is not None:
        _CACHE["dbg"] = (res[0]["dbg_h"], res[0]["dbg_m"])
    for n_ in DBGN:
        _CACHE[n_] = np.asarray(res[0][n_])
    cat = lambda k, ax: np.concatenate([res[c][k] for c in range(8)], axis=ax)
    stk = lambda k: np.stack([res[c][k] for c in range(4)], axis=1)
    y_prompt = np.stack([res[c]["yp"] for c in range(4)], axis=0)
    y_sample = cat("ys", 0)[:, None, :]
    return (y_prompt.astype(np.float32), y_sample.astype(np.float32),
            stk("gla_p"), cat("gla_s", 1), stk("conv_p"), cat("conv_s", 1),
            stk("sre_p"), stk("sim_p"), cat("sre_s", 1), cat("sim_s", 1), cat("vn_s", 1)[:, :, None, :])
```
